# Optimizing a Trainium2 kernel written in Bass

```python
import jax, jax.numpy as jnp
from jax import lax
import numpy as np

D_MODEL = 1024
BATCH = 32
SEQ = 2048
DEPTH = 1

HEAD_DIM = 64
N_HEADS = D_MODEL // HEAD_DIM
N_HEADS_A = N_HEADS // 2
N_HEADS_B = N_HEADS - N_HEADS_A
WIDTH_A = N_HEADS_A * HEAD_DIM
WIDTH_B = N_HEADS_B * HEAD_DIM
DILATED_PATTERNS = ((128, 1), (512, 4), (2048, 16))
Q_BLOCK = 128
ROT_DIM = HEAD_DIM // 4
ROPE_THETA = 500000.0
D_FF = ((8 * D_MODEL // 3 + 63) // 64) * 64
N_MOD = 9
EPS = 1e-6
ATTN_SCALE = HEAD_DIM ** -0.5
NEG = -1e30
COL_SIZES = (WIDTH_A, WIDTH_A, WIDTH_A, WIDTH_B, WIDTH_B, WIDTH_B, N_HEADS_B)
COL_OFFSETS = tuple(int(o) for o in np.cumsum(COL_SIZES)[:-1])
IN_COLS = int(sum(COL_SIZES))

kernel_name = 'hybrid_dilated_fox_macaron_block'


def rmsnorm(x, g):
    xf = x.astype(jnp.float32)
    y = xf * lax.rsqrt(jnp.mean(xf * xf, axis=-1, keepdims=True) + EPS)
    return (y * g.astype(jnp.float32)).astype(x.dtype)


def partial_rotary(t, positions):
    inv_freq = ROPE_THETA ** (-jnp.arange(0, ROT_DIM, 2, dtype=jnp.float32) / ROT_DIM)
    ang = positions.astype(jnp.float32)[:, None, :, None] * inv_freq
    cos, sin = jnp.cos(ang), jnp.sin(ang)
    tf = t.astype(jnp.float32)
    x1 = tf[..., :ROT_DIM // 2]
    x2 = tf[..., ROT_DIM // 2:ROT_DIM]
    rot = jnp.concatenate([x1 * cos - x2 * sin, x2 * cos + x1 * sin, tf[..., ROT_DIM:]], axis=-1)
    return rot.astype(t.dtype)


def swiglu(h, w_gate, w_up, w_down):
    return (jax.nn.silu(h @ w_gate) * (h @ w_up)) @ w_down


def banded_causal_attention(q, k, v, w):
    L = q.shape[-2]
    lead = q.shape[:-2]
    nb = -(-L // Q_BLOCK)
    Lp = nb * Q_BLOCK
    pad = [(0, 0)] * (q.ndim - 2)
    qp = jnp.pad(q, pad + [(0, Lp - L), (0, 0)])
    kp = jnp.pad(k, pad + [(w, Lp - L), (0, 0)])
    vp = jnp.pad(v, pad + [(w, Lp - L), (0, 0)])
    span = Q_BLOCK + w
    q_blk = qp.reshape(*lead, nb, Q_BLOCK, HEAD_DIM)
    idx = jnp.arange(nb)[:, None] * Q_BLOCK + jnp.arange(span)[None, :]
    k_blk = jnp.take(kp, idx, axis=-2)
    v_blk = jnp.take(vp, idx, axis=-2)
    s = jnp.einsum('...nqd,...nkd->...nqk', q_blk, k_blk,
                   preferred_element_type=jnp.float32) * ATTN_SCALE
    dist = jnp.arange(Q_BLOCK)[:, None] + w - jnp.arange(span)[None, :]
    key_pos = idx - w
    valid = ((dist >= 0) & (dist <= w))[None] & (key_pos >= 0)[:, None, :]
    s = jnp.where(valid, s, NEG)
    lse = jax.nn.logsumexp(s, axis=-1)
    p = jnp.exp(s - lse[..., None])
    o = jnp.einsum('...nqk,...nkd->...nqd', p.astype(v.dtype), v_blk)
    o = o.reshape(*lead, Lp, HEAD_DIM)[..., :L, :]
    lse = lse.reshape(*lead, Lp)[..., :L]
    return o, lse


def dilated_mixture_attention(q, k, v):
    B, H, S, hd = q.shape
    outs, lses = [], []
    for window, d in DILATED_PATTERNS:
        w_sub = window // d
        to_cls = lambda t: t.reshape(B, H, S // d, d, hd).swapaxes(2, 3)
        o, lse = banded_causal_attention(to_cls(q), to_cls(k), to_cls(v), w_sub)
        outs.append(o.swapaxes(2, 3).reshape(B, H, S, hd))
        lses.append(lse.swapaxes(2, 3).reshape(B, H, S))
    alpha = jax.nn.softmax(jnp.stack(lses, axis=0), axis=0)
    return jnp.einsum('pbhs,pbhsd->bhsd', alpha.astype(q.dtype), jnp.stack(outs, axis=0))


def forgetting_attention(q, k, v, f_logit):
    S = q.shape[2]
    log_f = jax.nn.log_sigmoid(f_logit.astype(jnp.float32)).transpose(0, 2, 1)
    F = lax.cumsum(log_f, axis=2)
    outs = []
    for i in range(S // Q_BLOCK):
        lo, hi = i * Q_BLOCK, (i + 1) * Q_BLOCK
        s = jnp.einsum('bhqd,bhkd->bhqk', q[:, :, lo:hi], k[:, :, :hi],
                       preferred_element_type=jnp.float32) * ATTN_SCALE
        s = s + F[:, :, lo:hi, None] - F[:, :, None, :hi]
        causal = (lo + jnp.arange(Q_BLOCK))[:, None] >= jnp.arange(hi)[None, :]
        p = jax.nn.softmax(jnp.where(causal, s, NEG), axis=-1)
        outs.append(jnp.einsum('bhqk,bhkd->bhqd', p.astype(v.dtype), v[:, :, :hi]))
    return jnp.concatenate(outs, axis=2)


def hybrid_mixer(h, positions, w_in, b_forget, g_out_a, g_out_b, w_out):
    B, S, _ = h.shape
    proj = h @ w_in
    qa, ka, va, qb, kb, vb, f_logit = jnp.split(proj, COL_OFFSETS, axis=-1)
    heads = lambda t, n: t.reshape(B, S, n, HEAD_DIM).transpose(0, 2, 1, 3)
    qa = partial_rotary(heads(qa, N_HEADS_A), positions)
    ka = partial_rotary(heads(ka, N_HEADS_A), positions)
    out_a = dilated_mixture_attention(qa, ka, heads(va, N_HEADS_A))
    out_b = forgetting_attention(heads(qb, N_HEADS_B), heads(kb, N_HEADS_B), heads(vb, N_HEADS_B),
                                 f_logit + b_forget)
    flat = lambda t: t.transpose(0, 2, 1, 3).reshape(B, S, -1)
    merged = jnp.concatenate([rmsnorm(flat(out_a), g_out_a), rmsnorm(flat(out_b), g_out_b)], axis=-1)
    return merged @ w_out


def setup_inputs(seed: int = 0) -> dict:
    key = jax.random.key(seed)
    ks = jax.random.split(key, 24)
    nrm = lambda k, shape, s: jax.random.normal(k, shape, jnp.float32) * s
    gain = lambda k, n: 1.0 + 0.05 * jax.random.normal(k, (DEPTH, n), jnp.float32)
    return {
        'x': nrm(ks[0], (BATCH, SEQ, D_MODEL), 1.0),
        'c': nrm(ks[1], (BATCH, D_MODEL), 1.0),
        'positions': jnp.broadcast_to(jnp.arange(SEQ, dtype=jnp.int32)[None, :], (BATCH, SEQ)),
        'w_ada': nrm(ks[2], (DEPTH, D_MODEL, N_MOD * D_MODEL), 0.01),
        'b_ada': nrm(ks[3], (DEPTH, N_MOD * D_MODEL), 0.02),
        'g_pre_ff1': gain(ks[4], D_MODEL),
        'g_post_ff1': gain(ks[5], D_MODEL),
        'w_ff1_gate': nrm(ks[6], (DEPTH, D_MODEL, D_FF), D_MODEL ** -0.5),
        'w_ff1_up': nrm(ks[7], (DEPTH, D_MODEL, D_FF), D_MODEL ** -0.5),
        'w_ff1_down': nrm(ks[8], (DEPTH, D_FF, D_MODEL), D_FF ** -0.5),
        'g_pre_mix': gain(ks[9], D_MODEL),
        'g_post_mix': gain(ks[10], D_MODEL),
        'w_in': nrm(ks[11], (DEPTH, D_MODEL, IN_COLS), D_MODEL ** -0.5),
        'b_forget': jax.random.uniform(ks[12], (DEPTH, N_HEADS_B), jnp.float32, 1.0, 4.0),
        'g_out_a': gain(ks[13], WIDTH_A),
        'g_out_b': gain(ks[14], WIDTH_B),
        'w_out': nrm(ks[15], (DEPTH, D_MODEL, D_MODEL), D_MODEL ** -0.5),
        'g_pre_ff2': gain(ks[16], D_MODEL),
        'g_post_ff2': gain(ks[17], D_MODEL),
        'w_ff2_gate': nrm(ks[18], (DEPTH, D_MODEL, D_FF), D_MODEL ** -0.5),
        'w_ff2_up': nrm(ks[19], (DEPTH, D_MODEL, D_FF), D_MODEL ** -0.5),
        'w_ff2_down': nrm(ks[20], (DEPTH, D_FF, D_MODEL), D_FF ** -0.5),
    }


def reference(x, c, positions, w_ada, b_ada, g_pre_ff1, g_post_ff1, w_ff1_gate, w_ff1_up, w_ff1_down,
              g_pre_mix, g_post_mix, w_in, b_forget, g_out_a, g_out_b, w_out,
              g_pre_ff2, g_post_ff2, w_ff2_gate, w_ff2_up, w_ff2_down):
    B = x.shape[0]
    silu_c = jax.nn.silu(c)
    for l in range(DEPTH):
        mod = (silu_c @ w_ada[l] + b_ada[l]).reshape(B, N_MOD, D_MODEL)
        m = lambda i: mod[:, i][:, None, :]
        h = rmsnorm(x, g_pre_ff1[l]) * (1.0 + m(1)) + m(0)
        y = rmsnorm(swiglu(h, w_ff1_gate[l], w_ff1_up[l], w_ff1_down[l]), g_post_ff1[l])
        x = x + 0.5 * m(2) * y
        h = rmsnorm(x, g_pre_mix[l]) * (1.0 + m(4)) + m(3)
        y = rmsnorm(hybrid_mixer(h, positions, w_in[l], b_forget[l], g_out_a[l], g_out_b[l], w_out[l]),
                    g_post_mix[l])
        x = x + m(5) * y
        h = rmsnorm(x, g_pre_ff2[l]) * (1.0 + m(7)) + m(6)
        y = rmsnorm(swiglu(h, w_ff2_gate[l], w_ff2_up[l], w_ff2_down[l]), g_post_ff2[l])
        x = x + 0.5 * m(8) * y
    return x
```

```python
import contextlib
import math

import numpy as np
import ml_dtypes

import concourse.bass as bass
import concourse.mybir as mybir
from concourse.bass_utils import run_bass_kernel_spmd

F32 = mybir.dt.float32
BF16 = mybir.dt.bfloat16
I32 = mybir.dt.int32
AF = mybir.ActivationFunctionType
ALU = mybir.AluOpType

D = 1024
KC = 8
DFF = 2752
FCH = 22
HD = 64
NHA = 8
NHB = 8
INCOLS = 3080
EPS = 1e-6
NCORES = 8
ROT = 16
THETA = 500000.0

MAGIC = 12582912.0
TWO_PI_HI = 6.28125
TWO_PI_LO = float(np.float32(2 * math.pi - 6.28125))
PI_SAFE = 3.1415925


class Buf:
    __slots__ = ("name", "w", "r", "dsem", "dcount")

    def __init__(self, name):
        self.name = name
        self.w = {}
        self.r = {}
        self.dsem = None
        self.dcount = 0


class Eng:
    def __init__(self, name, h, sem):
        self.name = name
        self.h = h
        self.sem = sem
        self.ticks = 0
        self.seen = {}
        self.stream = []


class Sched:
    def __init__(self, nc, es):
        self.nc = nc
        self.es = es
        self.sems = {}
        self.eng = {}
        for name, h in (("pe", nc.tensor), ("act", nc.scalar), ("dve", nc.vector),
                        ("pool", nc.gpsimd), ("sp", nc.sync)):
            sem = es.enter_context(nc.semaphore("e_" + name))
            self.eng[name] = Eng(name, h, sem)
            self.sems[("eng", name)] = sem
        self.nbuf = 0
        self.dry = False

    def buf(self, name):
        self.nbuf += 1
        return Buf(f"{name}_{self.nbuf}")

    def _deps(self, en, reads, writes):
        raw = {}
        war = {}
        for b in reads:
            for k, v in b.w.items():
                if raw.get(k, 0) < v:
                    raw[k] = v
        for b in writes:
            for k, v in b.w.items():
                if raw.get(k, 0) < v:
                    raw[k] = v
            for k, v in b.r.items():
                if war.get(k, 0) < v:
                    war[k] = v
        own = ("eng", en)
        deps = dict(raw)
        if en == "pe":
            deps.pop(own, None)
        for k, v in war.items():
            if k == own:
                continue
            if deps.get(k, 0) < v:
                deps[k] = v
        return deps

    def _emit_waits(self, E, deps):
        for k, v in deps.items():
            if E.seen.get(k, 0) >= v:
                continue
            E.h.wait_ge(self.sems[k], v)
            E.seen[k] = v
            E.stream.append(("wait", k, v))

    def op(self, en, fn, reads=(), writes=(), signal=True):
        if self.dry:
            return None
        E = self.eng[en]
        deps = self._deps(en, reads, writes)
        self._emit_waits(E, deps)
        inst = fn(E.h)
        key = ("eng", en)
        if signal:
            E.ticks += 1
            inst.then_inc(E.sem, 1)
            tick = E.ticks
            E.stream.append(("op", key, 1))
        else:
            tick = E.ticks + 1
            E.stream.append(("op", None, 0))
        for b in writes:
            if b.w.get(key, 0) < tick:
                b.w[key] = tick
        for b in reads:
            if b.r.get(key, 0) < tick:
                b.r[key] = tick
        return inst

    def dma(self, qn, out, in_, reads=(), writes=()):
        if self.dry:
            return None
        Q = self.eng[qn]
        deps = self._deps(qn, reads, writes)
        self._emit_waits(Q, deps)
        prim = writes[0] if writes else reads[0]
        if prim.dsem is None:
            prim.dsem = self.es.enter_context(self.nc.semaphore("d_" + prim.name))
            self.sems[("dma", prim.name)] = prim.dsem
        prim.dcount += 16
        key = ("dma", prim.name)
        Q.h.dma_start(out=out, in_=in_).then_inc(prim.dsem, 16)
        Q.stream.append(("op", key, 16))
        for b in writes:
            b.w[key] = prim.dcount
        for b in reads:
            b.r[key] = prim.dcount

    def final_wait(self, en, bufs):
        E = self.eng[en]
        deps = {}
        for b in bufs:
            for k, v in list(b.w.items()) + list(b.r.items()):
                if deps.get(k, 0) < v:
                    deps[k] = v
        deps.pop(("eng", en), None)
        self._emit_waits(E, deps)

    def check_deadlock(self):
        cnt = {}
        ptr = {n: 0 for n in self.eng}
        pending_fwd = {n: False for n in self.eng}
        progress = True
        while progress:
            progress = False
            for n, E in self.eng.items():
                while ptr[n] < len(E.stream):
                    kind, k, v = E.stream[ptr[n]]
                    if kind == "wait":
                        if cnt.get(k, 0) >= v:
                            ptr[n] += 1
                            progress = True
                        else:
                            break
                    else:
                        if k is not None:
                            cnt[k] = cnt.get(k, 0) + v
                        ptr[n] += 1
                        progress = True
        stuck = {n: (ptr[n], len(E.stream), E.stream[ptr[n]] if ptr[n] < len(E.stream) else None)
                 for n, E in self.eng.items() if ptr[n] < len(E.stream)}
        if stuck:
            raise RuntimeError(f"semaphore deadlock in generated program: {stuck} cnt={ {k: cnt.get(k) for _, (_, _, s) in stuck.items() for k in [s[1]]} }")


class Builder:
    def __init__(self, S, NSEQ):
        assert S % 512 == 0
        self.S = S
        self.NSEQ = NSEQ
        self.NB = S // 128
        self.NG = S // 512
        self.T = min(1024, S)
        self.NTG = S // self.T
        self.SGT = self.T // 512

    def sb(self, name, shape, dt):
        return self.es.enter_context(self.nc.sbuf_tensor(name, shape, dt))

    def build(self):
        nc = bass.Bass("TRN2", target_bir_lowering=False)
        self.nc = nc
        S, NSEQ, NB, NG = self.S, self.NSEQ, self.NB, self.NG
        dr = lambda name, shape, dt, kind="ExternalInput": nc.dram_tensor(name, shape, dt, kind=kind).ap()
        self.x_d = dr("x", [NSEQ, S, D], F32)
        self.cT_d = dr("cT", [128, KC, NSEQ], F32)
        self.pos_d = dr("pos", [128, NSEQ, NB], I32)
        self.w_ada_d = dr("w_ada", [D, 9 * D], F32)
        self.b_adaT_d = dr("b_adaT", [128, 72], F32)
        self.gains_d = dr("gains", [128, 7, KC], F32)
        self.bforget_d = dr("bforget", [128, NHB], F32)
        self.invfreq_d = dr("invfreq", [128, 8], F32)
        self.wg_d = [dr("w_ff1_gate", [D, DFF], F32), dr("w_ff2_gate", [D, DFF], F32)]
        self.wu_d = [dr("w_ff1_up", [D, DFF], F32), dr("w_ff2_up", [D, DFF], F32)]
        self.wd_d = [dr("w_ff1_down", [DFF, D], F32), dr("w_ff2_down", [DFF, D], F32)]
        self.w_in_d = dr("w_in", [D, INCOLS], F32)
        self.w_out_d = dr("w_out", [D, D], F32)
        self.c_identf_d = dr("c_identf", [128, 128], F32)
        self.c_trif_d = dr("c_trif", [128, 128], F32)
        self.c_mult_d = dr("c_mult", [128, 16 * 128], F32)
        self.c_neg_d = dr("c_neg", [128, 128], F32)
        self.out_d = dr("out", [NSEQ, S, D], F32, kind="ExternalOutput")

        with contextlib.ExitStack() as es:
            self.es = es
            self.sc = Sched(nc, es)
            self.alloc()
            self.slab_plan = []
            self.deferred = []
            for dry in (True, False):
                self.sc.dry = dry
                self.slab_count = 0
                self.slab_emitted = 0
                self.setup()
                for b in range(NSEQ):
                    self.sequence(b)
            self.sc.final_wait("sp", self.stage_b)
            self.sc.check_deadlock()
        return nc

    def alloc(self):
        nc, sc = self.nc, self.sc
        S, NSEQ, NB, T = self.S, self.NSEQ, self.NB, self.T
        sb = self.sb
        B = sc.buf
        self.xT = sb("xT", [128, KC, S], F32)
        self.xT_b = [B("xT") for _ in range(self.NG)]
        self.identb = sb("identb", [128, 128], BF16)
        self.identf = sb("identf", [128, 128], F32)
        self.onesb = sb("onesb", [128, 128], BF16)
        self.onesf = sb("onesf", [128, 128], F32)
        self.trif = sb("trif", [128, 128], F32)
        self.multm = sb("multm", [128, 16, 128], BF16)
        self.negm = sb("negm", [128, 128], BF16)
        self.const_b = B("const")
        self.cT = sb("cT_sb", [128, KC, NSEQ], F32)
        self.scT = sb("scT", [128, KC, NSEQ], BF16)
        self.sctmp = sb("sctmp", [128, KC, NSEQ], F32)
        self.modT = sb("modT", [128, 72, NSEQ], F32)
        self.b_adaT = sb("b_adaT_sb", [128, 72], F32)
        self.gains = sb("gains_sb", [128, 7, KC], F32)
        self.mods = sb("mods", [128, 9, KC, NSEQ], F32)
        self.bforget = sb("bforget_sb", [128, NHB], F32)
        self.invfreq = sb("invfreq_sb", [128, 8], F32)
        self.pos_i = sb("pos_i", [128, NSEQ, NB], I32)
        self.small_b = B("small")
        self.mod_b = B("mod")
        self.cq2 = sb("cq2", [128, NB, 16], F32)
        self.sq1 = sb("sq1", [128, NB, 8], F32)
        self.ck2 = sb("ck2", [128, NB, 16], F32)
        self.sk1 = sb("sk1", [128, NB, 8], F32)
        self.rope_b = B("rope")
        self.ropetmp_b = B("ropetmp")
        self.NSLOT = 3
        self.ring = [sb(f"ring{i}", [128, 4096], BF16) for i in range(self.NSLOT)]
        self.ring_b = [B(f"ring{i}") for i in range(self.NSLOT)]
        self.ring_i = 0
        self.wf = sb("wf", [128, KC, 8], BF16)
        self.wf_b = B("wf")
        REG = (int(self.nc.sbuf_bytes_remaining) - 256) // 256 * 256
        self.region = sb("region", [128, REG // 4], F32)
        self.REG = REG

        def carve(off, shape, dt):
            esz = 2 if dt == BF16 else 4
            n = int(np.prod(shape[1:]))
            assert off % 4 == 0 and off + n * esz <= REG, (off, shape, REG)
            ap = self.region[:, off // 4: off // 4 + (n * esz) // 4]
            if dt != F32:
                ap = ap.bitcast(dt)
            if len(shape) == 3:
                ap = ap.rearrange("p (a b) -> p a b", a=shape[1])
            elif len(shape) == 4:
                ap = ap.rearrange("p (a b c) -> p a b c", a=shape[1], b=shape[2])
            elif len(shape) == 5:
                ap = ap.rearrange("p (a b c d) -> p a b c d", a=shape[1], b=shape[2], c=shape[3])
            return ap, off + n * esz

        self.carve = carve
        SGT = self.SGT
        o = 0
        self.aT, o = carve(o, [128, FCH, T], BF16)
        self.aT_b = [B("aT") for _ in range(SGT)]
        r1 = o
        self.hT_f, o = carve(o, [128, KC, T], BF16)
        self.hTf_b = [B("hTf") for _ in range(SGT)]
        self.SQ, o = carve(o, [128, 4, 512], BF16)
        self.SQ_b = B("SQ")
        self.E = []
        for i in range(2):
            e, o = carve(o, [128, 512], F32)
            self.E.append(e)
        self.E_b = [B("E0"), B("E1")]
        self.LN, o = carve(o, [128, 512], F32)
        self.RSTD, o = carve(o, [128, 512], F32)
        self.LN_b = B("LN")
        self.HTMP = []
        for i in range(2):
            e, o = carve(o, [128, 512], F32)
            self.HTMP.append(e)
        self.HTMP_b = [B("HT0"), B("HT1")]
        self.ystage_f, o2 = carve(r1, [128, KC, T], F32)
        o = max(o, o2)
        self.R1_bufs = self.hTf_b + [self.SQ_b, self.LN_b] + self.E_b + self.HTMP_b
        self.SQ2, o = carve(o, [128, 4, 512], BF16)
        self.SQ2_b = B("SQ2")
        self.LN2, o = carve(o, [128, 512], F32)
        self.RSTD2, o = carve(o, [128, 512], F32)
        self.LN2_b = B("LN2")
        self.UTMP = []
        for i in range(2):
            e, o = carve(o, [128, 512], F32)
            self.UTMP.append(e)
        self.UTMP_b = [B("UT0"), B("UT1")]
        self.ffn_end = o
        o = 0
        self.merged, o = carve(o, [128, NB, D], BF16)
        self.merged_b = B("merged")
        self.hT_m, o = carve(o, [128, KC, S], BF16)
        self.hTm_b = B("hTm")
        m_un = o
        self.QT, o = carve(o, [128, S], BF16)
        self.KT, o = carve(o, [128, S], BF16)
        self.KT1, o = carve(o, [128, S], BF16)
        self.VAs = []
        for i in range(2):
            e, o = carve(o, [128, NB, 2, 66], BF16)
            self.VAs.append(e)
        self.VA = self.VAs[0]
        self.QKtm, o = carve(o, [128, NB, 2, 128], BF16)
        self.NPT = 7
        self.PT = []
        for i in range(self.NPT):
            e, o = carve(o, [128, 512], BF16)
            self.PT.append(e)
        self.PT_b = [B("PT") for _ in range(self.NPT)]
        rotf_off = o
        self.ROTF, o = carve(o, [128, NB, 2, 2, 16], F32)
        self.RH = max(NB // 2, 1)
        self.RA, o = carve(o, [128, self.RH, 2, 16], F32)
        self.RB1, o = carve(o, [128, self.RH, 2, 8], F32)
        self.RB2, o = carve(o, [128, self.RH, 2, 8], F32)
        self.RDEN, o = carve(o, [128, 4], F32)
        m1_end = o
        self.QT_b, self.KT_b, self.QKtm_b = B("QT"), B("KT"), B("QKtm")
        self.VAs_b = [B("VA0"), B("VA1")]
        self.KT1_b = B("KT1")
        self.rtmp_b = B("rtmp")
        self.rden_b = B("rden")
        o = m_un
        self.SQm, o = carve(o, [128, 4, 512], BF16)
        self.LNm, o = carve(o, [128, 512], F32)
        self.RSTDm, o = carve(o, [128, 512], F32)
        self.HTMPm = []
        for i in range(2):
            e, o = carve(o, [128, 512], F32)
            self.HTMPm.append(e)
        self.SQm_b, self.LNm_b, self.HTMPm_b = B("SQm"), B("LNm"), [B("HTm0"), B("HTm1")]
        m0_end = o
        o = m_un
        self.ystage_m, o = carve(o, [128, KC, 512], F32)
        self.ysm_b = B("ysm")
        self.MN = []
        for i in range(2):
            e, o = carve(o, [128, D], BF16)
            self.MN.append(e)
        self.MN_b = [B("MN0"), B("MN1")]
        self.SSAB, o = carve(o, [128, NB, 2], F32)
        self.RSAB, o = carve(o, [128, NB, 2], F32)
        self.SQJ, o = carve(o, [128, 512], BF16)
        self.ss_b = B("ssab")
        self.SQ2m, o = carve(o, [128, 4, 512], BF16)
        self.SQ2m_b = B("SQ2m")
        self.LN2m, o = carve(o, [128, 512], F32)
        self.RSTD2m, o = carve(o, [128, 512], F32)
        self.LN2m_b = B("LN2m")
        self.UTMPm = []
        for i in range(2):
            e, o = carve(o, [128, 512], F32)
            self.UTMPm.append(e)
        self.UTMPm_b = [B("UTm0"), B("UTm1")]
        m2_end = o
        alias_f = rotf_off >= m0_end
        o = rotf_off if alias_f else max(m1_end, m0_end)
        self.FL, o = carve(o, [128, NB, 8], F32)
        self.FL2, o = carve(o, [128, NB, 8], F32)
        self.NEGF, o = carve(o, [128, NB, 8], F32)
        self.CARRY, o = carve(o, [128, NB, 8], F32)
        self.F32T, o = carve(o, [128, NB, 8], F32)
        if alias_f:
            assert o <= rotf_off + NB * 2 * 2 * 16 * 4
            o = max(m1_end, m0_end)
        self.FH, o = carve(o, [128, NB, 8], BF16)
        self.FM, o = carve(o, [128, NB, 8], BF16)
        self.FLo, o = carve(o, [128, NB, 8], BF16)
        self.F_b = B("F")
        self.aug_b = B("aug")
        self.mix_end = max(o, m2_end)
        o = max(m2_end, self.ffn_end)
        self.RSTDN, o = carve(o, [128, self.NG, 512], F32)
        self.rstdn_b = [B("rstdn") for _ in range(self.NG)]
        self.layout_info = dict(REG=REG, ffn_end=self.ffn_end, m1_end=m1_end, m0_end=m0_end, m2_end=m2_end, mix_end=self.mix_end)
        self.NSTAGE = 4
        self.stage = []
        o = 0
        for i in range(self.NSTAGE):
            e, o = carve(o, [128, D], F32)
            self.stage.append(e)
        self.stage_b = [B(f"stage{i}") for i in range(self.NSTAGE)]
        self.rt = []
        for i in range(6):
            e, o = carve(o, [128, NB, 8], F32)
            self.rt.append(e)
        self.region_b = B("region")
        self.bank = [self.es.enter_context(nc.psum_tensor(f"bank{i}", [128, 512], F32)) for i in range(8)]
        self.bank_b = [B(f"bank{i}") for i in range(8)]

    def defer(self, delay, fn):
        self.deferred.append([delay, fn])

    def tick_deferred(self, flush=False):
        keep = []
        for item in self.deferred:
            item[0] -= 1
            if flush or item[0] <= 0:
                item[1]()
            else:
                keep.append(item)
        self.deferred = keep

    def tp_view(self, i):
        return self.bank[i][:].bitcast(BF16)

    def phase_sync(self):
        sc = self.sc
        if sc.dry:
            return
        snap = {("eng", n): E.ticks for n, E in sc.eng.items() if E.ticks > 0}
        for k, v in self.region_b.w.items():
            if k[0] == "dma":
                snap[k] = max(snap.get(k, 0), v)
        for n, E in sc.eng.items():
            deps = dict(snap)
            deps.pop(("eng", n), None)
            sc._emit_waits(E, deps)

    def slab(self, parts, keep=0):
        idx = self.slab_count
        self.slab_count += 1
        if self.sc.dry:
            self.slab_plan.append(parts)
            return idx % self.NSLOT
        plan = self.slab_plan
        while self.slab_emitted <= min(idx + self.NSLOT - 1 - keep, len(plan) - 1):
            i = self.slab_emitted
            si = i % self.NSLOT
            for dst_fn, srcap in plan[i]:
                self.sc.dma("pool", dst_fn(self.ring[si]), srcap, writes=[self.ring_b[si]])
            self.slab_emitted += 1
        return idx % self.NSLOT

    def setup(self):
        nc, sc = self.nc, self.sc
        NSEQ = self.NSEQ
        small = self.small_b
        for dst, src in ((self.cT, self.cT_d), (self.b_adaT, self.b_adaT_d), (self.gains, self.gains_d),
                         (self.bforget, self.bforget_d), (self.invfreq, self.invfreq_d),
                         (self.pos_i, self.pos_d), (self.identf, self.c_identf_d), (self.trif, self.c_trif_d)):
            sc.dma("sp", dst[:], src, writes=[small])
        cb = self.const_b
        sc.dma("pool", self.identb[:], self.c_identf_d, writes=[cb])
        sc.dma("pool", self.multm[:].rearrange("p a b -> p (a b)"), self.c_mult_d, writes=[cb])
        sc.dma("pool", self.negm[:], self.c_neg_d, writes=[cb])
        sc.op("dve", lambda e: e.memset(self.onesb[:], 1.0), writes=[cb])
        sc.op("dve", lambda e: e.memset(self.onesf[:], 1.0), writes=[cb])
        mb = self.mod_b
        sc.op("act", lambda e: e.activation(self.sctmp[:], self.cT[:], AF.Exp, scale=-1.0), reads=[small], writes=[mb])
        sc.op("dve", lambda e: e.tensor_scalar(out=self.sctmp[:], in0=self.sctmp[:], scalar1=1.0, scalar2=None, op0=ALU.add), writes=[mb])
        sc.op("dve", lambda e: e.reciprocal(out=self.sctmp[:], in_=self.sctmp[:]), writes=[mb])
        sc.op("dve", lambda e: e.tensor_tensor(out=self.scT[:], in0=self.cT[:], in1=self.sctmp[:], op=ALU.mult), reads=[small], writes=[mb])
        wv = self.w_ada_d.rearrange("(k p) f -> p k f", p=128)
        for sl in range(18):
            si = self.slab([(lambda r: r[:].rearrange("p (k f) -> p k f", k=KC), wv[:, :, sl * 512:(sl + 1) * 512])])
            slot = self.ring[si][:].rearrange("p (k f) -> p k f", k=KC)
            bk = sl % 2
            ps = self.bank[bk]
            for sub in range(4):
                for k in range(KC):
                    sc.op("pe", lambda e, k=k, sub=sub: e.matmul(ps[:, sub * NSEQ:(sub + 1) * NSEQ],
                                                                   slot[:, k, sub * 128:(sub + 1) * 128],
                                                                   self.scT[:, k, :], start=(k == 0), stop=(k == KC - 1)),
                          reads=[self.ring_b[si], mb], writes=[self.bank_b[bk]], signal=(k == KC - 1))
            sc.op("dve", lambda e, sl=sl, ps=ps: e.tensor_tensor(
                out=self.modT[:, sl * 4:(sl + 1) * 4, :],
                in0=ps[:, 0:4 * NSEQ].rearrange("p (a b) -> p a b", a=4),
                in1=self.b_adaT[:, sl * 4:(sl + 1) * 4].unsqueeze(2).to_broadcast([128, 4, NSEQ]), op=ALU.add),
                reads=[small], writes=[self.bank_b[bk], mb])
        gi_pre = (0, 2, 4)
        gi_post = (1, 3, 5)
        for s in range(3):
            m_shift = self.modT[:, (3 * s) * 8:(3 * s + 1) * 8, :]
            m_scale = self.modT[:, (3 * s + 1) * 8:(3 * s + 2) * 8, :]
            m_gate = self.modT[:, (3 * s + 2) * 8:(3 * s + 3) * 8, :]
            gpre = self.gains[:, gi_pre[s], :].unsqueeze(2).to_broadcast([128, KC, NSEQ])
            gpost = self.gains[:, gi_post[s], :].unsqueeze(2).to_broadcast([128, KC, NSEQ])
            sc.op("dve", lambda e, m_scale=m_scale, gpre=gpre, s=s: e.scalar_tensor_tensor(
                out=self.mods[:, 3 * s + 0, :, :], in0=m_scale, scalar=1.0, in1=gpre, op0=ALU.add, op1=ALU.mult),
                reads=[small], writes=[mb])
            sc.op("dve", lambda e, m_shift=m_shift, s=s: e.tensor_copy(self.mods[:, 3 * s + 1, :, :], m_shift), writes=[mb])
            sc.op("dve", lambda e, m_gate=m_gate, gpost=gpost, s=s: e.scalar_tensor_tensor(
                out=self.mods[:, 3 * s + 2, :, :], in0=m_gate, scalar=(0.5 if s != 1 else 1.0), in1=gpost,
                op0=ALU.mult, op1=ALU.mult), reads=[small], writes=[mb])

    def sequence(self, b):
        self.phase_sync()
        self.load_x(b)
        self.rope_tables(b)
        self.phase_sync()
        self.ffn(b, 0, 0)
        self.phase_sync()
        self.mixer(b)
        self.phase_sync()
        self.ffn(b, 2, 1)
        self.phase_sync()
        self.store_x(b)

    def load_x(self, b):
        sc = self.sc
        for blk in range(self.NB):
            st = blk % self.NSTAGE
            sc.dma("sp", self.stage[st][:], self.x_d[b, blk * 128:(blk + 1) * 128, :], writes=[self.stage_b[st]])
            for half in range(2):
                bk = 2 * (blk % 2) + half
                for cc in range(4):
                    c = half * 4 + cc
                    sc.op("pe", lambda e, c=c, cc=cc, st=st, bk=bk: e.transpose(
                        self.bank[bk][:, cc * 128:(cc + 1) * 128], self.stage[st][:, c * 128:(c + 1) * 128], self.identf[:]),
                        reads=[self.stage_b[st], self.small_b], writes=[self.bank_b[bk]], signal=(cc == 3))
                src = self.bank[bk][:].rearrange("p (a b) -> p a b", a=4)
                dst = self.xT[:, half * 4:(half + 1) * 4, blk * 128:(blk + 1) * 128]
                if half == 0:
                    sc.op("act", lambda e, dst=dst, src=src: e.copy(dst, src), writes=[self.bank_b[bk], self.xT_b[blk // 4]])
                else:
                    sc.op("dve", lambda e, dst=dst, src=src: e.tensor_copy(dst, src), writes=[self.bank_b[bk], self.xT_b[blk // 4]])
            if blk % 4 == 3:
                self.stats_ahead(blk // 4, self.SQ2, self.SQ2_b, self.LN2, self.LN2_b, ssbank=4)

    def store_x(self, b):
        sc = self.sc
        for blk in range(self.NB):
            st = blk % self.NSTAGE
            for half in range(2):
                bk = 2 * (blk % 2) + half
                for cc in range(4):
                    c = half * 4 + cc
                    sc.op("pe", lambda e, c=c, cc=cc, bk=bk, blk=blk: e.transpose(
                        self.bank[bk][:, cc * 128:(cc + 1) * 128], self.xT[:, c, blk * 128:(blk + 1) * 128], self.identf[:]),
                        reads=[self.xT_b[blk // 4], self.small_b], writes=[self.bank_b[bk]], signal=(cc == 3))
                dst = self.stage[st][:, half * 512:(half + 1) * 512]
                src = self.bank[bk][:]
                if half == 0:
                    sc.op("act", lambda e, dst=dst, src=src: e.copy(dst, src), writes=[self.bank_b[bk], self.stage_b[st]])
                else:
                    sc.op("dve", lambda e, dst=dst, src=src: e.tensor_copy(dst, src), writes=[self.bank_b[bk], self.stage_b[st]])
            sc.dma("sp", self.out_d[b, blk * 128:(blk + 1) * 128, :], self.stage[st][:], reads=[self.stage_b[st]])
        for st in range(self.NSTAGE):
            for k, v in list(self.stage_b[st].r.items()) + list(self.stage_b[st].w.items()):
                self.region_b.w[k] = max(self.region_b.w.get(k, 0), v)

    def rope_tables(self, b):
        sc = self.sc
        NB = self.NB
        rb, tb = self.rope_b, self.ropetmp_b
        posf, ang, t, k, r1, r2 = self.rt
        V = lambda fn, reads=(), writes=(): sc.op("dve", fn, reads=reads, writes=writes)
        V(lambda e: e.tensor_copy(posf[:], self.pos_i[:, b, :].unsqueeze(2).to_broadcast([128, NB, 8])),
          reads=[self.small_b], writes=[tb])
        V(lambda e: e.tensor_tensor(out=ang[:], in0=posf[:], in1=self.invfreq[:].unsqueeze(1).to_broadcast([128, NB, 8]),
                                    op=ALU.mult), reads=[self.small_b], writes=[tb])
        for which in ("sin", "cos"):
            src = ang
            if which == "cos":
                V(lambda e: e.tensor_scalar(out=posf[:], in0=ang[:], scalar1=math.pi / 2, scalar2=None, op0=ALU.add), writes=[tb])
                src = posf
            V(lambda e, src=src: e.tensor_scalar(out=t[:], in0=src[:], scalar1=1.0 / (2 * math.pi), scalar2=MAGIC,
                                                 op0=ALU.mult, op1=ALU.add), writes=[tb])
            V(lambda e: e.tensor_scalar(out=k[:], in0=t[:], scalar1=MAGIC, scalar2=None, op0=ALU.subtract), writes=[tb])
            V(lambda e, src=src: e.scalar_tensor_tensor(out=r1[:], in0=k[:], scalar=-TWO_PI_HI, in1=src[:],
                                                        op0=ALU.mult, op1=ALU.add), writes=[tb])
            V(lambda e: e.scalar_tensor_tensor(out=r2[:], in0=k[:], scalar=-TWO_PI_LO, in1=r1[:],
                                               op0=ALU.mult, op1=ALU.add), writes=[tb])
            V(lambda e: e.tensor_scalar(out=r2[:], in0=r2[:], scalar1=PI_SAFE, scalar2=-PI_SAFE, op0=ALU.min, op1=ALU.max),
              writes=[tb])
            if which == "sin":
                sc.op("act", lambda e: e.activation(self.sk1[:], r2[:], AF.Sin), reads=[tb], writes=[rb])
                sc.op("dve", lambda e: e.tensor_scalar(out=self.sq1[:], in0=self.sk1[:], scalar1=0.125, scalar2=None,
                                                       op0=ALU.mult), writes=[rb])
            else:
                for hf in range(2):
                    sc.op("act", lambda e, hf=hf: e.activation(self.ck2[:, :, hf * 8:(hf + 1) * 8], r2[:], AF.Sin),
                          reads=[tb], writes=[rb])
                sc.op("dve", lambda e: e.tensor_scalar(out=self.cq2[:], in0=self.ck2[:], scalar1=0.125, scalar2=None,
                                                       op0=ALU.mult), writes=[rb])

    def stats_ahead(self, sg, SQ, SQ_b, LN, LN_b, ssbank):
        sc = self.sc
        t0 = sg * 512
        xb = self.xT_b[sg]
        bb = self.bank_b[ssbank]
        ps = self.bank[ssbank]
        for half in range(2):
            for cc in range(4):
                c = half * 4 + cc
                if cc % 2 == 0:
                    sc.op("act", lambda e, c=c, cc=cc: e.activation(SQ[:, cc, :], self.xT[:, c, t0:t0 + 512], AF.Square),
                          reads=[xb], writes=[SQ_b])
                else:
                    sc.op("dve", lambda e, c=c, cc=cc: e.tensor_tensor(out=SQ[:, cc, :], in0=self.xT[:, c, t0:t0 + 512],
                                                                       in1=self.xT[:, c, t0:t0 + 512], op=ALU.mult),
                          reads=[xb], writes=[SQ_b])
            for cc in range(4):
                c = half * 4 + cc
                sc.op("pe", lambda e, c=c, cc=cc: e.matmul(ps[:], self.onesb[:], SQ[:, cc, :], start=(c == 0), stop=(c == KC - 1)),
                      reads=[SQ_b, self.const_b], writes=[bb], signal=(cc == 3))
        sc.op("act", lambda e: e.activation(LN[:], ps[:], AF.Ln, bias=EPS, scale=1.0 / D), writes=[bb, LN_b])
        sc.op("act", lambda e: e.activation(self.RSTDN[:, sg, :], LN[:], AF.Exp, scale=-0.5), reads=[LN_b], writes=[self.rstdn_b[sg]])

    def prenorm_apply(self, b, s, sg, hT_dst, hT_buf, HTMP, HTMP_b):
        sc = self.sc
        t0 = sg * 512
        xb = self.xT_b[sg]
        G = self.mods[:, 3 * s + 0, :, :]
        Sh = self.mods[:, 3 * s + 1, :, :]
        for c in range(KC):
            tmp = HTMP[c % 2]
            tb = HTMP_b[c % 2]
            sc.op("dve", lambda e, c=c, tmp=tmp: e.scalar_tensor_tensor(
                out=tmp[:], in0=self.xT[:, c, t0:t0 + 512], scalar=G[:, c, b:b + 1], in1=self.RSTDN[:, sg, :],
                op0=ALU.mult, op1=ALU.mult), reads=[xb, self.mod_b, self.rstdn_b[sg]], writes=[tb])
            sc.op("act", lambda e, c=c, tmp=tmp: e.activation(hT_dst[:, c, :], tmp[:], AF.Identity,
                                                               bias=Sh[:, c, b:b + 1], scale=1.0),
                  reads=[self.mod_b, tb], writes=[hT_buf])

    def postnorm_update(self, b, s, sg, ystage, ybufs, LN2, RSTD2, LN2_b, UTMP, UTMP_b, ssbank):
        sc = self.sc
        t0 = sg * 512
        xb = self.xT_b[sg]
        bb = self.bank_b[ssbank]
        ps = self.bank[ssbank]
        sc.op("act", lambda e: e.activation(LN2[:], ps[:], AF.Ln, bias=EPS, scale=1.0 / D), writes=[bb, LN2_b])
        sc.op("act", lambda e: e.activation(RSTD2[:], LN2[:], AF.Exp, scale=-0.5), writes=[LN2_b])
        Gp = self.mods[:, 3 * s + 2, :, :]
        for c in range(KC):
            tmp = UTMP[c % 2]
            tb = UTMP_b[c % 2]
            sc.op("dve", lambda e, c=c, tmp=tmp: e.tensor_tensor(out=tmp[:], in0=ystage[:, c, :], in1=RSTD2[:], op=ALU.mult),
                  reads=list(ybufs) + [LN2_b], writes=[tb])
            sc.op("dve", lambda e, c=c, tmp=tmp: e.scalar_tensor_tensor(
                out=self.xT[:, c, t0:t0 + 512], in0=tmp[:], scalar=Gp[:, c, b:b + 1], in1=self.xT[:, c, t0:t0 + 512],
                op0=ALU.mult, op1=ALU.add), reads=[self.mod_b, tb], writes=[xb])

    def ffn(self, b, s, wi):
        sc = self.sc
        T, SGT = self.T, self.SGT
        wgv = self.wg_d[wi].rearrange("(k p) f -> p k f", p=128)
        wuv = self.wu_d[wi].rearrange("(k p) f -> p k f", p=128)
        wd = self.wd_d[wi]
        R1 = self.R1_bufs
        for tg in range(self.NTG):
            sg0 = tg * SGT
            for sgl in range(SGT):
                self.prenorm_apply(b, s, sg0 + sgl, self.hT_f[:, :, sgl * 512:(sgl + 1) * 512], self.hTf_b[sgl],
                                   self.HTMP, self.HTMP_b)
            nslab = (DFF + 255) // 256
            gi = 0
            for sl in range(nslab):
                c0 = sl * 256
                ncol = min(256, DFF - c0)
                si = self.slab([
                    (lambda r, ncol=ncol: r[:, 0:2048].rearrange("p (k f) -> p k f", k=KC)[:, :, 0:ncol], wgv[:, :, c0:c0 + ncol]),
                    (lambda r, ncol=ncol: r[:, 2048:4096].rearrange("p (k f) -> p k f", k=KC)[:, :, 0:ncol], wuv[:, :, c0:c0 + ncol])])
                rb = self.ring_b[si]
                slotg = self.ring[si][:, 0:2048].rearrange("p (k f) -> p k f", k=KC)
                slotu = self.ring[si][:, 2048:4096].rearrange("p (k f) -> p k f", k=KC)
                nfc = (ncol + 127) // 128
                for sgl in range(SGT):
                    hb = self.hTf_b[sgl]
                    for fcl in range(nfc):
                        fc = sl * 2 + fcl
                        fr = min(128, ncol - fcl * 128)
                        gb, ub = (gi % 2), 2 + (gi % 2)
                        E = self.E[gi % 2]
                        eb = self.E_b[gi % 2]
                        gi += 1
                        gps, ups = self.bank[gb], self.bank[ub]
                        for k in range(KC):
                            sc.op("pe", lambda e, k=k, fcl=fcl, fr=fr, gps=gps, slotg=slotg, sgl=sgl: e.matmul(
                                gps[0:fr, :], slotg[:, k, fcl * 128:fcl * 128 + fr], self.hT_f[:, k, sgl * 512:(sgl + 1) * 512],
                                start=(k == 0), stop=(k == KC - 1)),
                                reads=[rb, hb], writes=[self.bank_b[gb]], signal=(k == KC - 1))
                        for k in range(KC):
                            sc.op("pe", lambda e, k=k, fcl=fcl, fr=fr, ups=ups, slotu=slotu, sgl=sgl: e.matmul(
                                ups[0:fr, :], slotu[:, k, fcl * 128:fcl * 128 + fr], self.hT_f[:, k, sgl * 512:(sgl + 1) * 512],
                                start=(k == 0), stop=(k == KC - 1)),
                                reads=[rb, hb], writes=[self.bank_b[ub]], signal=(k == KC - 1))
                        sc.op("act", lambda e, E=E, gps=gps, fr=fr: e.activation(E[0:fr, :], gps[0:fr, :], AF.Silu),
                              writes=[self.bank_b[gb], eb])
                        sc.op("dve", lambda e, E=E, ups=ups, fr=fr, fc=fc, sgl=sgl: e.tensor_tensor(
                            out=self.aT[0:fr, fc, sgl * 512:(sgl + 1) * 512], in0=ups[0:fr, :], in1=E[0:fr, :], op=ALU.mult),
                            reads=[eb], writes=[self.bank_b[ub], self.aT_b[sgl]])
            for c in range(KC):
                si = self.slab([
                    (lambda r: r[:, 0:FCH * 128].rearrange("p (j d) -> p j d", j=FCH)[:, 0:21, :],
                     wd[0:21 * 128, c * 128:(c + 1) * 128].rearrange("(j p) d -> p j d", p=128)),
                    (lambda r: r[:, 0:FCH * 128].rearrange("p (j d) -> p j d", j=FCH)[0:64, 21, :],
                     wd[21 * 128:DFF, c * 128:(c + 1) * 128])])
                rb = self.ring_b[si]
                slot = self.ring[si][:, 0:FCH * 128].rearrange("p (j d) -> p j d", j=FCH)
                for sgl in range(SGT):
                    yb = 4 + ((c * SGT + sgl) % 2)
                    yps = self.bank[yb]
                    for j in range(FCH):
                        fr = 128 if j < 21 else 64
                        sc.op("pe", lambda e, j=j, fr=fr, yps=yps, slot=slot, sgl=sgl: e.matmul(
                            yps[:], slot[0:fr, j, :], self.aT[0:fr, j, sgl * 512:(sgl + 1) * 512],
                            start=(j == 0), stop=(j == FCH - 1)),
                            reads=[rb, self.aT_b[sgl]], writes=[self.bank_b[yb]], signal=(j == FCH - 1))
                    ssb = 6 + sgl
                    sq = self.SQ2[:, (c * SGT + sgl) % 4, :]
                    sc.op("act", lambda e, c=c, sgl=sgl, yps=yps: e.copy(self.ystage_f[:, c, sgl * 512:(sgl + 1) * 512], yps[:]),
                          writes=[self.bank_b[yb]] + R1)
                    sc.op("act", lambda e, sq=sq, yps=yps: e.activation(sq, yps[:], AF.Square),
                          writes=[self.bank_b[yb], self.SQ2_b])
                    self.tick_deferred()
                    self.defer(1, lambda c=c, ssb=ssb, sq=sq: sc.op(
                        "pe", lambda e: e.matmul(self.bank[ssb][:], self.onesb[:], sq, start=(c == 0), stop=(c == KC - 1)),
                        reads=[self.SQ2_b, self.const_b], writes=[self.bank_b[ssb]], signal=True))
            self.tick_deferred(flush=True)
            for sgl in range(SGT):
                self.postnorm_update(b, s, sg0 + sgl, self.ystage_f[:, :, sgl * 512:(sgl + 1) * 512], R1,
                                     self.LN2, self.RSTD2, self.LN2_b, self.UTMP, self.UTMP_b, ssbank=6 + sgl)
            if s == 0:
                for sgl in range(SGT):
                    self.stats_ahead(sg0 + sgl, self.SQ2, self.SQ2_b, self.LN2, self.LN2_b, ssbank=6 + sgl)

    def mixer(self, b):
        sc = self.sc
        S, NB, NG = self.S, self.NB, self.NG
        s = 1
        for sg in range(NG):
            self.prenorm_apply(b, s, sg, self.hT_m[:, :, sg * 512:(sg + 1) * 512], self.hTm_b, self.HTMPm, self.HTMPm_b)
        self.fox_prep(b)
        self.phase_sync()
        for vi in range(2):
            sc.op("dve", lambda e, vi=vi: e.memset(self.VAs[vi][:, :, :, 64:65], 1.0), writes=[self.VAs_b[vi]])
        sc.op("dve", lambda e: e.memset(self.KT[64:128, :], 0.0), writes=[self.KT_b])
        sc.op("dve", lambda e: e.memset(self.KT1[0:64, :], 0.0), writes=[self.KT1_b])
        winv = self.w_in_d.rearrange("(k p) f -> p k f", p=128)
        NU = 4 + NHB

        def inproj_gen(u):
            isA = u < 4
            vi = u % 2
            if isA:
                ncol = 384
                parts = [(lambda r, i=i: r[:, 0:KC * 384].rearrange("p (k f) -> p k f", k=KC)[:, :, i * 128:(i + 1) * 128],
                          winv[:, :, base + u * 128: base + (u + 1) * 128]) for i, base in enumerate((0, 512, 1024))]
            else:
                h = u - 4
                ncol = 192
                parts = [(lambda r, i=i: r[:, 0:KC * 192].rearrange("p (k f) -> p k f", k=KC)[:, :, i * 64:(i + 1) * 64],
                          winv[:, :, base + h * 64: base + (h + 1) * 64]) for i, base in enumerate((1536, 2048, 2560))]
            si = self.slab(parts)
            rb = self.ring_b[si]
            slot = self.ring[si][:, 0:KC * ncol].rearrange("p (k f) -> p k f", k=KC)
            for blk in range(NB):
                bk = blk % 2
                ps = self.bank[bk]
                for k in range(KC):
                    sc.op("pe", lambda e, k=k: e.matmul(
                        ps[:, 0:ncol], self.hT_m[:, k, blk * 128:(blk + 1) * 128], slot[:, k, 0:ncol],
                        start=(k == 0), stop=(k == KC - 1)),
                        reads=[rb, self.hTm_b], writes=[self.bank_b[bk]], signal=(k == KC - 1))
                if isA:
                    self.defer(2, lambda blk=blk, ps=ps, bk=bk: self.evac_A(blk, ps, self.bank_b[bk], vi))
                else:
                    self.defer(2, lambda blk=blk, ps=ps, bk=bk: self.evac_B(blk, ps, self.bank_b[bk], u - 4, vi))
                yield blk

        def stageB(u):
            self.tick_deferred(flush=True)
            isA = u < 4
            if u == 4:
                sc.op("dve", lambda e: e.memset(self.QT[64:128, :], 0.0), writes=[self.QT_b])
            self.unit_transposes(isA, u)

        def att_gen(u):
            vi = u % 2
            if u < 4:
                for h2 in range(2):
                    yield from self.attention(True, hp0=h2 * 64, K=64, vsel=h2, col0=(u * 2 + h2) * 64, vi=vi)
            else:
                yield from self.attention(False, hp0=0, K=70, vsel=0, col0=512 + (u - 4) * 64, vi=vi)

        for _ in inproj_gen(0):
            self.tick_deferred()
        self.tick_deferred(flush=True)
        for p in self.rope_A_pieces():
            p()
        stageB(0)
        for u in range(NU):
            nxt = inproj_gen(u + 1) if u + 1 < NU else None
            nsteps = (2 if u < 4 else 1) * (sum(4 * g + 4 for g in range(NG)) + 5)
            stride = max(1, (nsteps * 55 // 100) // NB)
            for i, _ in enumerate(att_gen(u)):
                if nxt is not None and i % stride == stride - 1:
                    if next(nxt, None) is None:
                        nxt = None
                        if u + 1 < 4:
                            for pi, p in enumerate(self.rope_A_pieces()):
                                self.defer(4 + 3 * pi, p)
            if nxt is not None:
                for _ in nxt:
                    self.tick_deferred()
                if u + 1 < 4:
                    self.tick_deferred(flush=True)
                    for p in self.rope_A_pieces():
                        p()
            if u + 1 < NU:
                stageB(u + 1)
        self.tick_deferred(flush=True)
        self.phase_sync()
        self.outproj(b)

    def fox_prep(self, b):
        sc = self.sc
        NB = self.NB
        Fb = self.F_b
        winv = self.w_in_d.rearrange("(k p) f -> p k f", p=128)
        sc.dma("pool", self.wf[:], winv[:, :, 3072:3080], writes=[self.wf_b])
        ps = self.bank[2]
        for blk in range(NB):
            for k in range(KC):
                sc.op("pe", lambda e, k=k, blk=blk: e.matmul(ps[:, blk * 8:(blk + 1) * 8], self.hT_m[:, k, blk * 128:(blk + 1) * 128],
                                                            self.wf[:, k, :], start=(k == 0), stop=(k == KC - 1)),
                      reads=[self.wf_b, self.hTm_b], writes=[self.bank_b[2]], signal=(k == KC - 1))
        psv = ps[:, 0:NB * 8].rearrange("p (a b) -> p a b", a=NB)
        sc.op("dve", lambda e: e.tensor_tensor(out=self.FL[:], in0=psv, in1=self.bforget[:].unsqueeze(1).to_broadcast([128, NB, 8]),
                                               op=ALU.add), reads=[self.small_b], writes=[self.bank_b[2], Fb] + self.rstdn_b)
        sc.op("act", lambda e: e.activation(self.FL2[:], self.FL[:], AF.Exp, scale=-1.0), writes=[Fb])
        sc.op("act", lambda e: e.activation(self.FL[:], self.FL2[:], AF.Ln, bias=1.0, scale=1.0), writes=[Fb])
        wps, tps = self.bank[3], self.bank[4]
        flat = self.FL[:].rearrange("p a b -> p (a b)")
        sc.op("pe", lambda e: e.matmul(wps[:, 0:NB * 8], self.trif[:], flat, start=True, stop=True),
              reads=[Fb, self.small_b], writes=[self.bank_b[3]])
        sc.op("pe", lambda e: e.matmul(tps[:, 0:NB * 8], self.onesf[:], flat, start=True, stop=True),
              reads=[Fb, self.const_b], writes=[self.bank_b[4]])
        sc.op("dve", lambda e: e.tensor_copy(self.FL2[:], tps[:, 0:NB * 8].rearrange("p (a b) -> p a b", a=NB)),
              writes=[self.bank_b[4], Fb])
        sc.op("dve", lambda e: e.memset(self.CARRY[:, 0, :], 0.0), writes=[Fb])
        for i in range(1, NB):
            sc.op("dve", lambda e, i=i: e.tensor_tensor(out=self.CARRY[:, i, :], in0=self.CARRY[:, i - 1, :],
                                                        in1=self.FL2[:, i - 1, :], op=ALU.add), writes=[Fb])
        sc.op("dve", lambda e: e.tensor_tensor(out=self.NEGF[:], in0=wps[:, 0:NB * 8].rearrange("p (a b) -> p a b", a=NB),
                                               in1=self.CARRY[:], op=ALU.add), writes=[self.bank_b[3], Fb])
        V = lambda fn: sc.op("dve", fn, writes=[Fb])
        V(lambda e: e.tensor_copy(self.FH[:], self.NEGF[:]))
        V(lambda e: e.tensor_copy(self.F32T[:], self.FH[:]))
        V(lambda e: e.tensor_tensor(out=self.NEGF[:], in0=self.NEGF[:], in1=self.F32T[:], op=ALU.subtract))
        V(lambda e: e.tensor_copy(self.FM[:], self.NEGF[:]))
        V(lambda e: e.tensor_copy(self.F32T[:], self.FM[:]))
        V(lambda e: e.tensor_tensor(out=self.NEGF[:], in0=self.NEGF[:], in1=self.F32T[:], op=ALU.subtract))
        V(lambda e: e.tensor_copy(self.FLo[:], self.NEGF[:]))

    def evac_A(self, blk, ps, bb, vi):
        sc = self.sc
        qk = self.QKtm_b
        VA, VAb = self.VAs[vi], self.VAs_b[vi]
        sc.op("dve", lambda e: e.tensor_copy(VA[:, blk, :, 0:64], ps[:, 256:384].rearrange("p (a b) -> p a b", a=2)),
              writes=[bb, VAb])
        sc.op("dve", lambda e: e.tensor_copy(
            self.ROTF[:, blk, :, :, :], ps[:, 0:256].rearrange("p (q h d) -> p q h d", q=2, h=2)[:, :, :, 0:16]),
            writes=[bb, self.rtmp_b])
        sc.op("act", lambda e: e.activation(self.QKtm[:, blk, 0, :], ps[:, 0:128], AF.Copy, scale=0.125), writes=[bb, qk])
        sc.op("act", lambda e: e.copy(self.QKtm[:, blk, 1, :], ps[:, 128:256]), writes=[bb, qk])

    def rope_A_pieces(self):
        sc = self.sc
        NB, RH = self.NB, self.RH
        rt = self.rtmp_b
        qk = self.QKtm_b
        pieces = []
        for qi, (c2, s1) in enumerate(((self.cq2, self.sq1), (self.ck2, self.sk1))):
            for n0 in range(0, NB, RH):
                def piece(qi=qi, c2=c2, s1=s1, n0=n0):
                    X = self.ROTF[:, n0:n0 + RH, qi, :, :]
                    Dv = self.QKtm[:, n0:n0 + RH, qi, :].rearrange("p n (h d) -> p n h d", h=2)
                    cosb = c2[:, n0:n0 + RH, :].unsqueeze(2).to_broadcast([128, RH, 2, 16])
                    sinb = s1[:, n0:n0 + RH, :].unsqueeze(2).to_broadcast([128, RH, 2, 8])
                    sc.op("dve", lambda e: e.tensor_tensor(out=self.RA[:], in0=X, in1=cosb, op=ALU.mult),
                          reads=[self.rope_b], writes=[rt])
                    sc.op("dve", lambda e: e.tensor_tensor(out=self.RB1[:], in0=X[:, :, :, 8:16], in1=sinb, op=ALU.mult),
                          reads=[self.rope_b], writes=[rt])
                    sc.op("dve", lambda e: e.tensor_tensor(out=self.RB2[:], in0=X[:, :, :, 0:8], in1=sinb, op=ALU.mult),
                          reads=[self.rope_b], writes=[rt])
                    sc.op("dve", lambda e: e.tensor_tensor(out=Dv[:, :, :, 0:8], in0=self.RA[:, :, :, 0:8], in1=self.RB1[:],
                                                           op=ALU.subtract), reads=[rt], writes=[qk])
                    sc.op("dve", lambda e: e.tensor_tensor(out=Dv[:, :, :, 8:16], in0=self.RA[:, :, :, 8:16], in1=self.RB2[:],
                                                           op=ALU.add), reads=[rt], writes=[qk])
                pieces.append(piece)
        return pieces

    def evac_B(self, blk, ps, bb, h, vi):
        sc = self.sc
        qk = self.QKtm_b
        VA, VAb = self.VAs[vi], self.VAs_b[vi]
        sc.op("dve", lambda e: e.tensor_copy(VA[:, blk, 0, 0:64], ps[:, 128:192]), writes=[bb, VAb])
        sc.op("act", lambda e: e.activation(self.QKtm[:, blk, 0, 0:64], ps[:, 0:64], AF.Copy, scale=0.125), writes=[bb, qk])
        sc.op("dve", lambda e: e.tensor_copy(self.QKtm[:, blk, 1, 0:64], ps[:, 64:128]), writes=[bb, qk])
        if blk == self.NB - 1:
            Fb = self.F_b
            sc.op("dve", lambda e: e.memset(self.QKtm[:, :, 0, 67:70], 1.0), writes=[qk])
            sc.op("dve", lambda e: e.memset(self.QKtm[:, :, 1, 64:67], 1.0), writes=[qk])
            for i, src_t in enumerate((self.FH, self.FM, self.FLo)):
                sc.op("dve", lambda e, i=i, src_t=src_t: e.tensor_scalar(out=self.QKtm[:, :, 0, 64 + i], in0=src_t[:, :, h], scalar1=-1.0,
                                                                       scalar2=None, op0=ALU.mult), reads=[Fb], writes=[qk])
                sc.op("dve", lambda e, i=i, src_t=src_t: e.tensor_copy(self.QKtm[:, :, 1, 67 + i], src_t[:, :, h]), reads=[Fb], writes=[qk])

    def unit_transposes(self, isA, u):
        sc = self.sc
        NB = self.NB
        ncol = 128 if isA else 70
        for g4 in range(NB // 4):
            qb, kb = (6, 7) if g4 % 2 == 0 else (2, 3)
            tq = self.tp_view(qb)
            tk = self.tp_view(kb)
            for qi, (tp, tb) in enumerate(((tq, qb), (tk, kb))):
                for bl in range(4):
                    blk = g4 * 4 + bl
                    sc.op("pe", lambda e, qi=qi, bl=bl, blk=blk, tp=tp: e.transpose(
                        tp[0:ncol, bl * 128:(bl + 1) * 128], self.QKtm[:, blk, qi, 0:ncol], self.identb[:]),
                        reads=[self.QKtm_b, self.const_b], writes=[self.bank_b[tb]], signal=(bl == 3))
            sc.op("act", lambda e, tq=tq, g4=g4: e.copy(self.QT[0:ncol, g4 * 512:(g4 + 1) * 512], tq[0:ncol, 0:512]),
                  writes=[self.bank_b[qb], self.QT_b])
            if isA:
                sc.op("dve", lambda e, tk=tk, g4=g4: e.tensor_copy(self.KT[0:64, g4 * 512:(g4 + 1) * 512], tk[0:64, 0:512]),
                      writes=[self.bank_b[kb], self.KT_b])
                sc.op("dve", lambda e, tk=tk, g4=g4: e.tensor_copy(self.KT1[64:128, g4 * 512:(g4 + 1) * 512], tk[64:128, 0:512]),
                      writes=[self.bank_b[kb], self.KT1_b])
            else:
                sc.op("dve", lambda e, tk=tk, g4=g4: e.tensor_copy(self.KT[0:ncol, g4 * 512:(g4 + 1) * 512], tk[0:ncol, 0:512]),
                      writes=[self.bank_b[kb], self.KT_b])

    def attention(self, isA, hp0, K, vsel, col0, vi):
        KTt, KTb = (self.KT1, self.KT1_b) if (isA and hp0 == 64) else (self.KT, self.KT_b)
        VA, VAb = self.VAs[vi], self.VAs_b[vi]
        SB = (2, 3, 6, 7)
        sc = self.sc
        NG = self.NG
        LAG = 5
        NPT = self.NPT
        tiles = [(g, j) for g in range(NG) for j in range(4 * g + 4)]
        n = len(tiles)

        def emit_qk(t):
            g, j = tiles[t]
            c0 = max(j - 4 * g, 0)
            sbk = SB[t % 4]
            sps = self.bank[sbk]
            PT = self.PT[t % NPT]
            ptb = self.PT_b[t % NPT]
            diag = (not isA) and (j >= 4 * g)
            sc.op("pe", lambda e: e.matmul(
                sps[:, c0 * 128:512], KTt[:, j * 128:(j + 1) * 128],
                self.QT[:, g * 512 + c0 * 128:(g + 1) * 512], start=True, stop=(not diag)),
                reads=[self.QT_b, KTb], writes=[self.bank_b[sbk]], signal=(not diag))
            if diag:
                sc.op("pe", lambda e: e.matmul(
                    sps[:, c0 * 128:(c0 + 1) * 128], self.identb[:], self.negm[:], start=False, stop=True),
                    reads=[self.const_b], writes=[self.bank_b[sbk]], signal=True)
            sc.op("act", lambda e: e.activation(PT[:, c0 * 128:512], sps[:, c0 * 128:512], AF.Exp),
                  writes=[self.bank_b[sbk], ptb])
            if isA:
                d0 = 4 * g + c0 - j
                nblk = 4 - c0
                msk = self.multm[:, d0:d0 + nblk, :].rearrange("p a b -> p (a b)")
                sc.op("dve", lambda e: e.tensor_tensor(
                    out=PT[:, c0 * 128:512], in0=PT[:, c0 * 128:512], in1=msk, op=ALU.mult),
                    reads=[self.const_b], writes=[ptb])

        def emit_pv(t):
            g, j = tiles[t]
            c0 = max(j - 4 * g, 0)
            PT = self.PT[t % NPT]
            ptb = self.PT_b[t % NPT]
            ob = 4 + (g % 2)
            opsv = self.bank[ob][:, 0:260].rearrange("p (a b) -> p a b", a=4)
            for c in range(c0, 4):
                sc.op("pe", lambda e, c=c: e.matmul(
                    opsv[:, c, :], PT[:, c * 128:(c + 1) * 128], VA[:, j, vsel, 0:65],
                    start=(j == 0 and c == 0), stop=(j == 4 * g + c), skip_group_check=True),
                    reads=[ptb, VAb], writes=[self.bank_b[ob]], signal=(c == 3))
            if j == 4 * g + 3:
                def norm(opsv=opsv, ob=ob, g=g):
                    sc.op("dve", lambda e: e.reciprocal(out=self.RDEN[:], in_=opsv[:, :, 64]),
                          writes=[self.bank_b[ob], self.rden_b])
                    sc.op("dve", lambda e: e.tensor_tensor(
                        out=self.merged[:, 4 * g:4 * g + 4, col0:col0 + 64], in0=opsv[:, :, 0:64],
                        in1=self.RDEN[:].unsqueeze(2).to_broadcast([128, 4, 64]), op=ALU.mult),
                        reads=[self.rden_b], writes=[self.bank_b[ob], self.merged_b])
                self.defer(4, norm)

        for t in range(n + LAG):
            if t < n:
                emit_qk(t)
            if t - LAG >= 0:
                emit_pv(t - LAG)
            self.tick_deferred()
            yield t

    def outproj(self, b):
        sc = self.sc
        NB, NG = self.NB, self.NG
        s = 1
        mergedT = self.hT_m
        mtb = self.hTm_b
        ssb = self.ss_b
        sc.op("dve", lambda e: e.memset(self.SSAB[:], 0.0), writes=[ssb])
        for blk in range(NB):
            for grp in range(2):
                sc.op("act", lambda e, blk=blk, grp=grp: e.activation(
                    self.SQJ[:], self.merged[:, blk, grp * 512:(grp + 1) * 512], AF.Square,
                    accum_out=self.SSAB[:, blk, grp:grp + 1]),
                    reads=[self.merged_b], writes=[ssb])
        sc.op("act", lambda e: e.activation(self.RSAB[:], self.SSAB[:], AF.Ln, bias=EPS, scale=1.0 / 512), writes=[ssb])
        sc.op("act", lambda e: e.activation(self.RSAB[:], self.RSAB[:], AF.Exp, scale=-0.5), writes=[ssb])
        gout = self.gains[:, 6, :]
        wov = self.w_out_d.rearrange("(k p) f -> p k f", p=128)
        slots = []
        for hf in range(2):
            si = self.slab([(lambda r: r[:].rearrange("p (k f) -> p k f", k=KC), wov[:, :, hf * 512:(hf + 1) * 512])], keep=hf)
            slot = self.ring[si][:].rearrange("p (k f) -> p k f", k=KC)
            slots.append((slot, self.ring_b[si]))

        def norm_transpose(blk):
            mn = self.MN[blk % 2]
            mnb = self.MN_b[blk % 2]
            sc.op("dve", lambda e: e.tensor_tensor(
                out=mn[:].rearrange("p (a b) -> p a b", a=2), in0=self.merged[:, blk, :].rearrange("p (a b) -> p a b", a=2),
                in1=self.RSAB[:, blk, :].unsqueeze(2).to_broadcast([128, 2, 512]), op=ALU.mult),
                reads=[self.merged_b, ssb], writes=[mnb])
            tb = 6 + (blk % 2)
            tp = self.tp_view(tb)
            for c in range(KC):
                sc.op("pe", lambda e, c=c: e.transpose(tp[:, c * 128:(c + 1) * 128], mn[:, c * 128:(c + 1) * 128], self.identb[:]),
                      reads=[mnb, self.const_b], writes=[self.bank_b[tb]], signal=(c == KC - 1))
            sc.op("dve", lambda e: e.tensor_tensor(
                out=mergedT[:, :, blk * 128:(blk + 1) * 128], in0=tp.rearrange("p (a b) -> p a b", a=KC),
                in1=gout.unsqueeze(2).to_broadcast([128, KC, 128]), op=ALU.mult),
                reads=[self.small_b], writes=[self.bank_b[tb], mtb])

        yi = [0]

        def mm_chunk(sg, c):
            slot, rb = slots[c // 4]
            yb = yi[0] % 4
            yi[0] += 1
            yps = self.bank[yb]
            pnb = 4 + (sg % 2)
            for k in range(KC):
                sc.op("pe", lambda e, k=k: e.matmul(
                    yps[:], slot[:, k, (c % 4) * 128:(c % 4 + 1) * 128], mergedT[:, k, sg * 512:(sg + 1) * 512],
                    start=(k == 0), stop=(k == KC - 1)),
                    reads=[rb, mtb], writes=[self.bank_b[yb]], signal=(k == KC - 1))
            sq = self.SQ2m[:, c % 4, :]

            def evac():
                sc.op("act", lambda e: e.copy(self.ystage_m[:, c, :], yps[:]), writes=[self.bank_b[yb], self.ysm_b])
                sc.op("act", lambda e: e.activation(sq, yps[:], AF.Square), writes=[self.bank_b[yb], self.SQ2m_b])

            def stat():
                sc.op("pe", lambda e: e.matmul(self.bank[pnb][:], self.onesb[:], sq, start=(c == 0), stop=(c == KC - 1)),
                      reads=[self.SQ2m_b, self.const_b], writes=[self.bank_b[pnb]], signal=True)
            return evac, stat

        def pn_sa(sg):
            self.postnorm_update(b, s, sg, self.ystage_m, [self.ysm_b], self.LN2m, self.RSTD2m, self.LN2m_b,
                                 self.UTMPm, self.UTMPm_b, ssbank=4 + (sg % 2))

        for bl in range(4):
            norm_transpose(bl)
        for sg in range(NG):
            if sg + 1 < NG:
                for bl in range(4):
                    norm_transpose((sg + 1) * 4 + bl)
            pend = [mm_chunk(sg, c) for c in range(4)]
            if sg > 0:
                pn_sa(sg - 1)
            prev_stat = None
            for c in range(KC):
                if c < 4:
                    evac, stat = pend[c]
                else:
                    evac, stat = mm_chunk(sg, c)
                evac()
                if prev_stat is not None:
                    prev_stat()
                prev_stat = stat
            prev_stat()
            if sg > 0:
                self.stats_ahead(sg - 1, self.SQ2m, self.SQ2m_b, self.LN2m, self.LN2m_b, ssbank=4 + ((sg - 1) % 2))
        pn_sa(NG - 1)
        self.stats_ahead(NG - 1, self.SQ2m, self.SQ2m_b, self.LN2m, self.LN2m_b, ssbank=4 + ((NG - 1) % 2))


def _consts():
    identf = np.eye(128, dtype=np.float32)
    s_idx = np.arange(128)[:, None]
    t_idx = np.arange(128)[None, :]
    trif = (s_idx <= t_idx).astype(np.float32)
    kk = np.arange(128)[:, None, None]
    DD = np.arange(16)[None, :, None]
    tq = np.arange(128)[None, None, :]
    delta = DD * 128 + tq - kk
    m = ((delta >= 0) & (delta <= 128)).astype(np.float32)
    m += ((delta >= 0) & (delta <= 512) & (delta % 4 == 0)).astype(np.float32)
    m += ((delta >= 0) & (delta <= 2048) & (delta % 16 == 0)).astype(np.float32)
    mult = m.reshape(128, 16 * 128).astype(np.float32)
    neg = np.where(t_idx >= s_idx, 0.0, -30000.0).astype(np.float32)
    inv_freq = (THETA ** (-np.arange(0, ROT, 2, dtype=np.float32) / ROT)).astype(np.float32)
    invfreq = np.ascontiguousarray(np.broadcast_to(inv_freq[None, :], (128, 8))).astype(np.float32)
    return identf, trif, mult, neg, invfreq


def make_in_maps(inputs, S, NSEQ, ncores):
    f = lambda a: np.ascontiguousarray(np.asarray(a))
    x = f(inputs["x"]).astype(np.float32, copy=False)
    c = f(inputs["c"]).astype(np.float32, copy=False)
    pos = f(inputs["positions"]).astype(np.int32, copy=False)
    NB = S // 128
    identf, trif, mult, neg, invfreq = _consts()
    fm = lambda g: np.ascontiguousarray(np.asarray(g, dtype=np.float32).reshape(-1, 128).T)
    gains = np.stack([fm(inputs["g_pre_ff1"][0]), fm(inputs["g_post_ff1"][0]), fm(inputs["g_pre_mix"][0]),
                      fm(inputs["g_post_mix"][0]), fm(inputs["g_pre_ff2"][0]), fm(inputs["g_post_ff2"][0]),
                      fm(np.concatenate([np.asarray(inputs["g_out_a"][0]), np.asarray(inputs["g_out_b"][0])]))], axis=1)
    gains = np.ascontiguousarray(gains.astype(np.float32))
    shared = {
        "w_ada": f(inputs["w_ada"][0]), "b_adaT": fm(inputs["b_ada"][0]), "gains": gains,
        "bforget": np.ascontiguousarray(np.broadcast_to(np.asarray(inputs["b_forget"][0], dtype=np.float32)[None, :], (128, NHB))),
        "invfreq": invfreq,
        "w_ff1_gate": f(inputs["w_ff1_gate"][0]), "w_ff1_up": f(inputs["w_ff1_up"][0]), "w_ff1_down": f(inputs["w_ff1_down"][0]),
        "w_ff2_gate": f(inputs["w_ff2_gate"][0]), "w_ff2_up": f(inputs["w_ff2_up"][0]), "w_ff2_down": f(inputs["w_ff2_down"][0]),
        "w_in": f(inputs["w_in"][0]), "w_out": f(inputs["w_out"][0]),
        "c_identf": identf, "c_trif": trif, "c_mult": mult, "c_neg": neg,
    }
    maps = []
    for ci in range(ncores):
        sl = slice(ci * NSEQ, (ci + 1) * NSEQ)
        m = dict(shared)
        m["x"] = np.ascontiguousarray(x[sl])
        m["cT"] = np.ascontiguousarray(c[sl].reshape(NSEQ, KC, 128).transpose(2, 1, 0))
        m["pos"] = np.ascontiguousarray(pos[sl].reshape(NSEQ, NB, 128).transpose(2, 0, 1))
        maps.append(m)
    return maps


_NC_CACHE = {}


def run(inputs, S, NSEQ, ncores=NCORES, trace=False):
    key = (S, NSEQ)
    if key not in _NC_CACHE:
        _NC_CACHE[key] = Builder(S, NSEQ).build()
    nc = _NC_CACHE[key]
    maps = make_in_maps(inputs, S, NSEQ, ncores)
    res = run_bass_kernel_spmd(nc, maps, core_ids=list(range(ncores)), **({"trace": True} if trace else {}))
    out = np.concatenate([r["out"] for r in res.results], axis=0)
    return out, res


def kernel(**inputs):
    x = np.asarray(inputs["x"])
    Btot, S, _ = x.shape
    NSEQ = Btot // NCORES
    out, _ = run(inputs, S, NSEQ)
    return out.astype(np.float32, copy=False)
```

```python
import contextlib
import math

import numpy as np
import ml_dtypes

import concourse.bass as bass
import concourse.mybir as mybir
from concourse.bass_utils import run_bass_kernel_spmd

F32 = mybir.dt.float32
BF16 = mybir.dt.bfloat16
I32 = mybir.dt.int32
AF = mybir.ActivationFunctionType
ALU = mybir.AluOpType

D = 1024
KC = 8
DFF = 2752
FCH = 22
HD = 64
NHA = 8
NHB = 8
INCOLS = 3080
EPS = 1e-6
NCORES = 8
ROT = 16
THETA = 500000.0

MAGIC = 12582912.0
TWO_PI_HI = 6.28125
TWO_PI_LO = float(np.float32(2 * math.pi - 6.28125))
PI_SAFE = 3.1415925


class Buf:
    __slots__ = ("name", "w", "r", "dsem", "dcount")

    def __init__(self, name):
        self.name = name
        self.w = {}
        self.r = {}
        self.dsem = None
        self.dcount = 0


class Eng:
    def __init__(self, name, h, sem):
        self.name = name
        self.h = h
        self.sem = sem
        self.ticks = 0
        self.seen = {}
        self.stream = []


class Sched:
    def __init__(self, nc, es):
        self.nc = nc
        self.es = es
        self.sems = {}
        self.eng = {}
        for name, h in (("pe", nc.tensor), ("act", nc.scalar), ("dve", nc.vector),
                        ("pool", nc.gpsimd), ("sp", nc.sync)):
            sem = es.enter_context(nc.semaphore("e_" + name))
            self.eng[name] = Eng(name, h, sem)
            self.sems[("eng", name)] = sem
        self.nbuf = 0
        self.dry = False

    def buf(self, name):
        self.nbuf += 1
        return Buf(f"{name}_{self.nbuf}")

    def _deps(self, en, reads, writes):
        raw = {}
        war = {}
        for b in reads:
            for k, v in b.w.items():
                if raw.get(k, 0) < v:
                    raw[k] = v
        for b in writes:
            for k, v in b.w.items():
                if raw.get(k, 0) < v:
                    raw[k] = v
            for k, v in b.r.items():
                if war.get(k, 0) < v:
                    war[k] = v
        own = ("eng", en)
        deps = dict(raw)
        if en == "pe":
            deps.pop(own, None)
        for k, v in war.items():
            if k == own:
                continue
            if deps.get(k, 0) < v:
                deps[k] = v
        return deps

    def _emit_waits(self, E, deps):
        for k, v in deps.items():
            if E.seen.get(k, 0) >= v:
                continue
            E.h.wait_ge(self.sems[k], v)
            E.seen[k] = v
            E.stream.append(("wait", k, v))

    def op(self, en, fn, reads=(), writes=(), signal=True):
        if self.dry:
            return None
        E = self.eng[en]
        deps = self._deps(en, reads, writes)
        self._emit_waits(E, deps)
        inst = fn(E.h)
        key = ("eng", en)
        if signal:
            E.ticks += 1
            inst.then_inc(E.sem, 1)
            tick = E.ticks
            E.stream.append(("op", key, 1))
        else:
            tick = E.ticks + 1
            E.stream.append(("op", None, 0))
        for b in writes:
            if b.w.get(key, 0) < tick:
                b.w[key] = tick
        for b in reads:
            if b.r.get(key, 0) < tick:
                b.r[key] = tick
        return inst

    def dma(self, qn, out, in_, reads=(), writes=()):
        if self.dry:
            return None
        Q = self.eng[qn]
        deps = self._deps(qn, reads, writes)
        self._emit_waits(Q, deps)
        prim = writes[0] if writes else reads[0]
        if prim.dsem is None:
            prim.dsem = self.es.enter_context(self.nc.semaphore("d_" + prim.name))
            self.sems[("dma", prim.name)] = prim.dsem
        prim.dcount += 16
        key = ("dma", prim.name)
        Q.h.dma_start(out=out, in_=in_).then_inc(prim.dsem, 16)
        Q.stream.append(("op", key, 16))
        for b in writes:
            b.w[key] = prim.dcount
        for b in reads:
            b.r[key] = prim.dcount

    def final_wait(self, en, bufs):
        E = self.eng[en]
        deps = {}
        for b in bufs:
            for k, v in list(b.w.items()) + list(b.r.items()):
                if deps.get(k, 0) < v:
                    deps[k] = v
        deps.pop(("eng", en), None)
        self._emit_waits(E, deps)

    def check_deadlock(self):
        cnt = {}
        ptr = {n: 0 for n in self.eng}
        pending_fwd = {n: False for n in self.eng}
        progress = True
        while progress:
            progress = False
            for n, E in self.eng.items():
                while ptr[n] < len(E.stream):
                    kind, k, v = E.stream[ptr[n]]
                    if kind == "wait":
                        if cnt.get(k, 0) >= v:
                            ptr[n] += 1
                            progress = True
                        else:
                            break
                    else:
                        if k is not None:
                            cnt[k] = cnt.get(k, 0) + v
                        ptr[n] += 1
                        progress = True
        stuck = {n: (ptr[n], len(E.stream), E.stream[ptr[n]] if ptr[n] < len(E.stream) else None)
                 for n, E in self.eng.items() if ptr[n] < len(E.stream)}
        if stuck:
            raise RuntimeError(f"semaphore deadlock in generated program: {stuck} cnt={ {k: cnt.get(k) for _, (_, _, s) in stuck.items() for k in [s[1]]} }")


class Builder:
    def __init__(self, S, NSEQ):
        assert S % 512 == 0
        self.S = S
        self.NSEQ = NSEQ
        self.NB = S // 128
        self.NG = S // 512
        self.T = min(1024, S)
        self.NTG = S // self.T
        self.SGT = self.T // 512

    def sb(self, name, shape, dt):
        return self.es.enter_context(self.nc.sbuf_tensor(name, shape, dt))

    def build(self):
        nc = bass.Bass("TRN2", target_bir_lowering=False)
        self.nc = nc
        S, NSEQ, NB, NG = self.S, self.NSEQ, self.NB, self.NG
        dr = lambda name, shape, dt, kind="ExternalInput": nc.dram_tensor(name, shape, dt, kind=kind).ap()
        self.x_d = dr("x", [NSEQ, S, D], F32)
        self.cT_d = dr("cT", [128, KC, NSEQ], F32)
        self.pos_d = dr("pos", [128, NSEQ, NB], I32)
        self.w_ada_d = dr("w_ada", [D, 9 * D], F32)
        self.b_adaT_d = dr("b_adaT", [128, 72], F32)
        self.gains_d = dr("gains", [128, 7, KC], F32)
        self.bforget_d = dr("bforget", [128, NHB], F32)
        self.invfreq_d = dr("invfreq", [128, 8], F32)
        self.wg_d = [dr("w_ff1_gate", [D, DFF], F32), dr("w_ff2_gate", [D, DFF], F32)]
        self.wu_d = [dr("w_ff1_up", [D, DFF], F32), dr("w_ff2_up", [D, DFF], F32)]
        self.wd_d = [dr("w_ff1_down", [DFF, D], F32), dr("w_ff2_down", [DFF, D], F32)]
        self.w_in_d = dr("w_in", [D, INCOLS], F32)
        self.w_out_d = dr("w_out", [D, D], F32)
        self.c_identf_d = dr("c_identf", [128, 128], F32)
        self.c_trif_d = dr("c_trif", [128, 128], F32)
        self.c_mult_d = dr("c_mult", [128, 16 * 128], F32)
        self.c_neg_d = dr("c_neg", [128, 128], F32)
        self.out_d = dr("out", [NSEQ, S, D], F32, kind="ExternalOutput")

        with contextlib.ExitStack() as es:
            self.es = es
            self.sc = Sched(nc, es)
            self.alloc()
            self.slab_plan = []
            self.deferred = []
            for dry in (True, False):
                self.sc.dry = dry
                self.slab_count = 0
                self.slab_emitted = 0
                self.setup()
                for b in range(NSEQ):
                    self.sequence(b)
            self.sc.final_wait("sp", self.stage_b)
            self.sc.check_deadlock()
        return nc

    def alloc(self):
        nc, sc = self.nc, self.sc
        S, NSEQ, NB, T = self.S, self.NSEQ, self.NB, self.T
        sb = self.sb
        B = sc.buf
        self.xT = sb("xT", [128, KC, S], F32)
        self.xT_b = [B("xT") for _ in range(self.NG)]
        self.identb = sb("identb", [128, 128], BF16)
        self.identf = sb("identf", [128, 128], F32)
        self.onesb = sb("onesb", [128, 128], BF16)
        self.onesf = sb("onesf", [128, 128], F32)
        self.trif = sb("trif", [128, 128], F32)
        self.multm = sb("multm", [128, 16, 128], BF16)
        self.negm = sb("negm", [128, 128], BF16)
        self.const_b = B("const")
        self.cT = sb("cT_sb", [128, KC, NSEQ], F32)
        self.scT = sb("scT", [128, KC, NSEQ], BF16)
        self.sctmp = sb("sctmp", [128, KC, NSEQ], F32)
        self.modT = sb("modT", [128, 72, NSEQ], F32)
        self.b_adaT = sb("b_adaT_sb", [128, 72], F32)
        self.gains = sb("gains_sb", [128, 7, KC], F32)
        self.mods = sb("mods", [128, 9, KC, NSEQ], F32)
        self.bforget = sb("bforget_sb", [128, NHB], F32)
        self.invfreq = sb("invfreq_sb", [128, 8], F32)
        self.pos_i = sb("pos_i", [128, NSEQ, NB], I32)
        self.small_b = B("small")
        self.mod_b = B("mod")
        self.cq2 = sb("cq2", [128, NB, 16], F32)
        self.sq1 = sb("sq1", [128, NB, 8], F32)
        self.ck2 = sb("ck2", [128, NB, 16], F32)
        self.sk1 = sb("sk1", [128, NB, 8], F32)
        self.rope_b = B("rope")
        self.ropetmp_b = B("ropetmp")
        self.NSLOT = 3
        self.ring = [sb(f"ring{i}", [128, 4096], BF16) for i in range(self.NSLOT)]
        self.ring_b = [B(f"ring{i}") for i in range(self.NSLOT)]
        self.ring_i = 0
        self.wf = sb("wf", [128, KC, 8], BF16)
        self.wf_b = B("wf")
        REG = (int(self.nc.sbuf_bytes_remaining) - 256) // 256 * 256
        self.region = sb("region", [128, REG // 4], F32)
        self.REG = REG

        def carve(off, shape, dt):
            esz = 2 if dt == BF16 else 4
            n = int(np.prod(shape[1:]))
            assert off % 4 == 0 and off + n * esz <= REG, (off, shape, REG)
            ap = self.region[:, off // 4: off // 4 + (n * esz) // 4]
            if dt != F32:
                ap = ap.bitcast(dt)
            if len(shape) == 3:
                ap = ap.rearrange("p (a b) -> p a b", a=shape[1])
            elif len(shape) == 4:
                ap = ap.rearrange("p (a b c) -> p a b c", a=shape[1], b=shape[2])
            elif len(shape) == 5:
                ap = ap.rearrange("p (a b c d) -> p a b c d", a=shape[1], b=shape[2], c=shape[3])
            return ap, off + n * esz

        self.carve = carve
        SGT = self.SGT
        o = 0
        self.aT, o = carve(o, [128, FCH, T], BF16)
        self.aT_b = [B("aT") for _ in range(SGT)]
        r1 = o
        self.hT_f, o = carve(o, [128, KC, T], BF16)
        self.hTf_b = [B("hTf") for _ in range(SGT)]
        self.SQ, o = carve(o, [128, 4, 512], BF16)
        self.SQ_b = B("SQ")
        self.E = []
        for i in range(2):
            e, o = carve(o, [128, 512], F32)
            self.E.append(e)
        self.E_b = [B("E0"), B("E1")]
        self.LN, o = carve(o, [128, 512], F32)
        self.RSTD, o = carve(o, [128, 512], F32)
        self.LN_b = B("LN")
        self.HTMP = []
        for i in range(2):
            e, o = carve(o, [128, 512], F32)
            self.HTMP.append(e)
        self.HTMP_b = [B("HT0"), B("HT1")]
        self.ystage_f, o2 = carve(r1, [128, KC, T], F32)
        o = max(o, o2)
        self.R1_bufs = self.hTf_b + [self.SQ_b, self.LN_b] + self.E_b + self.HTMP_b
        self.SQ2, o = carve(o, [128, 4, 512], BF16)
        self.SQ2_b = B("SQ2")
        self.LN2, o = carve(o, [128, 512], F32)
        self.RSTD2, o = carve(o, [128, 512], F32)
        self.LN2_b = B("LN2")
        self.UTMP = []
        for i in range(2):
            e, o = carve(o, [128, 512], F32)
            self.UTMP.append(e)
        self.UTMP_b = [B("UT0"), B("UT1")]
        self.ffn_end = o
        o = 0
        self.merged, o = carve(o, [128, NB, D], BF16)
        self.merged_b = B("merged")
        self.hT_m, o = carve(o, [128, KC, S], BF16)
        self.hTm_b = B("hTm")
        m_un = o
        self.QT, o = carve(o, [128, S], BF16)
        self.KT, o = carve(o, [128, S], BF16)
        self.KT1, o = carve(o, [128, S], BF16)
        self.VAs = []
        for i in range(2):
            e, o = carve(o, [128, NB, 2, 66], BF16)
            self.VAs.append(e)
        self.VA = self.VAs[0]
        self.QKtm, o = carve(o, [128, NB, 2, 128], BF16)
        self.NPT = 7
        self.PT = []
        for i in range(self.NPT):
            e, o = carve(o, [128, 512], BF16)
            self.PT.append(e)
        self.PT_b = [B("PT") for _ in range(self.NPT)]
        rotf_off = o
        self.ROTF, o = carve(o, [128, NB, 2, 2, 16], F32)
        self.RH = max(NB // 2, 1)
        self.RA, o = carve(o, [128, self.RH, 2, 16], F32)
        self.RB1, o = carve(o, [128, self.RH, 2, 8], F32)
        self.RB2, o = carve(o, [128, self.RH, 2, 8], F32)
        self.RDEN, o = carve(o, [128, 4], F32)
        m1_end = o
        self.QT_b, self.KT_b, self.QKtm_b = B("QT"), B("KT"), B("QKtm")
        self.VAs_b = [B("VA0"), B("VA1")]
        self.KT1_b = B("KT1")
        self.rtmp_b = B("rtmp")
        self.rden_b = B("rden")
        o = m_un
        self.SQm, o = carve(o, [128, 4, 512], BF16)
        self.LNm, o = carve(o, [128, 512], F32)
        self.RSTDm, o = carve(o, [128, 512], F32)
        self.HTMPm = []
        for i in range(2):
            e, o = carve(o, [128, 512], F32)
            self.HTMPm.append(e)
        self.SQm_b, self.LNm_b, self.HTMPm_b = B("SQm"), B("LNm"), [B("HTm0"), B("HTm1")]
        m0_end = o
        o = m_un
        self.ystage_m, o = carve(o, [128, KC, 512], F32)
        self.ysm_b = B("ysm")
        self.MN = []
        for i in range(2):
            e, o = carve(o, [128, D], BF16)
            self.MN.append(e)
        self.MN_b = [B("MN0"), B("MN1")]
        self.SSAB, o = carve(o, [128, NB, 2], F32)
        self.RSAB, o = carve(o, [128, NB, 2], F32)
        self.SQJ, o = carve(o, [128, 512], BF16)
        self.ss_b = B("ssab")
        self.SQ2m, o = carve(o, [128, 4, 512], BF16)
        self.SQ2m_b = B("SQ2m")
        self.LN2m, o = carve(o, [128, 512], F32)
        self.RSTD2m, o = carve(o, [128, 512], F32)
        self.LN2m_b = B("LN2m")
        self.UTMPm = []
        for i in range(2):
            e, o = carve(o, [128, 512], F32)
            self.UTMPm.append(e)
        self.UTMPm_b = [B("UTm0"), B("UTm1")]
        m2_end = o
        alias_f = rotf_off >= m0_end
        o = rotf_off if alias_f else max(m1_end, m0_end)
        self.FL, o = carve(o, [128, NB, 8], F32)
        self.FL2, o = carve(o, [128, NB, 8], F32)
        self.NEGF, o = carve(o, [128, NB, 8], F32)
        self.CARRY, o = carve(o, [128, NB, 8], F32)
        self.F32T, o = carve(o, [128, NB, 8], F32)
        if alias_f:
            assert o <= rotf_off + NB * 2 * 2 * 16 * 4
            o = max(m1_end, m0_end)
        self.FH, o = carve(o, [128, NB, 8], BF16)
        self.FM, o = carve(o, [128, NB, 8], BF16)
        self.FLo, o = carve(o, [128, NB, 8], BF16)
        self.F_b = B("F")
        self.aug_b = B("aug")
        self.mix_end = max(o, m2_end)
        o = max(m2_end, self.ffn_end)
        self.RSTDN, o = carve(o, [128, self.NG, 512], F32)
        self.rstdn_b = [B("rstdn") for _ in range(self.NG)]
        self.layout_info = dict(REG=REG, ffn_end=self.ffn_end, m1_end=m1_end, m0_end=m0_end, m2_end=m2_end, mix_end=self.mix_end)
        self.NSTAGE = 4
        self.stage = []
        o = 0
        for i in range(self.NSTAGE):
            e, o = carve(o, [128, D], F32)
            self.stage.append(e)
        self.stage_b = [B(f"stage{i}") for i in range(self.NSTAGE)]
        self.rt = []
        for i in range(6):
            e, o = carve(o, [128, NB, 8], F32)
            self.rt.append(e)
        self.region_b = B("region")
        self.bank = [self.es.enter_context(nc.psum_tensor(f"bank{i}", [128, 512], F32)) for i in range(8)]
        self.bank_b = [B(f"bank{i}") for i in range(8)]

    def defer(self, delay, fn):
        self.deferred.append([delay, fn])

    def tick_deferred(self, flush=False):
        keep = []
        for item in self.deferred:
            item[0] -= 1
            if flush or item[0] <= 0:
                item[1]()
            else:
                keep.append(item)
        self.deferred = keep

    def tp_view(self, i):
        return self.bank[i][:].bitcast(BF16)

    def phase_sync(self):
        sc = self.sc
        if sc.dry:
            return
        snap = {("eng", n): E.ticks for n, E in sc.eng.items() if E.ticks > 0}
        for k, v in self.region_b.w.items():
            if k[0] == "dma":
                snap[k] = max(snap.get(k, 0), v)
        for n, E in sc.eng.items():
            deps = dict(snap)
            deps.pop(("eng", n), None)
            sc._emit_waits(E, deps)

    def slab(self, parts, keep=0):
        idx = self.slab_count
        self.slab_count += 1
        if self.sc.dry:
            self.slab_plan.append(parts)
            return idx % self.NSLOT
        plan = self.slab_plan
        while self.slab_emitted <= min(idx + self.NSLOT - 1 - keep, len(plan) - 1):
            i = self.slab_emitted
            si = i % self.NSLOT
            for dst_fn, srcap in plan[i]:
                self.sc.dma("pool", dst_fn(self.ring[si]), srcap, writes=[self.ring_b[si]])
            self.slab_emitted += 1
        return idx % self.NSLOT

    def setup(self):
        nc, sc = self.nc, self.sc
        NSEQ = self.NSEQ
        small = self.small_b
        for dst, src in ((self.cT, self.cT_d), (self.b_adaT, self.b_adaT_d), (self.gains, self.gains_d),
                         (self.bforget, self.bforget_d), (self.invfreq, self.invfreq_d),
                         (self.pos_i, self.pos_d), (self.identf, self.c_identf_d), (self.trif, self.c_trif_d)):
            sc.dma("sp", dst[:], src, writes=[small])
        cb = self.const_b
        sc.dma("pool", self.identb[:], self.c_identf_d, writes=[cb])
        sc.dma("pool", self.multm[:].rearrange("p a b -> p (a b)"), self.c_mult_d, writes=[cb])
        sc.dma("pool", self.negm[:], self.c_neg_d, writes=[cb])
        sc.op("dve", lambda e: e.memset(self.onesb[:], 1.0), writes=[cb])
        sc.op("dve", lambda e: e.memset(self.onesf[:], 1.0), writes=[cb])
        mb = self.mod_b
        sc.op("act", lambda e: e.activation(self.sctmp[:], self.cT[:], AF.Exp, scale=-1.0), reads=[small], writes=[mb])
        sc.op("dve", lambda e: e.tensor_scalar(out=self.sctmp[:], in0=self.sctmp[:], scalar1=1.0, scalar2=None, op0=ALU.add), writes=[mb])
        sc.op("dve", lambda e: e.reciprocal(out=self.sctmp[:], in_=self.sctmp[:]), writes=[mb])
        sc.op("dve", lambda e: e.tensor_tensor(out=self.scT[:], in0=self.cT[:], in1=self.sctmp[:], op=ALU.mult), reads=[small], writes=[mb])
        wv = self.w_ada_d.rearrange("(k p) f -> p k f", p=128)
        for sl in range(18):
            si = self.slab([(lambda r: r[:].rearrange("p (k f) -> p k f", k=KC), wv[:, :, sl * 512:(sl + 1) * 512])])
            slot = self.ring[si][:].rearrange("p (k f) -> p k f", k=KC)
            bk = sl % 2
            ps = self.bank[bk]
            for sub in range(4):
                for k in range(KC):
                    sc.op("pe", lambda e, k=k, sub=sub: e.matmul(ps[:, sub * NSEQ:(sub + 1) * NSEQ],
                                                                   slot[:, k, sub * 128:(sub + 1) * 128],
                                                                   self.scT[:, k, :], start=(k == 0), stop=(k == KC - 1)),
                          reads=[self.ring_b[si], mb], writes=[self.bank_b[bk]], signal=(k == KC - 1))
            sc.op("dve", lambda e, sl=sl, ps=ps: e.tensor_tensor(
                out=self.modT[:, sl * 4:(sl + 1) * 4, :],
                in0=ps[:, 0:4 * NSEQ].rearrange("p (a b) -> p a b", a=4),
                in1=self.b_adaT[:, sl * 4:(sl + 1) * 4].unsqueeze(2).to_broadcast([128, 4, NSEQ]), op=ALU.add),
                reads=[small], writes=[self.bank_b[bk], mb])
        gi_pre = (0, 2, 4)
        gi_post = (1, 3, 5)
        for s in range(3):
            m_shift = self.modT[:, (3 * s) * 8:(3 * s + 1) * 8, :]
            m_scale = self.modT[:, (3 * s + 1) * 8:(3 * s + 2) * 8, :]
            m_gate = self.modT[:, (3 * s + 2) * 8:(3 * s + 3) * 8, :]
            gpre = self.gains[:, gi_pre[s], :].unsqueeze(2).to_broadcast([128, KC, NSEQ])
            gpost = self.gains[:, gi_post[s], :].unsqueeze(2).to_broadcast([128, KC, NSEQ])
            sc.op("dve", lambda e, m_scale=m_scale, gpre=gpre, s=s: e.scalar_tensor_tensor(
                out=self.mods[:, 3 * s + 0, :, :], in0=m_scale, scalar=1.0, in1=gpre, op0=ALU.add, op1=ALU.mult),
                reads=[small], writes=[mb])
            sc.op("dve", lambda e, m_shift=m_shift, s=s: e.tensor_copy(self.mods[:, 3 * s + 1, :, :], m_shift), writes=[mb])
            sc.op("dve", lambda e, m_gate=m_gate, gpost=gpost, s=s: e.scalar_tensor_tensor(
                out=self.mods[:, 3 * s + 2, :, :], in0=m_gate, scalar=(0.5 if s != 1 else 1.0), in1=gpost,
                op0=ALU.mult, op1=ALU.mult), reads=[small], writes=[mb])

    def sequence(self, b):
        self.phase_sync()
        self.load_x(b)
        self.rope_tables(b)
        self.phase_sync()
        self.ffn(b, 0, 0)
        self.phase_sync()
        self.mixer(b)
        self.phase_sync()
        self.ffn(b, 2, 1)
        self.phase_sync()
        self.store_x(b)

    def load_x(self, b):
        sc = self.sc
        for blk in range(self.NB):
            st = blk % self.NSTAGE
            sc.dma("sp", self.stage[st][:], self.x_d[b, blk * 128:(blk + 1) * 128, :], writes=[self.stage_b[st]])
            for half in range(2):
                bk = 2 * (blk % 2) + half
                for cc in range(4):
                    c = half * 4 + cc
                    sc.op("pe", lambda e, c=c, cc=cc, st=st, bk=bk: e.transpose(
                        self.bank[bk][:, cc * 128:(cc + 1) * 128], self.stage[st][:, c * 128:(c + 1) * 128], self.identf[:]),
                        reads=[self.stage_b[st], self.small_b], writes=[self.bank_b[bk]], signal=(cc == 3))
                src = self.bank[bk][:].rearrange("p (a b) -> p a b", a=4)
                dst = self.xT[:, half * 4:(half + 1) * 4, blk * 128:(blk + 1) * 128]
                if half == 0:
                    sc.op("act", lambda e, dst=dst, src=src: e.copy(dst, src), writes=[self.bank_b[bk], self.xT_b[blk // 4]])
                else:
                    sc.op("dve", lambda e, dst=dst, src=src: e.tensor_copy(dst, src), writes=[self.bank_b[bk], self.xT_b[blk // 4]])
            if blk % 4 == 3:
                self.stats_ahead(blk // 4, self.SQ2, self.SQ2_b, self.LN2, self.LN2_b, ssbank=4)

    def store_x(self, b):
        sc = self.sc
        for blk in range(self.NB):
            st = blk % self.NSTAGE
            for half in range(2):
                bk = 2 * (blk % 2) + half
                for cc in range(4):
                    c = half * 4 + cc
                    sc.op("pe", lambda e, c=c, cc=cc, bk=bk, blk=blk: e.transpose(
                        self.bank[bk][:, cc * 128:(cc + 1) * 128], self.xT[:, c, blk * 128:(blk + 1) * 128], self.identf[:]),
                        reads=[self.xT_b[blk // 4], self.small_b], writes=[self.bank_b[bk]], signal=(cc == 3))
                dst = self.stage[st][:, half * 512:(half + 1) * 512]
                src = self.bank[bk][:]
                if half == 0:
                    sc.op("act", lambda e, dst=dst, src=src: e.copy(dst, src), writes=[self.bank_b[bk], self.stage_b[st]])
                else:
                    sc.op("dve", lambda e, dst=dst, src=src: e.tensor_copy(dst, src), writes=[self.bank_b[bk], self.stage_b[st]])
            sc.dma("sp", self.out_d[b, blk * 128:(blk + 1) * 128, :], self.stage[st][:], reads=[self.stage_b[st]])
        for st in range(self.NSTAGE):
            for k, v in list(self.stage_b[st].r.items()) + list(self.stage_b[st].w.items()):
                self.region_b.w[k] = max(self.region_b.w.get(k, 0), v)

    def rope_tables(self, b):
        sc = self.sc
        NB = self.NB
        rb, tb = self.rope_b, self.ropetmp_b
        posf, ang, t, k, r1, r2 = self.rt
        V = lambda fn, reads=(), writes=(): sc.op("dve", fn, reads=reads, writes=writes)
        V(lambda e: e.tensor_copy(posf[:], self.pos_i[:, b, :].unsqueeze(2).to_broadcast([128, NB, 8])),
          reads=[self.small_b], writes=[tb])
        V(lambda e: e.tensor_tensor(out=ang[:], in0=posf[:], in1=self.invfreq[:].unsqueeze(1).to_broadcast([128, NB, 8]),
                                    op=ALU.mult), reads=[self.small_b], writes=[tb])
        for which in ("sin", "cos"):
            src = ang
            if which == "cos":
                V(lambda e: e.tensor_scalar(out=posf[:], in0=ang[:], scalar1=math.pi / 2, scalar2=None, op0=ALU.add), writes=[tb])
                src = posf
            V(lambda e, src=src: e.tensor_scalar(out=t[:], in0=src[:], scalar1=1.0 / (2 * math.pi), scalar2=MAGIC,
                                                 op0=ALU.mult, op1=ALU.add), writes=[tb])
            V(lambda e: e.tensor_scalar(out=k[:], in0=t[:], scalar1=MAGIC, scalar2=None, op0=ALU.subtract), writes=[tb])
            V(lambda e, src=src: e.scalar_tensor_tensor(out=r1[:], in0=k[:], scalar=-TWO_PI_HI, in1=src[:],
                                                        op0=ALU.mult, op1=ALU.add), writes=[tb])
            V(lambda e: e.scalar_tensor_tensor(out=r2[:], in0=k[:], scalar=-TWO_PI_LO, in1=r1[:],
                                               op0=ALU.mult, op1=ALU.add), writes=[tb])
            V(lambda e: e.tensor_scalar(out=r2[:], in0=r2[:], scalar1=PI_SAFE, scalar2=-PI_SAFE, op0=ALU.min, op1=ALU.max),
              writes=[tb])
            if which == "sin":
                sc.op("act", lambda e: e.activation(self.sk1[:], r2[:], AF.Sin), reads=[tb], writes=[rb])
                sc.op("dve", lambda e: e.tensor_scalar(out=self.sq1[:], in0=self.sk1[:], scalar1=0.125, scalar2=None,
                                                       op0=ALU.mult), writes=[rb])
            else:
                for hf in range(2):
                    sc.op("act", lambda e, hf=hf: e.activation(self.ck2[:, :, hf * 8:(hf + 1) * 8], r2[:], AF.Sin),
                          reads=[tb], writes=[rb])
                sc.op("dve", lambda e: e.tensor_scalar(out=self.cq2[:], in0=self.ck2[:], scalar1=0.125, scalar2=None,
                                                       op0=ALU.mult), writes=[rb])

    def stats_ahead(self, sg, SQ, SQ_b, LN, LN_b, ssbank):
        sc = self.sc
        t0 = sg * 512
        xb = self.xT_b[sg]
        bb = self.bank_b[ssbank]
        ps = self.bank[ssbank]
        for half in range(2):
            for cc in range(4):
                c = half * 4 + cc
                if cc % 2 == 0:
                    sc.op("act", lambda e, c=c, cc=cc: e.activation(SQ[:, cc, :], self.xT[:, c, t0:t0 + 512], AF.Square),
                          reads=[xb], writes=[SQ_b])
                else:
                    sc.op("dve", lambda e, c=c, cc=cc: e.tensor_tensor(out=SQ[:, cc, :], in0=self.xT[:, c, t0:t0 + 512],
                                                                       in1=self.xT[:, c, t0:t0 + 512], op=ALU.mult),
                          reads=[xb], writes=[SQ_b])
            for cc in range(4):
                c = half * 4 + cc
                sc.op("pe", lambda e, c=c, cc=cc: e.matmul(ps[:], self.onesb[:], SQ[:, cc, :], start=(c == 0), stop=(c == KC - 1)),
                      reads=[SQ_b, self.const_b], writes=[bb], signal=(cc == 3))
        sc.op("act", lambda e: e.activation(LN[:], ps[:], AF.Ln, bias=EPS, scale=1.0 / D), writes=[bb, LN_b])
        sc.op("act", lambda e: e.activation(self.RSTDN[:, sg, :], LN[:], AF.Exp, scale=-0.5), reads=[LN_b], writes=[self.rstdn_b[sg]])

    def prenorm_apply(self, b, s, sg, hT_dst, hT_buf, HTMP, HTMP_b):
        sc = self.sc
        t0 = sg * 512
        xb = self.xT_b[sg]
        G = self.mods[:, 3 * s + 0, :, :]
        Sh = self.mods[:, 3 * s + 1, :, :]
        for c in range(KC):
            tmp = HTMP[c % 2]
            tb = HTMP_b[c % 2]
            sc.op("dve", lambda e, c=c, tmp=tmp: e.scalar_tensor_tensor(
                out=tmp[:], in0=self.xT[:, c, t0:t0 + 512], scalar=G[:, c, b:b + 1], in1=self.RSTDN[:, sg, :],
                op0=ALU.mult, op1=ALU.mult), reads=[xb, self.mod_b, self.rstdn_b[sg]], writes=[tb])
            sc.op("act", lambda e, c=c, tmp=tmp: e.activation(hT_dst[:, c, :], tmp[:], AF.Identity,
                                                               bias=Sh[:, c, b:b + 1], scale=1.0),
                  reads=[self.mod_b, tb], writes=[hT_buf])

    def postnorm_update(self, b, s, sg, ystage, ybufs, LN2, RSTD2, LN2_b, UTMP, UTMP_b, ssbank):
        sc = self.sc
        t0 = sg * 512
        xb = self.xT_b[sg]
        bb = self.bank_b[ssbank]
        ps = self.bank[ssbank]
        sc.op("act", lambda e: e.activation(LN2[:], ps[:], AF.Ln, bias=EPS, scale=1.0 / D), writes=[bb, LN2_b])
        sc.op("act", lambda e: e.activation(RSTD2[:], LN2[:], AF.Exp, scale=-0.5), writes=[LN2_b])
        Gp = self.mods[:, 3 * s + 2, :, :]
        for c in range(KC):
            tmp = UTMP[c % 2]
            tb = UTMP_b[c % 2]
            sc.op("dve", lambda e, c=c, tmp=tmp: e.tensor_tensor(out=tmp[:], in0=ystage[:, c, :], in1=RSTD2[:], op=ALU.mult),
                  reads=list(ybufs) + [LN2_b], writes=[tb])
            sc.op("dve", lambda e, c=c, tmp=tmp: e.scalar_tensor_tensor(
                out=self.xT[:, c, t0:t0 + 512], in0=tmp[:], scalar=Gp[:, c, b:b + 1], in1=self.xT[:, c, t0:t0 + 512],
                op0=ALU.mult, op1=ALU.add), reads=[self.mod_b, tb], writes=[xb])

    def ffn(self, b, s, wi):
        sc = self.sc
        T, SGT = self.T, self.SGT
        wgv = self.wg_d[wi].rearrange("(k p) f -> p k f", p=128)
        wuv = self.wu_d[wi].rearrange("(k p) f -> p k f", p=128)
        wd = self.wd_d[wi]
        R1 = self.R1_bufs
        for tg in range(self.NTG):
            sg0 = tg * SGT
            for sgl in range(SGT):
                self.prenorm_apply(b, s, sg0 + sgl, self.hT_f[:, :, sgl * 512:(sgl + 1) * 512], self.hTf_b[sgl],
                                   self.HTMP, self.HTMP_b)
            nslab = (DFF + 255) // 256
            gi = 0
            for sl in range(nslab):
                c0 = sl * 256
                ncol = min(256, DFF - c0)
                si = self.slab([
                    (lambda r, ncol=ncol: r[:, 0:2048].rearrange("p (k f) -> p k f", k=KC)[:, :, 0:ncol], wgv[:, :, c0:c0 + ncol]),
                    (lambda r, ncol=ncol: r[:, 2048:4096].rearrange("p (k f) -> p k f", k=KC)[:, :, 0:ncol], wuv[:, :, c0:c0 + ncol])])
                rb = self.ring_b[si]
                slotg = self.ring[si][:, 0:2048].rearrange("p (k f) -> p k f", k=KC)
                slotu = self.ring[si][:, 2048:4096].rearrange("p (k f) -> p k f", k=KC)
                nfc = (ncol + 127) // 128
                for sgl in range(SGT):
                    hb = self.hTf_b[sgl]
                    for fcl in range(nfc):
                        fc = sl * 2 + fcl
                        fr = min(128, ncol - fcl * 128)
                        gb, ub = (gi % 2), 2 + (gi % 2)
                        E = self.E[gi % 2]
                        eb = self.E_b[gi % 2]
                        gi += 1
                        gps, ups = self.bank[gb], self.bank[ub]
                        for k in range(KC):
                            sc.op("pe", lambda e, k=k, fcl=fcl, fr=fr, gps=gps, slotg=slotg, sgl=sgl: e.matmul(
                                gps[0:fr, :], slotg[:, k, fcl * 128:fcl * 128 + fr], self.hT_f[:, k, sgl * 512:(sgl + 1) * 512],
                                start=(k == 0), stop=(k == KC - 1)),
                                reads=[rb, hb], writes=[self.bank_b[gb]], signal=(k == KC - 1))
                        for k in range(KC):
                            sc.op("pe", lambda e, k=k, fcl=fcl, fr=fr, ups=ups, slotu=slotu, sgl=sgl: e.matmul(
                                ups[0:fr, :], slotu[:, k, fcl * 128:fcl * 128 + fr], self.hT_f[:, k, sgl * 512:(sgl + 1) * 512],
                                start=(k == 0), stop=(k == KC - 1)),
                                reads=[rb, hb], writes=[self.bank_b[ub]], signal=(k == KC - 1))
                        sc.op("act", lambda e, E=E, gps=gps, fr=fr: e.activation(E[0:fr, :], gps[0:fr, :], AF.Silu),
                              writes=[self.bank_b[gb], eb])
                        sc.op("dve", lambda e, E=E, ups=ups, fr=fr, fc=fc, sgl=sgl: e.tensor_tensor(
                            out=self.aT[0:fr, fc, sgl * 512:(sgl + 1) * 512], in0=ups[0:fr, :], in1=E[0:fr, :], op=ALU.mult),
                            reads=[eb], writes=[self.bank_b[ub], self.aT_b[sgl]])
            for c in range(KC):
                si = self.slab([
                    (lambda r: r[:, 0:FCH * 128].rearrange("p (j d) -> p j d", j=FCH)[:, 0:21, :],
                     wd[0:21 * 128, c * 128:(c + 1) * 128].rearrange("(j p) d -> p j d", p=128)),
                    (lambda r: r[:, 0:FCH * 128].rearrange("p (j d) -> p j d", j=FCH)[0:64, 21, :],
                     wd[21 * 128:DFF, c * 128:(c + 1) * 128])])
                rb = self.ring_b[si]
                slot = self.ring[si][:, 0:FCH * 128].rearrange("p (j d) -> p j d", j=FCH)
                for sgl in range(SGT):
                    yb = 4 + ((c * SGT + sgl) % 2)
                    yps = self.bank[yb]
                    for j in range(FCH):
                        fr = 128 if j < 21 else 64
                        sc.op("pe", lambda e, j=j, fr=fr, yps=yps, slot=slot, sgl=sgl: e.matmul(
                            yps[:], slot[0:fr, j, :], self.aT[0:fr, j, sgl * 512:(sgl + 1) * 512],
                            start=(j == 0), stop=(j == FCH - 1)),
                            reads=[rb, self.aT_b[sgl]], writes=[self.bank_b[yb]], signal=(j == FCH - 1))
                    ssb = 6 + sgl
                    sq = self.SQ2[:, (c * SGT + sgl) % 4, :]
                    sc.op("act", lambda e, c=c, sgl=sgl, yps=yps: e.copy(self.ystage_f[:, c, sgl * 512:(sgl + 1) * 512], yps[:]),
                          writes=[self.bank_b[yb]] + R1)
                    sc.op("act", lambda e, sq=sq, yps=yps: e.activation(sq, yps[:], AF.Square),
                          writes=[self.bank_b[yb], self.SQ2_b])
                    self.tick_deferred()
                    self.defer(1, lambda c=c, ssb=ssb, sq=sq: sc.op(
                        "pe", lambda e: e.matmul(self.bank[ssb][:], self.onesb[:], sq, start=(c == 0), stop=(c == KC - 1)),
                        reads=[self.SQ2_b, self.const_b], writes=[self.bank_b[ssb]], signal=True))
            self.tick_deferred(flush=True)
            for sgl in range(SGT):
                self.postnorm_update(b, s, sg0 + sgl, self.ystage_f[:, :, sgl * 512:(sgl + 1) * 512], R1,
                                     self.LN2, self.RSTD2, self.LN2_b, self.UTMP, self.UTMP_b, ssbank=6 + sgl)
            if s == 0:
                for sgl in range(SGT):
                    self.stats_ahead(sg0 + sgl, self.SQ2, self.SQ2_b, self.LN2, self.LN2_b, ssbank=6 + sgl)

    def mixer(self, b):
        sc = self.sc
        S, NB, NG = self.S, self.NB, self.NG
        s = 1
        for sg in range(NG):
            self.prenorm_apply(b, s, sg, self.hT_m[:, :, sg * 512:(sg + 1) * 512], self.hTm_b, self.HTMPm, self.HTMPm_b)
        self.fox_prep(b)
        self.phase_sync()
        for vi in range(2):
            sc.op("dve", lambda e, vi=vi: e.memset(self.VAs[vi][:, :, :, 64:65], 1.0), writes=[self.VAs_b[vi]])
        sc.op("dve", lambda e: e.memset(self.KT[64:128, :], 0.0), writes=[self.KT_b])
        sc.op("dve", lambda e: e.memset(self.KT1[0:64, :], 0.0), writes=[self.KT1_b])
        winv = self.w_in_d.rearrange("(k p) f -> p k f", p=128)
        NU = 4 + NHB

        def inproj_gen(u):
            isA = u < 4
            vi = u % 2
            if isA:
                ncol = 384
                parts = [(lambda r, i=i: r[:, 0:KC * 384].rearrange("p (k f) -> p k f", k=KC)[:, :, i * 128:(i + 1) * 128],
                          winv[:, :, base + u * 128: base + (u + 1) * 128]) for i, base in enumerate((0, 512, 1024))]
            else:
                h = u - 4
                ncol = 192
                parts = [(lambda r, i=i: r[:, 0:KC * 192].rearrange("p (k f) -> p k f", k=KC)[:, :, i * 64:(i + 1) * 64],
                          winv[:, :, base + h * 64: base + (h + 1) * 64]) for i, base in enumerate((1536, 2048, 2560))]
            si = self.slab(parts)
            rb = self.ring_b[si]
            slot = self.ring[si][:, 0:KC * ncol].rearrange("p (k f) -> p k f", k=KC)
            for blk in range(NB):
                bk = blk % 2
                ps = self.bank[bk]
                for k in range(KC):
                    sc.op("pe", lambda e, k=k: e.matmul(
                        ps[:, 0:ncol], self.hT_m[:, k, blk * 128:(blk + 1) * 128], slot[:, k, 0:ncol],
                        start=(k == 0), stop=(k == KC - 1)),
                        reads=[rb, self.hTm_b], writes=[self.bank_b[bk]], signal=(k == KC - 1))
                if isA:
                    self.defer(2, lambda blk=blk, ps=ps, bk=bk: self.evac_A(blk, ps, self.bank_b[bk], vi))
                else:
                    self.defer(2, lambda blk=blk, ps=ps, bk=bk: self.evac_B(blk, ps, self.bank_b[bk], u - 4, vi))
                yield blk

        def stageB(u):
            self.tick_deferred(flush=True)
            isA = u < 4
            if u == 4:
                sc.op("dve", lambda e: e.memset(self.QT[64:128, :], 0.0), writes=[self.QT_b])
            self.unit_transposes(isA, u)

        def att_gen(u):
            vi = u % 2
            if u < 4:
                for h2 in range(2):
                    yield from self.attention(True, hp0=h2 * 64, K=64, vsel=h2, col0=(u * 2 + h2) * 64, vi=vi)
            else:
                yield from self.attention(False, hp0=0, K=70, vsel=0, col0=512 + (u - 4) * 64, vi=vi)

        for _ in inproj_gen(0):
            self.tick_deferred()
        self.tick_deferred(flush=True)
        for p in self.rope_A_pieces():
            p()
        stageB(0)
        for u in range(NU):
            nxt = inproj_gen(u + 1) if u + 1 < NU else None
            nsteps = (2 if u < 4 else 1) * (sum(4 * g + 4 for g in range(NG)) + 5)
            stride = max(2, (nsteps * 55 // 100) // NB) if (u + 1 < 4) else max(2, nsteps // (NB + 1))
            for i, _ in enumerate(att_gen(u)):
                if nxt is not None and i % stride == stride - 1:
                    if next(nxt, None) is None:
                        nxt = None
                        if u + 1 < 4:
                            for pi, p in enumerate(self.rope_A_pieces()):
                                self.defer(4 + 3 * pi, p)
            if nxt is not None:
                for _ in nxt:
                    self.tick_deferred()
                if u + 1 < 4:
                    self.tick_deferred(flush=True)
                    for p in self.rope_A_pieces():
                        p()
            if u + 1 < NU:
                stageB(u + 1)
        self.tick_deferred(flush=True)
        self.phase_sync()
        self.outproj(b)

    def fox_prep(self, b):
        sc = self.sc
        NB = self.NB
        Fb = self.F_b
        winv = self.w_in_d.rearrange("(k p) f -> p k f", p=128)
        sc.dma("pool", self.wf[:], winv[:, :, 3072:3080], writes=[self.wf_b])
        ps = self.bank[2]
        for blk in range(NB):
            for k in range(KC):
                sc.op("pe", lambda e, k=k, blk=blk: e.matmul(ps[:, blk * 8:(blk + 1) * 8], self.hT_m[:, k, blk * 128:(blk + 1) * 128],
                                                            self.wf[:, k, :], start=(k == 0), stop=(k == KC - 1)),
                      reads=[self.wf_b, self.hTm_b], writes=[self.bank_b[2]], signal=(k == KC - 1))
        psv = ps[:, 0:NB * 8].rearrange("p (a b) -> p a b", a=NB)
        sc.op("dve", lambda e: e.tensor_tensor(out=self.FL[:], in0=psv, in1=self.bforget[:].unsqueeze(1).to_broadcast([128, NB, 8]),
                                               op=ALU.add), reads=[self.small_b], writes=[self.bank_b[2], Fb] + self.rstdn_b)
        sc.op("act", lambda e: e.activation(self.FL2[:], self.FL[:], AF.Exp, scale=-1.0), writes=[Fb])
        sc.op("act", lambda e: e.activation(self.FL[:], self.FL2[:], AF.Ln, bias=1.0, scale=1.0), writes=[Fb])
        wps, tps = self.bank[3], self.bank[4]
        flat = self.FL[:].rearrange("p a b -> p (a b)")
        sc.op("pe", lambda e: e.matmul(wps[:, 0:NB * 8], self.trif[:], flat, start=True, stop=True),
              reads=[Fb, self.small_b], writes=[self.bank_b[3]])
        sc.op("pe", lambda e: e.matmul(tps[:, 0:NB * 8], self.onesf[:], flat, start=True, stop=True),
              reads=[Fb, self.const_b], writes=[self.bank_b[4]])
        sc.op("dve", lambda e: e.tensor_copy(self.FL2[:], tps[:, 0:NB * 8].rearrange("p (a b) -> p a b", a=NB)),
              writes=[self.bank_b[4], Fb])
        sc.op("dve", lambda e: e.memset(self.CARRY[:, 0, :], 0.0), writes=[Fb])
        for i in range(1, NB):
            sc.op("dve", lambda e, i=i: e.tensor_tensor(out=self.CARRY[:, i, :], in0=self.CARRY[:, i - 1, :],
                                                        in1=self.FL2[:, i - 1, :], op=ALU.add), writes=[Fb])
        sc.op("dve", lambda e: e.tensor_tensor(out=self.NEGF[:], in0=wps[:, 0:NB * 8].rearrange("p (a b) -> p a b", a=NB),
                                               in1=self.CARRY[:], op=ALU.add), writes=[self.bank_b[3], Fb])
        V = lambda fn: sc.op("dve", fn, writes=[Fb])
        V(lambda e: e.tensor_copy(self.FH[:], self.NEGF[:]))
        V(lambda e: e.tensor_copy(self.F32T[:], self.FH[:]))
        V(lambda e: e.tensor_tensor(out=self.NEGF[:], in0=self.NEGF[:], in1=self.F32T[:], op=ALU.subtract))
        V(lambda e: e.tensor_copy(self.FM[:], self.NEGF[:]))
        V(lambda e: e.tensor_copy(self.F32T[:], self.FM[:]))
        V(lambda e: e.tensor_tensor(out=self.NEGF[:], in0=self.NEGF[:], in1=self.F32T[:], op=ALU.subtract))
        V(lambda e: e.tensor_copy(self.FLo[:], self.NEGF[:]))

    def evac_A(self, blk, ps, bb, vi):
        sc = self.sc
        qk = self.QKtm_b
        VA, VAb = self.VAs[vi], self.VAs_b[vi]
        sc.op("dve", lambda e: e.tensor_copy(VA[:, blk, :, 0:64], ps[:, 256:384].rearrange("p (a b) -> p a b", a=2)),
              writes=[bb, VAb])
        sc.op("dve", lambda e: e.tensor_copy(
            self.ROTF[:, blk, :, :, :], ps[:, 0:256].rearrange("p (q h d) -> p q h d", q=2, h=2)[:, :, :, 0:16]),
            writes=[bb, self.rtmp_b])
        sc.op("act", lambda e: e.activation(self.QKtm[:, blk, 0, :], ps[:, 0:128], AF.Copy, scale=0.125), writes=[bb, qk])
        sc.op("act", lambda e: e.copy(self.QKtm[:, blk, 1, :], ps[:, 128:256]), writes=[bb, qk])

    def rope_A_pieces(self):
        sc = self.sc
        NB, RH = self.NB, self.RH
        rt = self.rtmp_b
        qk = self.QKtm_b
        pieces = []
        for qi, (c2, s1) in enumerate(((self.cq2, self.sq1), (self.ck2, self.sk1))):
            for n0 in range(0, NB, RH):
                def piece(qi=qi, c2=c2, s1=s1, n0=n0):
                    X = self.ROTF[:, n0:n0 + RH, qi, :, :]
                    Dv = self.QKtm[:, n0:n0 + RH, qi, :].rearrange("p n (h d) -> p n h d", h=2)
                    cosb = c2[:, n0:n0 + RH, :].unsqueeze(2).to_broadcast([128, RH, 2, 16])
                    sinb = s1[:, n0:n0 + RH, :].unsqueeze(2).to_broadcast([128, RH, 2, 8])
                    sc.op("dve", lambda e: e.tensor_tensor(out=self.RA[:], in0=X, in1=cosb, op=ALU.mult),
                          reads=[self.rope_b], writes=[rt])
                    sc.op("dve", lambda e: e.tensor_tensor(out=self.RB1[:], in0=X[:, :, :, 8:16], in1=sinb, op=ALU.mult),
                          reads=[self.rope_b], writes=[rt])
                    sc.op("dve", lambda e: e.tensor_tensor(out=self.RB2[:], in0=X[:, :, :, 0:8], in1=sinb, op=ALU.mult),
                          reads=[self.rope_b], writes=[rt])
                    sc.op("dve", lambda e: e.tensor_tensor(out=Dv[:, :, :, 0:8], in0=self.RA[:, :, :, 0:8], in1=self.RB1[:],
                                                           op=ALU.subtract), reads=[rt], writes=[qk])
                    sc.op("dve", lambda e: e.tensor_tensor(out=Dv[:, :, :, 8:16], in0=self.RA[:, :, :, 8:16], in1=self.RB2[:],
                                                           op=ALU.add), reads=[rt], writes=[qk])
                pieces.append(piece)
        return pieces

    def evac_B(self, blk, ps, bb, h, vi):
        sc = self.sc
        qk = self.QKtm_b
        VA, VAb = self.VAs[vi], self.VAs_b[vi]
        sc.op("dve", lambda e: e.tensor_copy(VA[:, blk, 0, 0:64], ps[:, 128:192]), writes=[bb, VAb])
        sc.op("act", lambda e: e.activation(self.QKtm[:, blk, 0, 0:64], ps[:, 0:64], AF.Copy, scale=0.125), writes=[bb, qk])
        sc.op("dve", lambda e: e.tensor_copy(self.QKtm[:, blk, 1, 0:64], ps[:, 64:128]), writes=[bb, qk])
        if blk == self.NB - 1:
            Fb = self.F_b
            sc.op("dve", lambda e: e.memset(self.QKtm[:, :, 0, 67:70], 1.0), writes=[qk])
            sc.op("dve", lambda e: e.memset(self.QKtm[:, :, 1, 64:67], 1.0), writes=[qk])
            for i, src_t in enumerate((self.FH, self.FM, self.FLo)):
                sc.op("dve", lambda e, i=i, src_t=src_t: e.tensor_scalar(out=self.QKtm[:, :, 0, 64 + i], in0=src_t[:, :, h], scalar1=-1.0,
                                                                       scalar2=None, op0=ALU.mult), reads=[Fb], writes=[qk])
                sc.op("dve", lambda e, i=i, src_t=src_t: e.tensor_copy(self.QKtm[:, :, 1, 67 + i], src_t[:, :, h]), reads=[Fb], writes=[qk])

    def unit_transposes(self, isA, u):
        sc = self.sc
        NB = self.NB
        ncol = 128 if isA else 70
        for g4 in range(NB // 4):
            qb, kb = (6, 7) if g4 % 2 == 0 else (2, 3)
            tq = self.tp_view(qb)
            tk = self.tp_view(kb)
            for qi, (tp, tb) in enumerate(((tq, qb), (tk, kb))):
                for bl in range(4):
                    blk = g4 * 4 + bl
                    sc.op("pe", lambda e, qi=qi, bl=bl, blk=blk, tp=tp: e.transpose(
                        tp[0:ncol, bl * 128:(bl + 1) * 128], self.QKtm[:, blk, qi, 0:ncol], self.identb[:]),
                        reads=[self.QKtm_b, self.const_b], writes=[self.bank_b[tb]], signal=(bl == 3))
            sc.op("act", lambda e, tq=tq, g4=g4: e.copy(self.QT[0:ncol, g4 * 512:(g4 + 1) * 512], tq[0:ncol, 0:512]),
                  writes=[self.bank_b[qb], self.QT_b])
            if isA:
                sc.op("dve", lambda e, tk=tk, g4=g4: e.tensor_copy(self.KT[0:64, g4 * 512:(g4 + 1) * 512], tk[0:64, 0:512]),
                      writes=[self.bank_b[kb], self.KT_b])
                sc.op("dve", lambda e, tk=tk, g4=g4: e.tensor_copy(self.KT1[64:128, g4 * 512:(g4 + 1) * 512], tk[64:128, 0:512]),
                      writes=[self.bank_b[kb], self.KT1_b])
            else:
                sc.op("dve", lambda e, tk=tk, g4=g4: e.tensor_copy(self.KT[0:ncol, g4 * 512:(g4 + 1) * 512], tk[0:ncol, 0:512]),
                      writes=[self.bank_b[kb], self.KT_b])

    def attention(self, isA, hp0, K, vsel, col0, vi):
        KTt, KTb = (self.KT1, self.KT1_b) if (isA and hp0 == 64) else (self.KT, self.KT_b)
        VA, VAb = self.VAs[vi], self.VAs_b[vi]
        SB = (2, 3, 6, 7)
        sc = self.sc
        NG = self.NG
        LAG = 5
        NPT = self.NPT
        tiles = [(g, j) for g in range(NG) for j in range(4 * g + 4)]
        n = len(tiles)

        def emit_qk(t):
            g, j = tiles[t]
            c0 = max(j - 4 * g, 0)
            sbk = SB[t % 4]
            sps = self.bank[sbk]
            PT = self.PT[t % NPT]
            ptb = self.PT_b[t % NPT]
            diag = (not isA) and (j >= 4 * g)
            sc.op("pe", lambda e: e.matmul(
                sps[:, c0 * 128:512], KTt[:, j * 128:(j + 1) * 128],
                self.QT[:, g * 512 + c0 * 128:(g + 1) * 512], start=True, stop=(not diag)),
                reads=[self.QT_b, KTb], writes=[self.bank_b[sbk]], signal=(not diag))
            if diag:
                sc.op("pe", lambda e: e.matmul(
                    sps[:, c0 * 128:(c0 + 1) * 128], self.identb[:], self.negm[:], start=False, stop=True),
                    reads=[self.const_b], writes=[self.bank_b[sbk]], signal=True)
            sc.op("act", lambda e: e.activation(PT[:, c0 * 128:512], sps[:, c0 * 128:512], AF.Exp),
                  writes=[self.bank_b[sbk], ptb])
            if isA:
                d0 = 4 * g + c0 - j
                nblk = 4 - c0
                msk = self.multm[:, d0:d0 + nblk, :].rearrange("p a b -> p (a b)")
                sc.op("dve", lambda e: e.tensor_tensor(
                    out=PT[:, c0 * 128:512], in0=PT[:, c0 * 128:512], in1=msk, op=ALU.mult),
                    reads=[self.const_b], writes=[ptb])

        def emit_pv(t):
            g, j = tiles[t]
            c0 = max(j - 4 * g, 0)
            PT = self.PT[t % NPT]
            ptb = self.PT_b[t % NPT]
            ob = 4 + (g % 2)
            opsv = self.bank[ob][:, 0:260].rearrange("p (a b) -> p a b", a=4)
            for c in range(c0, 4):
                sc.op("pe", lambda e, c=c: e.matmul(
                    opsv[:, c, :], PT[:, c * 128:(c + 1) * 128], VA[:, j, vsel, 0:65],
                    start=(j == 0 and c == 0), stop=(j == 4 * g + c), skip_group_check=True),
                    reads=[ptb, VAb], writes=[self.bank_b[ob]], signal=(c == 3))
            if j == 4 * g + 3:
                def norm(opsv=opsv, ob=ob, g=g):
                    sc.op("dve", lambda e: e.reciprocal(out=self.RDEN[:], in_=opsv[:, :, 64]),
                          writes=[self.bank_b[ob], self.rden_b])
                    sc.op("dve", lambda e: e.tensor_tensor(
                        out=self.merged[:, 4 * g:4 * g + 4, col0:col0 + 64], in0=opsv[:, :, 0:64],
                        in1=self.RDEN[:].unsqueeze(2).to_broadcast([128, 4, 64]), op=ALU.mult),
                        reads=[self.rden_b], writes=[self.bank_b[ob], self.merged_b])
                self.defer(4, norm)

        for t in range(n + LAG):
            if t < n:
                emit_qk(t)
            if t - LAG >= 0:
                emit_pv(t - LAG)
            self.tick_deferred()
            yield t

    def outproj(self, b):
        sc = self.sc
        NB, NG = self.NB, self.NG
        s = 1
        mergedT = self.hT_m
        mtb = self.hTm_b
        ssb = self.ss_b
        sc.op("dve", lambda e: e.memset(self.SSAB[:], 0.0), writes=[ssb])
        for blk in range(NB):
            for grp in range(2):
                sc.op("act", lambda e, blk=blk, grp=grp: e.activation(
                    self.SQJ[:], self.merged[:, blk, grp * 512:(grp + 1) * 512], AF.Square,
                    accum_out=self.SSAB[:, blk, grp:grp + 1]),
                    reads=[self.merged_b], writes=[ssb])
        sc.op("act", lambda e: e.activation(self.RSAB[:], self.SSAB[:], AF.Ln, bias=EPS, scale=1.0 / 512), writes=[ssb])
        sc.op("act", lambda e: e.activation(self.RSAB[:], self.RSAB[:], AF.Exp, scale=-0.5), writes=[ssb])
        gout = self.gains[:, 6, :]
        wov = self.w_out_d.rearrange("(k p) f -> p k f", p=128)
        slots = []
        for hf in range(2):
            si = self.slab([(lambda r: r[:].rearrange("p (k f) -> p k f", k=KC), wov[:, :, hf * 512:(hf + 1) * 512])], keep=hf)
            slot = self.ring[si][:].rearrange("p (k f) -> p k f", k=KC)
            slots.append((slot, self.ring_b[si]))

        def norm_transpose(blk):
            mn = self.MN[blk % 2]
            mnb = self.MN_b[blk % 2]
            sc.op("dve", lambda e: e.tensor_tensor(
                out=mn[:].rearrange("p (a b) -> p a b", a=2), in0=self.merged[:, blk, :].rearrange("p (a b) -> p a b", a=2),
                in1=self.RSAB[:, blk, :].unsqueeze(2).to_broadcast([128, 2, 512]), op=ALU.mult),
                reads=[self.merged_b, ssb], writes=[mnb])
            tb = 6 + (blk % 2)
            tp = self.tp_view(tb)
            for c in range(KC):
                sc.op("pe", lambda e, c=c: e.transpose(tp[:, c * 128:(c + 1) * 128], mn[:, c * 128:(c + 1) * 128], self.identb[:]),
                      reads=[mnb, self.const_b], writes=[self.bank_b[tb]], signal=(c == KC - 1))
            sc.op("dve", lambda e: e.tensor_tensor(
                out=mergedT[:, :, blk * 128:(blk + 1) * 128], in0=tp.rearrange("p (a b) -> p a b", a=KC),
                in1=gout.unsqueeze(2).to_broadcast([128, KC, 128]), op=ALU.mult),
                reads=[self.small_b], writes=[self.bank_b[tb], mtb])

        yi = [0]

        def mm_chunk(sg, c):
            slot, rb = slots[c // 4]
            yb = yi[0] % 4
            yi[0] += 1
            yps = self.bank[yb]
            pnb = 4 + (sg % 2)
            for k in range(KC):
                sc.op("pe", lambda e, k=k: e.matmul(
                    yps[:], slot[:, k, (c % 4) * 128:(c % 4 + 1) * 128], mergedT[:, k, sg * 512:(sg + 1) * 512],
                    start=(k == 0), stop=(k == KC - 1)),
                    reads=[rb, mtb], writes=[self.bank_b[yb]], signal=(k == KC - 1))
            sq = self.SQ2m[:, c % 4, :]

            def evac():
                sc.op("act", lambda e: e.copy(self.ystage_m[:, c, :], yps[:]), writes=[self.bank_b[yb], self.ysm_b])
                sc.op("act", lambda e: e.activation(sq, yps[:], AF.Square), writes=[self.bank_b[yb], self.SQ2m_b])

            def stat():
                sc.op("pe", lambda e: e.matmul(self.bank[pnb][:], self.onesb[:], sq, start=(c == 0), stop=(c == KC - 1)),
                      reads=[self.SQ2m_b, self.const_b], writes=[self.bank_b[pnb]], signal=True)
            return evac, stat

        def pn_sa(sg):
            self.postnorm_update(b, s, sg, self.ystage_m, [self.ysm_b], self.LN2m, self.RSTD2m, self.LN2m_b,
                                 self.UTMPm, self.UTMPm_b, ssbank=4 + (sg % 2))

        for bl in range(4):
            norm_transpose(bl)
        for sg in range(NG):
            if sg + 1 < NG:
                for bl in range(4):
                    norm_transpose((sg + 1) * 4 + bl)
            pend = [mm_chunk(sg, c) for c in range(4)]
            if sg > 0:
                pn_sa(sg - 1)
            prev_stat = None
            for c in range(KC):
                if c < 4:
                    evac, stat = pend[c]
                else:
                    evac, stat = mm_chunk(sg, c)
                evac()
                if prev_stat is not None:
                    prev_stat()
                prev_stat = stat
            prev_stat()
            if sg > 0:
                self.stats_ahead(sg - 1, self.SQ2m, self.SQ2m_b, self.LN2m, self.LN2m_b, ssbank=4 + ((sg - 1) % 2))
        pn_sa(NG - 1)
        self.stats_ahead(NG - 1, self.SQ2m, self.SQ2m_b, self.LN2m, self.LN2m_b, ssbank=4 + ((NG - 1) % 2))


def _consts():
    identf = np.eye(128, dtype=np.float32)
    s_idx = np.arange(128)[:, None]
    t_idx = np.arange(128)[None, :]
    trif = (s_idx <= t_idx).astype(np.float32)
    kk = np.arange(128)[:, None, None]
    DD = np.arange(16)[None, :, None]
    tq = np.arange(128)[None, None, :]
    delta = DD * 128 + tq - kk
    m = ((delta >= 0) & (delta <= 128)).astype(np.float32)
    m += ((delta >= 0) & (delta <= 512) & (delta % 4 == 0)).astype(np.float32)
    m += ((delta >= 0) & (delta <= 2048) & (delta % 16 == 0)).astype(np.float32)
    mult = m.reshape(128, 16 * 128).astype(np.float32)
    neg = np.where(t_idx >= s_idx, 0.0, -30000.0).astype(np.float32)
    inv_freq = (THETA ** (-np.arange(0, ROT, 2, dtype=np.float32) / ROT)).astype(np.float32)
    invfreq = np.ascontiguousarray(np.broadcast_to(inv_freq[None, :], (128, 8))).astype(np.float32)
    return identf, trif, mult, neg, invfreq


def make_in_maps(inputs, S, NSEQ, ncores):
    f = lambda a: np.ascontiguousarray(np.asarray(a))
    x = f(inputs["x"]).astype(np.float32, copy=False)
    c = f(inputs["c"]).astype(np.float32, copy=False)
    pos = f(inputs["positions"]).astype(np.int32, copy=False)
    NB = S // 128
    identf, trif, mult, neg, invfreq = _consts()
    fm = lambda g: np.ascontiguousarray(np.asarray(g, dtype=np.float32).reshape(-1, 128).T)
    gains = np.stack([fm(inputs["g_pre_ff1"][0]), fm(inputs["g_post_ff1"][0]), fm(inputs["g_pre_mix"][0]),
                      fm(inputs["g_post_mix"][0]), fm(inputs["g_pre_ff2"][0]), fm(inputs["g_post_ff2"][0]),
                      fm(np.concatenate([np.asarray(inputs["g_out_a"][0]), np.asarray(inputs["g_out_b"][0])]))], axis=1)
    gains = np.ascontiguousarray(gains.astype(np.float32))
    shared = {
        "w_ada": f(inputs["w_ada"][0]), "b_adaT": fm(inputs["b_ada"][0]), "gains": gains,
        "bforget": np.ascontiguousarray(np.broadcast_to(np.asarray(inputs["b_forget"][0], dtype=np.float32)[None, :], (128, NHB))),
        "invfreq": invfreq,
        "w_ff1_gate": f(inputs["w_ff1_gate"][0]), "w_ff1_up": f(inputs["w_ff1_up"][0]), "w_ff1_down": f(inputs["w_ff1_down"][0]),
        "w_ff2_gate": f(inputs["w_ff2_gate"][0]), "w_ff2_up": f(inputs["w_ff2_up"][0]), "w_ff2_down": f(inputs["w_ff2_down"][0]),
        "w_in": f(inputs["w_in"][0]), "w_out": f(inputs["w_out"][0]),
        "c_identf": identf, "c_trif": trif, "c_mult": mult, "c_neg": neg,
    }
    maps = []
    for ci in range(ncores):
        sl = slice(ci * NSEQ, (ci + 1) * NSEQ)
        m = dict(shared)
        m["x"] = np.ascontiguousarray(x[sl])
        m["cT"] = np.ascontiguousarray(c[sl].reshape(NSEQ, KC, 128).transpose(2, 1, 0))
        m["pos"] = np.ascontiguousarray(pos[sl].reshape(NSEQ, NB, 128).transpose(2, 0, 1))
        maps.append(m)
    return maps


_NC_CACHE = {}


def run(inputs, S, NSEQ, ncores=NCORES, trace=False):
    key = (S, NSEQ)
    if key not in _NC_CACHE:
        _NC_CACHE[key] = Builder(S, NSEQ).build()
    nc = _NC_CACHE[key]
    maps = make_in_maps(inputs, S, NSEQ, ncores)
    res = run_bass_kernel_spmd(nc, maps, core_ids=list(range(ncores)), **({"trace": True} if trace else {}))
    out = np.concatenate([r["out"] for r in res.results], axis=0)
    return out, res


def kernel(**inputs):
    x = np.asarray(inputs["x"])
    Btot, S, _ = x.shape
    NSEQ = Btot // NCORES
    out, _ = run(inputs, S, NSEQ)
    return out.astype(np.float32, copy=False)
```

```python
import contextlib
import math

import numpy as np
import ml_dtypes

import concourse.bass as bass
import concourse.mybir as mybir
from concourse.bass_utils import run_bass_kernel_spmd

F32 = mybir.dt.float32
BF16 = mybir.dt.bfloat16
I32 = mybir.dt.int32
AF = mybir.ActivationFunctionType
ALU = mybir.AluOpType

D = 1024
KC = 8
DFF = 2752
FCH = 22
HD = 64
NHA = 8
NHB = 8
INCOLS = 3080
EPS = 1e-6
NCORES = 8
ROT = 16
THETA = 500000.0

MAGIC = 12582912.0
TWO_PI_HI = 6.28125
TWO_PI_LO = float(np.float32(2 * math.pi - 6.28125))
PI_SAFE = 3.1415925


class Buf:
    __slots__ = ("name", "w", "r", "dsem", "dcount")

    def __init__(self, name):
        self.name = name
        self.w = {}
        self.r = {}
        self.dsem = None
        self.dcount = 0


class Eng:
    def __init__(self, name, h, sem):
        self.name = name
        self.h = h
        self.sem = sem
        self.ticks = 0
        self.seen = {}
        self.stream = []


class Sched:
    def __init__(self, nc, es):
        self.nc = nc
        self.es = es
        self.sems = {}
        self.eng = {}
        for name, h in (("pe", nc.tensor), ("act", nc.scalar), ("dve", nc.vector),
                        ("pool", nc.gpsimd), ("sp", nc.sync)):
            sem = es.enter_context(nc.semaphore("e_" + name))
            self.eng[name] = Eng(name, h, sem)
            self.sems[("eng", name)] = sem
        self.nbuf = 0
        self.dry = False

    def buf(self, name):
        self.nbuf += 1
        return Buf(f"{name}_{self.nbuf}")

    def _deps(self, en, reads, writes):
        raw = {}
        war = {}
        for b in reads:
            for k, v in b.w.items():
                if raw.get(k, 0) < v:
                    raw[k] = v
        for b in writes:
            for k, v in b.w.items():
                if raw.get(k, 0) < v:
                    raw[k] = v
            for k, v in b.r.items():
                if war.get(k, 0) < v:
                    war[k] = v
        own = ("eng", en)
        deps = dict(raw)
        if en == "pe":
            deps.pop(own, None)
        for k, v in war.items():
            if k == own:
                continue
            if deps.get(k, 0) < v:
                deps[k] = v
        return deps

    def _emit_waits(self, E, deps):
        for k, v in deps.items():
            if E.seen.get(k, 0) >= v:
                continue
            E.h.wait_ge(self.sems[k], v)
            E.seen[k] = v
            E.stream.append(("wait", k, v))

    def op(self, en, fn, reads=(), writes=(), signal=True):
        if self.dry:
            return None
        E = self.eng[en]
        deps = self._deps(en, reads, writes)
        self._emit_waits(E, deps)
        inst = fn(E.h)
        key = ("eng", en)
        if signal:
            E.ticks += 1
            inst.then_inc(E.sem, 1)
            tick = E.ticks
            E.stream.append(("op", key, 1))
        else:
            tick = E.ticks + 1
            E.stream.append(("op", None, 0))
        for b in writes:
            if b.w.get(key, 0) < tick:
                b.w[key] = tick
        for b in reads:
            if b.r.get(key, 0) < tick:
                b.r[key] = tick
        return inst

    def dma(self, qn, out, in_, reads=(), writes=()):
        if self.dry:
            return None
        Q = self.eng[qn]
        deps = self._deps(qn, reads, writes)
        self._emit_waits(Q, deps)
        prim = writes[0] if writes else reads[0]
        if prim.dsem is None:
            prim.dsem = self.es.enter_context(self.nc.semaphore("d_" + prim.name))
            self.sems[("dma", prim.name)] = prim.dsem
        prim.dcount += 16
        key = ("dma", prim.name)
        Q.h.dma_start(out=out, in_=in_).then_inc(prim.dsem, 16)
        Q.stream.append(("op", key, 16))
        for b in writes:
            b.w[key] = prim.dcount
        for b in reads:
            b.r[key] = prim.dcount

    def final_wait(self, en, bufs):
        E = self.eng[en]
        deps = {}
        for b in bufs:
            for k, v in list(b.w.items()) + list(b.r.items()):
                if deps.get(k, 0) < v:
                    deps[k] = v
        deps.pop(("eng", en), None)
        self._emit_waits(E, deps)

    def check_deadlock(self):
        cnt = {}
        ptr = {n: 0 for n in self.eng}
        pending_fwd = {n: False for n in self.eng}
        progress = True
        while progress:
            progress = False
            for n, E in self.eng.items():
                while ptr[n] < len(E.stream):
                    kind, k, v = E.stream[ptr[n]]
                    if kind == "wait":
                        if cnt.get(k, 0) >= v:
                            ptr[n] += 1
                            progress = True
                        else:
                            break
                    else:
                        if k is not None:
                            cnt[k] = cnt.get(k, 0) + v
                        ptr[n] += 1
                        progress = True
        stuck = {n: (ptr[n], len(E.stream), E.stream[ptr[n]] if ptr[n] < len(E.stream) else None)
                 for n, E in self.eng.items() if ptr[n] < len(E.stream)}
        if stuck:
            raise RuntimeError(f"semaphore deadlock in generated program: {stuck} cnt={ {k: cnt.get(k) for _, (_, _, s) in stuck.items() for k in [s[1]]} }")


class Builder:
    def __init__(self, S, NSEQ):
        assert S % 512 == 0
        self.S = S
        self.NSEQ = NSEQ
        self.NB = S // 128
        self.NG = S // 512
        self.T = min(1024, S)
        self.NTG = S // self.T
        self.SGT = self.T // 512

    def sb(self, name, shape, dt):
        return self.es.enter_context(self.nc.sbuf_tensor(name, shape, dt))

    def build(self):
        nc = bass.Bass("TRN2", target_bir_lowering=False)
        self.nc = nc
        S, NSEQ, NB, NG = self.S, self.NSEQ, self.NB, self.NG
        dr = lambda name, shape, dt, kind="ExternalInput": nc.dram_tensor(name, shape, dt, kind=kind).ap()
        self.x_d = dr("x", [NSEQ, S, D], F32)
        self.cT_d = dr("cT", [128, KC, NSEQ], F32)
        self.pos_d = dr("pos", [128, NSEQ, NB], I32)
        self.w_ada_d = dr("w_ada", [D, 9 * D], F32)
        self.b_adaT_d = dr("b_adaT", [128, 72], F32)
        self.gains_d = dr("gains", [128, 7, KC], F32)
        self.bforget_d = dr("bforget", [128, NHB], F32)
        self.invfreq_d = dr("invfreq", [128, 8], F32)
        self.wg_d = [dr("w_ff1_gate", [D, DFF], F32), dr("w_ff2_gate", [D, DFF], F32)]
        self.wu_d = [dr("w_ff1_up", [D, DFF], F32), dr("w_ff2_up", [D, DFF], F32)]
        self.wd_d = [dr("w_ff1_down", [DFF, D], F32), dr("w_ff2_down", [DFF, D], F32)]
        self.w_in_d = dr("w_in", [D, INCOLS], F32)
        self.w_out_d = dr("w_out", [D, D], F32)
        self.c_identf_d = dr("c_identf", [128, 128], F32)
        self.c_trif_d = dr("c_trif", [128, 128], F32)
        self.c_mult_d = dr("c_mult", [128, 16 * 128], F32)
        self.c_neg_d = dr("c_neg", [128, 128], F32)
        self.out_d = dr("out", [NSEQ, S, D], F32, kind="ExternalOutput")

        with contextlib.ExitStack() as es:
            self.es = es
            self.sc = Sched(nc, es)
            self.alloc()
            self.slab_plan = []
            self.deferred = []
            for dry in (True, False):
                self.sc.dry = dry
                self.slab_count = 0
                self.slab_emitted = 0
                self.setup()
                for b in range(NSEQ):
                    self.sequence(b)
            self.sc.final_wait("sp", self.stage_b)
            self.sc.check_deadlock()
        return nc

    def alloc(self):
        nc, sc = self.nc, self.sc
        S, NSEQ, NB, T = self.S, self.NSEQ, self.NB, self.T
        sb = self.sb
        B = sc.buf
        self.xT = sb("xT", [128, KC, S], F32)
        self.xT_b = [B("xT") for _ in range(self.NG)]
        self.identb = sb("identb", [128, 128], BF16)
        self.identf = sb("identf", [128, 128], F32)
        self.onesb = sb("onesb", [128, 128], BF16)
        self.onesf = sb("onesf", [128, 128], F32)
        self.trif = sb("trif", [128, 128], F32)
        self.multm = sb("multm", [128, 16, 128], BF16)
        self.negm = sb("negm", [128, 128], BF16)
        self.const_b = B("const")
        self.cT = sb("cT_sb", [128, KC, NSEQ], F32)
        self.scT = sb("scT", [128, KC, NSEQ], BF16)
        self.sctmp = sb("sctmp", [128, KC, NSEQ], F32)
        self.modT = sb("modT", [128, 72, NSEQ], F32)
        self.b_adaT = sb("b_adaT_sb", [128, 72], F32)
        self.gains = sb("gains_sb", [128, 7, KC], F32)
        self.mods = sb("mods", [128, 9, KC, NSEQ], F32)
        self.bforget = sb("bforget_sb", [128, NHB], F32)
        self.invfreq = sb("invfreq_sb", [128, 8], F32)
        self.pos_i = sb("pos_i", [128, NSEQ, NB], I32)
        self.small_b = B("small")
        self.mod_b = B("mod")
        self.cq2 = sb("cq2", [128, NB, 16], F32)
        self.sq1 = sb("sq1", [128, NB, 8], F32)
        self.ck2 = sb("ck2", [128, NB, 16], F32)
        self.sk1 = sb("sk1", [128, NB, 8], F32)
        self.rope_b = B("rope")
        self.ropetmp_b = B("ropetmp")
        self.NSLOT = 3
        self.ring = [sb(f"ring{i}", [128, 4096], BF16) for i in range(self.NSLOT)]
        self.ring_b = [B(f"ring{i}") for i in range(self.NSLOT)]
        self.ring_i = 0
        self.wf = sb("wf", [128, KC, 8], BF16)
        self.wf_b = B("wf")
        REG = (int(self.nc.sbuf_bytes_remaining) - 256) // 256 * 256
        self.region = sb("region", [128, REG // 4], F32)
        self.REG = REG

        def carve(off, shape, dt):
            esz = 2 if dt == BF16 else 4
            n = int(np.prod(shape[1:]))
            assert off % 4 == 0 and off + n * esz <= REG, (off, shape, REG)
            ap = self.region[:, off // 4: off // 4 + (n * esz) // 4]
            if dt != F32:
                ap = ap.bitcast(dt)
            if len(shape) == 3:
                ap = ap.rearrange("p (a b) -> p a b", a=shape[1])
            elif len(shape) == 4:
                ap = ap.rearrange("p (a b c) -> p a b c", a=shape[1], b=shape[2])
            elif len(shape) == 5:
                ap = ap.rearrange("p (a b c d) -> p a b c d", a=shape[1], b=shape[2], c=shape[3])
            return ap, off + n * esz

        self.carve = carve
        SGT = self.SGT
        o = 0
        self.aT, o = carve(o, [128, FCH, T], BF16)
        self.aT_b = [B("aT") for _ in range(SGT)]
        r1 = o
        self.hT_f, o = carve(o, [128, KC, T], BF16)
        self.hTf_b = [B("hTf") for _ in range(SGT)]
        self.SQ, o = carve(o, [128, 4, 512], BF16)
        self.SQ_b = B("SQ")
        self.E = []
        for i in range(2):
            e, o = carve(o, [128, 512], F32)
            self.E.append(e)
        self.E_b = [B("E0"), B("E1")]
        self.LN, o = carve(o, [128, 512], F32)
        self.RSTD, o = carve(o, [128, 512], F32)
        self.LN_b = B("LN")
        self.HTMP = []
        for i in range(2):
            e, o = carve(o, [128, 512], F32)
            self.HTMP.append(e)
        self.HTMP_b = [B("HT0"), B("HT1")]
        self.ystage_f, o2 = carve(r1, [128, KC, T], F32)
        o = max(o, o2)
        self.R1_bufs = self.hTf_b + [self.SQ_b, self.LN_b] + self.E_b + self.HTMP_b
        self.SQ2, o = carve(o, [128, 4, 512], BF16)
        self.SQ2_b = [B("SQ2") for _ in range(4)]
        self.LN2, o = carve(o, [128, 512], F32)
        self.RSTD2, o = carve(o, [128, 512], F32)
        self.LN2_b = B("LN2")
        self.UTMP = []
        for i in range(2):
            e, o = carve(o, [128, 512], F32)
            self.UTMP.append(e)
        self.UTMP_b = [B("UT0"), B("UT1")]
        self.ffn_end = o
        o = 0
        self.merged, o = carve(o, [128, NB, D], BF16)
        self.merged_b = B("merged")
        self.hT_m, o = carve(o, [128, KC, S], BF16)
        self.hTm_b = [B("hTm") for _ in range(self.NG)]
        m_un = o
        self.QT, o = carve(o, [128, S], BF16)
        self.KT, o = carve(o, [128, S], BF16)
        self.KT1, o = carve(o, [128, S], BF16)
        self.VAs = []
        for i in range(2):
            e, o = carve(o, [128, NB, 2, 66], BF16)
            self.VAs.append(e)
        self.VA = self.VAs[0]
        self.QKtm, o = carve(o, [128, NB, 2, 128], BF16)
        self.NPT = 7
        self.PT = []
        for i in range(self.NPT):
            e, o = carve(o, [128, 512], BF16)
            self.PT.append(e)
        self.PT_b = [B("PT") for _ in range(self.NPT)]
        rotf_off = o
        self.ROTF, o = carve(o, [128, NB, 2, 2, 16], F32)
        self.RH = max(NB // 2, 1)
        self.RA, o = carve(o, [128, self.RH, 2, 16], F32)
        self.RB1, o = carve(o, [128, self.RH, 2, 8], F32)
        self.RB2, o = carve(o, [128, self.RH, 2, 8], F32)
        self.RDEN, o = carve(o, [128, 4], F32)
        m1_end = o
        self.QT_b, self.KT_b, self.QKtm_b = B("QT"), B("KT"), B("QKtm")
        self.VAs_b = [B("VA0"), B("VA1")]
        self.KT1_b = B("KT1")
        self.rtmp_b = B("rtmp")
        self.rden_b = B("rden")
        o = m_un
        self.SQm, o = carve(o, [128, 4, 512], BF16)
        self.LNm, o = carve(o, [128, 512], F32)
        self.RSTDm, o = carve(o, [128, 512], F32)
        self.HTMPm = []
        for i in range(2):
            e, o = carve(o, [128, 512], F32)
            self.HTMPm.append(e)
        self.SQm_b, self.LNm_b, self.HTMPm_b = B("SQm"), B("LNm"), [B("HTm0"), B("HTm1")]
        m0_end = o
        o = m_un
        self.ystage_m, o = carve(o, [128, KC, 512], F32)
        self.ysm_b = B("ysm")
        self.MN = []
        for i in range(2):
            e, o = carve(o, [128, D], BF16)
            self.MN.append(e)
        self.MN_b = [B("MN0"), B("MN1")]
        self.SSAB, o = carve(o, [128, NB, 2], F32)
        self.RSAB, o = carve(o, [128, NB, 2], F32)
        self.SQJ, o = carve(o, [128, 512], BF16)
        self.ss_b = B("ssab")
        self.SQ2m, o = carve(o, [128, 4, 512], BF16)
        self.SQ2m_b = [B("SQ2m") for _ in range(4)]
        self.LN2m, o = carve(o, [128, 512], F32)
        self.RSTD2m, o = carve(o, [128, 512], F32)
        self.LN2m_b = B("LN2m")
        self.UTMPm = []
        for i in range(2):
            e, o = carve(o, [128, 512], F32)
            self.UTMPm.append(e)
        self.UTMPm_b = [B("UTm0"), B("UTm1")]
        m2_end = o
        alias_f = rotf_off >= m0_end
        o = rotf_off if alias_f else max(m1_end, m0_end)
        self.FL, o = carve(o, [128, NB, 8], F32)
        self.FL2, o = carve(o, [128, NB, 8], F32)
        self.NEGF, o = carve(o, [128, NB, 8], F32)
        self.CARRY, o = carve(o, [128, NB, 8], F32)
        self.F32T, o = carve(o, [128, NB, 8], F32)
        if alias_f:
            assert o <= rotf_off + NB * 2 * 2 * 16 * 4
            o = max(m1_end, m0_end)
        self.FH, o = carve(o, [128, NB, 8], BF16)
        self.FM, o = carve(o, [128, NB, 8], BF16)
        self.FLo, o = carve(o, [128, NB, 8], BF16)
        self.F_b = B("F")
        self.aug_b = B("aug")
        self.mix_end = max(o, m2_end)
        o = max(m2_end, self.ffn_end)
        self.RSTDN, o = carve(o, [128, self.NG, 512], F32)
        self.rstdn_b = [B("rstdn") for _ in range(self.NG)]
        self.layout_info = dict(REG=REG, ffn_end=self.ffn_end, m1_end=m1_end, m0_end=m0_end, m2_end=m2_end, mix_end=self.mix_end)
        self.NSTAGE = 4
        self.stage = []
        o = 0
        for i in range(self.NSTAGE):
            e, o = carve(o, [128, D], F32)
            self.stage.append(e)
        self.stage_b = [B(f"stage{i}") for i in range(self.NSTAGE)]
        self.rt = []
        for i in range(6):
            e, o = carve(o, [128, NB, 8], F32)
            self.rt.append(e)
        self.region_b = B("region")
        self.bank = [self.es.enter_context(nc.psum_tensor(f"bank{i}", [128, 512], F32)) for i in range(8)]
        self.bank_b = [B(f"bank{i}") for i in range(8)]

    def defer(self, delay, fn):
        self.deferred.append([delay, fn])

    def tick_deferred(self, flush=False):
        keep = []
        for item in self.deferred:
            item[0] -= 1
            if flush or item[0] <= 0:
                item[1]()
            else:
                keep.append(item)
        self.deferred = keep

    def tp_view(self, i):
        return self.bank[i][:].bitcast(BF16)

    def phase_sync(self):
        sc = self.sc
        if sc.dry:
            return
        snap = {("eng", n): E.ticks for n, E in sc.eng.items() if E.ticks > 0}
        for k, v in self.region_b.w.items():
            if k[0] == "dma":
                snap[k] = max(snap.get(k, 0), v)
        for n, E in sc.eng.items():
            deps = dict(snap)
            deps.pop(("eng", n), None)
            sc._emit_waits(E, deps)

    def slab(self, parts, keep=0):
        idx = self.slab_count
        self.slab_count += 1
        if self.sc.dry:
            self.slab_plan.append(parts)
            return idx % self.NSLOT
        plan = self.slab_plan
        while self.slab_emitted <= min(idx + self.NSLOT - 1 - keep, len(plan) - 1):
            i = self.slab_emitted
            si = i % self.NSLOT
            for dst_fn, srcap in plan[i]:
                self.sc.dma("pool", dst_fn(self.ring[si]), srcap, writes=[self.ring_b[si]])
            self.slab_emitted += 1
        return idx % self.NSLOT

    def setup(self):
        nc, sc = self.nc, self.sc
        NSEQ = self.NSEQ
        small = self.small_b
        for dst, src in ((self.cT, self.cT_d), (self.b_adaT, self.b_adaT_d), (self.gains, self.gains_d),
                         (self.bforget, self.bforget_d), (self.invfreq, self.invfreq_d),
                         (self.pos_i, self.pos_d), (self.identf, self.c_identf_d), (self.trif, self.c_trif_d)):
            sc.dma("sp", dst[:], src, writes=[small])
        cb = self.const_b
        sc.dma("pool", self.identb[:], self.c_identf_d, writes=[cb])
        sc.dma("pool", self.multm[:].rearrange("p a b -> p (a b)"), self.c_mult_d, writes=[cb])
        sc.dma("pool", self.negm[:], self.c_neg_d, writes=[cb])
        sc.op("dve", lambda e: e.memset(self.onesb[:], 1.0), writes=[cb])
        sc.op("dve", lambda e: e.memset(self.onesf[:], 1.0), writes=[cb])
        mb = self.mod_b
        sc.op("act", lambda e: e.activation(self.sctmp[:], self.cT[:], AF.Exp, scale=-1.0), reads=[small], writes=[mb])
        sc.op("dve", lambda e: e.tensor_scalar(out=self.sctmp[:], in0=self.sctmp[:], scalar1=1.0, scalar2=None, op0=ALU.add), writes=[mb])
        sc.op("dve", lambda e: e.reciprocal(out=self.sctmp[:], in_=self.sctmp[:]), writes=[mb])
        sc.op("dve", lambda e: e.tensor_tensor(out=self.scT[:], in0=self.cT[:], in1=self.sctmp[:], op=ALU.mult), reads=[small], writes=[mb])
        wv = self.w_ada_d.rearrange("(k p) f -> p k f", p=128)
        for sl in range(18):
            si = self.slab([(lambda r: r[:].rearrange("p (k f) -> p k f", k=KC), wv[:, :, sl * 512:(sl + 1) * 512])])
            slot = self.ring[si][:].rearrange("p (k f) -> p k f", k=KC)
            bk = sl % 2
            ps = self.bank[bk]
            for sub in range(4):
                for k in range(KC):
                    sc.op("pe", lambda e, k=k, sub=sub: e.matmul(ps[:, sub * NSEQ:(sub + 1) * NSEQ],
                                                                   slot[:, k, sub * 128:(sub + 1) * 128],
                                                                   self.scT[:, k, :], start=(k == 0), stop=(k == KC - 1)),
                          reads=[self.ring_b[si], mb], writes=[self.bank_b[bk]], signal=(k == KC - 1))
            sc.op("dve", lambda e, sl=sl, ps=ps: e.tensor_tensor(
                out=self.modT[:, sl * 4:(sl + 1) * 4, :],
                in0=ps[:, 0:4 * NSEQ].rearrange("p (a b) -> p a b", a=4),
                in1=self.b_adaT[:, sl * 4:(sl + 1) * 4].unsqueeze(2).to_broadcast([128, 4, NSEQ]), op=ALU.add),
                reads=[small], writes=[self.bank_b[bk], mb])
        gi_pre = (0, 2, 4)
        gi_post = (1, 3, 5)
        for s in range(3):
            m_shift = self.modT[:, (3 * s) * 8:(3 * s + 1) * 8, :]
            m_scale = self.modT[:, (3 * s + 1) * 8:(3 * s + 2) * 8, :]
            m_gate = self.modT[:, (3 * s + 2) * 8:(3 * s + 3) * 8, :]
            gpre = self.gains[:, gi_pre[s], :].unsqueeze(2).to_broadcast([128, KC, NSEQ])
            gpost = self.gains[:, gi_post[s], :].unsqueeze(2).to_broadcast([128, KC, NSEQ])
            sc.op("dve", lambda e, m_scale=m_scale, gpre=gpre, s=s: e.scalar_tensor_tensor(
                out=self.mods[:, 3 * s + 0, :, :], in0=m_scale, scalar=1.0, in1=gpre, op0=ALU.add, op1=ALU.mult),
                reads=[small], writes=[mb])
            sc.op("dve", lambda e, m_shift=m_shift, s=s: e.tensor_copy(self.mods[:, 3 * s + 1, :, :], m_shift), writes=[mb])
            sc.op("dve", lambda e, m_gate=m_gate, gpost=gpost, s=s: e.scalar_tensor_tensor(
                out=self.mods[:, 3 * s + 2, :, :], in0=m_gate, scalar=(0.5 if s != 1 else 1.0), in1=gpost,
                op0=ALU.mult, op1=ALU.mult), reads=[small], writes=[mb])

    def sequence(self, b):
        self.phase_sync()
        self.load_x(b)
        self.rope_tables(b)
        self.phase_sync()
        self.ffn(b, 0, 0)
        self.phase_sync()
        self.mixer(b)
        self.phase_sync()
        self.ffn(b, 2, 1)
        self.phase_sync()
        self.store_x(b)

    def load_x(self, b):
        sc = self.sc
        for blk in range(self.NB):
            st = blk % self.NSTAGE
            sc.dma("sp", self.stage[st][:], self.x_d[b, blk * 128:(blk + 1) * 128, :], writes=[self.stage_b[st]])
            for half in range(2):
                bk = 2 * (blk % 2) + half
                for cc in range(4):
                    c = half * 4 + cc
                    sc.op("pe", lambda e, c=c, cc=cc, st=st, bk=bk: e.transpose(
                        self.bank[bk][:, cc * 128:(cc + 1) * 128], self.stage[st][:, c * 128:(c + 1) * 128], self.identf[:]),
                        reads=[self.stage_b[st], self.small_b], writes=[self.bank_b[bk]], signal=(cc == 3))
                src = self.bank[bk][:].rearrange("p (a b) -> p a b", a=4)
                dst = self.xT[:, half * 4:(half + 1) * 4, blk * 128:(blk + 1) * 128]
                if half == 0:
                    sc.op("act", lambda e, dst=dst, src=src: e.copy(dst, src), writes=[self.bank_b[bk], self.xT_b[blk // 4]])
                else:
                    sc.op("dve", lambda e, dst=dst, src=src: e.tensor_copy(dst, src), writes=[self.bank_b[bk], self.xT_b[blk // 4]])
            if blk % 4 == 3:
                self.stats_ahead(blk // 4, self.SQ2, self.SQ2_b, self.LN2, self.LN2_b, ssbank=4)

    def store_x(self, b):
        sc = self.sc
        for blk in range(self.NB):
            st = blk % self.NSTAGE
            for half in range(2):
                bk = 2 * (blk % 2) + half
                for cc in range(4):
                    c = half * 4 + cc
                    sc.op("pe", lambda e, c=c, cc=cc, bk=bk, blk=blk: e.transpose(
                        self.bank[bk][:, cc * 128:(cc + 1) * 128], self.xT[:, c, blk * 128:(blk + 1) * 128], self.identf[:]),
                        reads=[self.xT_b[blk // 4], self.small_b], writes=[self.bank_b[bk]], signal=(cc == 3))
                dst = self.stage[st][:, half * 512:(half + 1) * 512]
                src = self.bank[bk][:]
                if half == 0:
                    sc.op("act", lambda e, dst=dst, src=src: e.copy(dst, src), writes=[self.bank_b[bk], self.stage_b[st]])
                else:
                    sc.op("dve", lambda e, dst=dst, src=src: e.tensor_copy(dst, src), writes=[self.bank_b[bk], self.stage_b[st]])
            sc.dma("sp", self.out_d[b, blk * 128:(blk + 1) * 128, :], self.stage[st][:], reads=[self.stage_b[st]])
        for st in range(self.NSTAGE):
            for k, v in list(self.stage_b[st].r.items()) + list(self.stage_b[st].w.items()):
                self.region_b.w[k] = max(self.region_b.w.get(k, 0), v)

    def rope_tables(self, b):
        sc = self.sc
        NB = self.NB
        rb, tb = self.rope_b, self.ropetmp_b
        posf, ang, t, k, r1, r2 = self.rt
        V = lambda fn, reads=(), writes=(): sc.op("dve", fn, reads=reads, writes=writes)
        V(lambda e: e.tensor_copy(posf[:], self.pos_i[:, b, :].unsqueeze(2).to_broadcast([128, NB, 8])),
          reads=[self.small_b], writes=[tb])
        V(lambda e: e.tensor_tensor(out=ang[:], in0=posf[:], in1=self.invfreq[:].unsqueeze(1).to_broadcast([128, NB, 8]),
                                    op=ALU.mult), reads=[self.small_b], writes=[tb])
        for which in ("sin", "cos"):
            src = ang
            if which == "cos":
                V(lambda e: e.tensor_scalar(out=posf[:], in0=ang[:], scalar1=math.pi / 2, scalar2=None, op0=ALU.add), writes=[tb])
                src = posf
            V(lambda e, src=src: e.tensor_scalar(out=t[:], in0=src[:], scalar1=1.0 / (2 * math.pi), scalar2=MAGIC,
                                                 op0=ALU.mult, op1=ALU.add), writes=[tb])
            V(lambda e: e.tensor_scalar(out=k[:], in0=t[:], scalar1=MAGIC, scalar2=None, op0=ALU.subtract), writes=[tb])
            V(lambda e, src=src: e.scalar_tensor_tensor(out=r1[:], in0=k[:], scalar=-TWO_PI_HI, in1=src[:],
                                                        op0=ALU.mult, op1=ALU.add), writes=[tb])
            V(lambda e: e.scalar_tensor_tensor(out=r2[:], in0=k[:], scalar=-TWO_PI_LO, in1=r1[:],
                                               op0=ALU.mult, op1=ALU.add), writes=[tb])
            V(lambda e: e.tensor_scalar(out=r2[:], in0=r2[:], scalar1=PI_SAFE, scalar2=-PI_SAFE, op0=ALU.min, op1=ALU.max),
              writes=[tb])
            if which == "sin":
                sc.op("act", lambda e: e.activation(self.sk1[:], r2[:], AF.Sin), reads=[tb], writes=[rb])
                sc.op("dve", lambda e: e.tensor_scalar(out=self.sq1[:], in0=self.sk1[:], scalar1=0.125, scalar2=None,
                                                       op0=ALU.mult), writes=[rb])
            else:
                for hf in range(2):
                    sc.op("act", lambda e, hf=hf: e.activation(self.ck2[:, :, hf * 8:(hf + 1) * 8], r2[:], AF.Sin),
                          reads=[tb], writes=[rb])
                sc.op("dve", lambda e: e.tensor_scalar(out=self.cq2[:], in0=self.ck2[:], scalar1=0.125, scalar2=None,
                                                       op0=ALU.mult), writes=[rb])

    def stats_ahead(self, sg, SQ, SQ_b, LN, LN_b, ssbank):
        sc = self.sc
        t0 = sg * 512
        xb = self.xT_b[sg]
        bb = self.bank_b[ssbank]
        ps = self.bank[ssbank]
        for half in range(2):
            for cc in range(4):
                c = half * 4 + cc
                if cc % 2 == 0:
                    sc.op("act", lambda e, c=c, cc=cc: e.activation(SQ[:, cc, :], self.xT[:, c, t0:t0 + 512], AF.Square),
                          reads=[xb], writes=[SQ_b[cc]])
                else:
                    sc.op("dve", lambda e, c=c, cc=cc: e.tensor_tensor(out=SQ[:, cc, :], in0=self.xT[:, c, t0:t0 + 512],
                                                                       in1=self.xT[:, c, t0:t0 + 512], op=ALU.mult),
                          reads=[xb], writes=[SQ_b[cc]])
            for cc in range(4):
                c = half * 4 + cc
                sc.op("pe", lambda e, c=c, cc=cc: e.matmul(ps[:], self.onesb[:], SQ[:, cc, :], start=(c == 0), stop=(c == KC - 1)),
                      reads=[SQ_b[cc], self.const_b], writes=[bb], signal=(cc == 3))
        sc.op("act", lambda e: e.activation(LN[:], ps[:], AF.Ln, bias=EPS, scale=1.0 / D), writes=[bb, LN_b])
        sc.op("act", lambda e: e.activation(self.RSTDN[:, sg, :], LN[:], AF.Exp, scale=-0.5), reads=[LN_b], writes=[self.rstdn_b[sg]])

    def prenorm_apply(self, b, s, sg, hT_dst, hT_buf, HTMP, HTMP_b):
        sc = self.sc
        t0 = sg * 512
        xb = self.xT_b[sg]
        G = self.mods[:, 3 * s + 0, :, :]
        Sh = self.mods[:, 3 * s + 1, :, :]
        for c in range(KC):
            tmp = HTMP[c % 2]
            tb = HTMP_b[c % 2]
            sc.op("dve", lambda e, c=c, tmp=tmp: e.scalar_tensor_tensor(
                out=tmp[:], in0=self.xT[:, c, t0:t0 + 512], scalar=G[:, c, b:b + 1], in1=self.RSTDN[:, sg, :],
                op0=ALU.mult, op1=ALU.mult), reads=[xb, self.mod_b, self.rstdn_b[sg]], writes=[tb])
            sc.op("act", lambda e, c=c, tmp=tmp: e.activation(hT_dst[:, c, :], tmp[:], AF.Identity,
                                                               bias=Sh[:, c, b:b + 1], scale=1.0),
                  reads=[self.mod_b, tb], writes=[hT_buf])

    def postnorm_update(self, b, s, sg, ystage, ybufs, LN2, RSTD2, LN2_b, UTMP, UTMP_b, ssbank):
        sc = self.sc
        t0 = sg * 512
        xb = self.xT_b[sg]
        bb = self.bank_b[ssbank]
        ps = self.bank[ssbank]
        sc.op("act", lambda e: e.activation(LN2[:], ps[:], AF.Ln, bias=EPS, scale=1.0 / D), writes=[bb, LN2_b])
        sc.op("act", lambda e: e.activation(RSTD2[:], LN2[:], AF.Exp, scale=-0.5), writes=[LN2_b])
        Gp = self.mods[:, 3 * s + 2, :, :]
        for c in range(KC):
            tmp = UTMP[c % 2]
            tb = UTMP_b[c % 2]
            sc.op("dve", lambda e, c=c, tmp=tmp: e.tensor_tensor(out=tmp[:], in0=ystage[:, c, :], in1=RSTD2[:], op=ALU.mult),
                  reads=list(ybufs) + [LN2_b], writes=[tb])
            sc.op("dve", lambda e, c=c, tmp=tmp: e.scalar_tensor_tensor(
                out=self.xT[:, c, t0:t0 + 512], in0=tmp[:], scalar=Gp[:, c, b:b + 1], in1=self.xT[:, c, t0:t0 + 512],
                op0=ALU.mult, op1=ALU.add), reads=[self.mod_b, tb], writes=[xb])

    def ffn(self, b, s, wi):
        sc = self.sc
        T, SGT = self.T, self.SGT
        wgv = self.wg_d[wi].rearrange("(k p) f -> p k f", p=128)
        wuv = self.wu_d[wi].rearrange("(k p) f -> p k f", p=128)
        wd = self.wd_d[wi]
        R1 = self.R1_bufs
        for tg in range(self.NTG):
            sg0 = tg * SGT
            for sgl in range(SGT):
                self.prenorm_apply(b, s, sg0 + sgl, self.hT_f[:, :, sgl * 512:(sgl + 1) * 512], self.hTf_b[sgl],
                                   self.HTMP, self.HTMP_b)
            nslab = (DFF + 255) // 256
            gi = 0
            for sl in range(nslab):
                c0 = sl * 256
                ncol = min(256, DFF - c0)
                si = self.slab([
                    (lambda r, ncol=ncol: r[:, 0:2048].rearrange("p (k f) -> p k f", k=KC)[:, :, 0:ncol], wgv[:, :, c0:c0 + ncol]),
                    (lambda r, ncol=ncol: r[:, 2048:4096].rearrange("p (k f) -> p k f", k=KC)[:, :, 0:ncol], wuv[:, :, c0:c0 + ncol])])
                rb = self.ring_b[si]
                slotg = self.ring[si][:, 0:2048].rearrange("p (k f) -> p k f", k=KC)
                slotu = self.ring[si][:, 2048:4096].rearrange("p (k f) -> p k f", k=KC)
                nfc = (ncol + 127) // 128
                for sgl in range(SGT):
                    hb = self.hTf_b[sgl]
                    for fcl in range(nfc):
                        fc = sl * 2 + fcl
                        fr = min(128, ncol - fcl * 128)
                        gb, ub = (gi % 2), 2 + (gi % 2)
                        E = self.E[gi % 2]
                        eb = self.E_b[gi % 2]
                        gi += 1
                        gps, ups = self.bank[gb], self.bank[ub]
                        for k in range(KC):
                            sc.op("pe", lambda e, k=k, fcl=fcl, fr=fr, gps=gps, slotg=slotg, sgl=sgl: e.matmul(
                                gps[0:fr, :], slotg[:, k, fcl * 128:fcl * 128 + fr], self.hT_f[:, k, sgl * 512:(sgl + 1) * 512],
                                start=(k == 0), stop=(k == KC - 1)),
                                reads=[rb, hb], writes=[self.bank_b[gb]], signal=(k == KC - 1))
                        for k in range(KC):
                            sc.op("pe", lambda e, k=k, fcl=fcl, fr=fr, ups=ups, slotu=slotu, sgl=sgl: e.matmul(
                                ups[0:fr, :], slotu[:, k, fcl * 128:fcl * 128 + fr], self.hT_f[:, k, sgl * 512:(sgl + 1) * 512],
                                start=(k == 0), stop=(k == KC - 1)),
                                reads=[rb, hb], writes=[self.bank_b[ub]], signal=(k == KC - 1))
                        sc.op("act", lambda e, E=E, gps=gps, fr=fr: e.activation(E[0:fr, :], gps[0:fr, :], AF.Silu),
                              writes=[self.bank_b[gb], eb])
                        sc.op("dve", lambda e, E=E, ups=ups, fr=fr, fc=fc, sgl=sgl: e.tensor_tensor(
                            out=self.aT[0:fr, fc, sgl * 512:(sgl + 1) * 512], in0=ups[0:fr, :], in1=E[0:fr, :], op=ALU.mult),
                            reads=[eb], writes=[self.bank_b[ub], self.aT_b[sgl]])
            for c in range(KC):
                si = self.slab([
                    (lambda r: r[:, 0:FCH * 128].rearrange("p (j d) -> p j d", j=FCH)[:, 0:21, :],
                     wd[0:21 * 128, c * 128:(c + 1) * 128].rearrange("(j p) d -> p j d", p=128)),
                    (lambda r: r[:, 0:FCH * 128].rearrange("p (j d) -> p j d", j=FCH)[0:64, 21, :],
                     wd[21 * 128:DFF, c * 128:(c + 1) * 128])])
                rb = self.ring_b[si]
                slot = self.ring[si][:, 0:FCH * 128].rearrange("p (j d) -> p j d", j=FCH)
                for sgl in range(SGT):
                    yb = 4 + ((c * SGT + sgl) % 2)
                    yps = self.bank[yb]
                    for j in range(FCH):
                        fr = 128 if j < 21 else 64
                        sc.op("pe", lambda e, j=j, fr=fr, yps=yps, slot=slot, sgl=sgl: e.matmul(
                            yps[:], slot[0:fr, j, :], self.aT[0:fr, j, sgl * 512:(sgl + 1) * 512],
                            start=(j == 0), stop=(j == FCH - 1)),
                            reads=[rb, self.aT_b[sgl]], writes=[self.bank_b[yb]], signal=(j == FCH - 1))
                    ssb = 6 + sgl
                    sqi = (c * SGT + sgl) % 4
                    sq = self.SQ2[:, sqi, :]
                    sc.op("act", lambda e, c=c, sgl=sgl, yps=yps: e.copy(self.ystage_f[:, c, sgl * 512:(sgl + 1) * 512], yps[:]),
                          writes=[self.bank_b[yb]] + R1)
                    sc.op("act", lambda e, sq=sq, yps=yps: e.activation(sq, yps[:], AF.Square),
                          writes=[self.bank_b[yb], self.SQ2_b[sqi]])
                    self.tick_deferred()
                    self.defer(1, lambda c=c, ssb=ssb, sq=sq, sqi=sqi: sc.op(
                        "pe", lambda e: e.matmul(self.bank[ssb][:], self.onesb[:], sq, start=(c == 0), stop=(c == KC - 1)),
                        reads=[self.SQ2_b[sqi], self.const_b], writes=[self.bank_b[ssb]], signal=True))
            self.tick_deferred(flush=True)
            for sgl in range(SGT):
                self.postnorm_update(b, s, sg0 + sgl, self.ystage_f[:, :, sgl * 512:(sgl + 1) * 512], R1,
                                     self.LN2, self.RSTD2, self.LN2_b, self.UTMP, self.UTMP_b, ssbank=6 + sgl)
            if s == 0:
                for sgl in range(SGT):
                    self.stats_ahead(sg0 + sgl, self.SQ2, self.SQ2_b, self.LN2, self.LN2_b, ssbank=6 + sgl)

    def mixer(self, b):
        sc = self.sc
        S, NB, NG = self.S, self.NB, self.NG
        s = 1
        for sg in range(NG):
            self.prenorm_apply(b, s, sg, self.hT_m[:, :, sg * 512:(sg + 1) * 512], self.hTm_b[sg], self.HTMPm, self.HTMPm_b)
        self.fox_prep(b)
        self.phase_sync()
        for vi in range(2):
            sc.op("dve", lambda e, vi=vi: e.memset(self.VAs[vi][:, :, :, 64:65], 1.0), writes=[self.VAs_b[vi]])
        sc.op("dve", lambda e: e.memset(self.KT[64:128, :], 0.0), writes=[self.KT_b])
        sc.op("dve", lambda e: e.memset(self.KT1[0:64, :], 0.0), writes=[self.KT1_b])
        winv = self.w_in_d.rearrange("(k p) f -> p k f", p=128)
        NU = 4 + NHB

        def inproj_gen(u):
            isA = u < 4
            vi = u % 2
            if isA:
                ncol = 384
                parts = [(lambda r, i=i: r[:, 0:KC * 384].rearrange("p (k f) -> p k f", k=KC)[:, :, i * 128:(i + 1) * 128],
                          winv[:, :, base + u * 128: base + (u + 1) * 128]) for i, base in enumerate((0, 512, 1024))]
            else:
                h = u - 4
                ncol = 192
                parts = [(lambda r, i=i: r[:, 0:KC * 192].rearrange("p (k f) -> p k f", k=KC)[:, :, i * 64:(i + 1) * 64],
                          winv[:, :, base + h * 64: base + (h + 1) * 64]) for i, base in enumerate((1536, 2048, 2560))]
            si = self.slab(parts)
            rb = self.ring_b[si]
            slot = self.ring[si][:, 0:KC * ncol].rearrange("p (k f) -> p k f", k=KC)
            for blk in range(NB):
                bk = blk % 2
                ps = self.bank[bk]
                for k in range(KC):
                    sc.op("pe", lambda e, k=k: e.matmul(
                        ps[:, 0:ncol], self.hT_m[:, k, blk * 128:(blk + 1) * 128], slot[:, k, 0:ncol],
                        start=(k == 0), stop=(k == KC - 1)),
                        reads=[rb, self.hTm_b[blk // 4]], writes=[self.bank_b[bk]], signal=(k == KC - 1))
                if isA:
                    self.defer(2, lambda blk=blk, ps=ps, bk=bk: self.evac_A(blk, ps, self.bank_b[bk], vi))
                else:
                    self.defer(2, lambda blk=blk, ps=ps, bk=bk: self.evac_B(blk, ps, self.bank_b[bk], u - 4, vi))
                yield blk

        def stageB(u):
            self.tick_deferred(flush=True)
            isA = u < 4
            if u == 4:
                sc.op("dve", lambda e: e.memset(self.QT[64:128, :], 0.0), writes=[self.QT_b])
            self.unit_transposes(isA, u)

        def att_gen(u):
            vi = u % 2
            if u < 4:
                for h2 in range(2):
                    yield from self.attention(True, hp0=h2 * 64, K=64, vsel=h2, col0=(u * 2 + h2) * 64, vi=vi)
            else:
                yield from self.attention(False, hp0=0, K=70, vsel=0, col0=512 + (u - 4) * 64, vi=vi)

        for _ in inproj_gen(0):
            self.tick_deferred()
        self.tick_deferred(flush=True)
        for p in self.rope_A_pieces():
            p()
        stageB(0)
        for u in range(NU):
            nxt = inproj_gen(u + 1) if u + 1 < NU else None
            nsteps = (2 if u < 4 else 1) * (sum(4 * g + 4 for g in range(NG)) + 5)
            stride = max(2, (nsteps * 55 // 100) // NB) if (u + 1 < 4) else max(2, nsteps // (NB + 1))
            for i, _ in enumerate(att_gen(u)):
                if nxt is not None and i % stride == stride - 1:
                    if next(nxt, None) is None:
                        nxt = None
                        if u + 1 < 4:
                            for pi, p in enumerate(self.rope_A_pieces()):
                                self.defer(4 + 3 * pi, p)
            if nxt is not None:
                for _ in nxt:
                    self.tick_deferred()
                if u + 1 < 4:
                    self.tick_deferred(flush=True)
                    for p in self.rope_A_pieces():
                        p()
            if u + 1 < NU:
                stageB(u + 1)
        self.tick_deferred(flush=True)
        self.phase_sync()
        self.outproj(b)

    def fox_prep(self, b):
        sc = self.sc
        NB = self.NB
        Fb = self.F_b
        winv = self.w_in_d.rearrange("(k p) f -> p k f", p=128)
        sc.dma("pool", self.wf[:], winv[:, :, 3072:3080], writes=[self.wf_b])
        ps = self.bank[2]
        for blk in range(NB):
            for k in range(KC):
                sc.op("pe", lambda e, k=k, blk=blk: e.matmul(ps[:, blk * 8:(blk + 1) * 8], self.hT_m[:, k, blk * 128:(blk + 1) * 128],
                                                            self.wf[:, k, :], start=(k == 0), stop=(k == KC - 1)),
                      reads=[self.wf_b, self.hTm_b[blk // 4]], writes=[self.bank_b[2]], signal=(k == KC - 1))
        psv = ps[:, 0:NB * 8].rearrange("p (a b) -> p a b", a=NB)
        sc.op("dve", lambda e: e.tensor_tensor(out=self.FL[:], in0=psv, in1=self.bforget[:].unsqueeze(1).to_broadcast([128, NB, 8]),
                                               op=ALU.add), reads=[self.small_b], writes=[self.bank_b[2], Fb] + self.rstdn_b)
        sc.op("act", lambda e: e.activation(self.FL2[:], self.FL[:], AF.Exp, scale=-1.0), writes=[Fb])
        sc.op("act", lambda e: e.activation(self.FL[:], self.FL2[:], AF.Ln, bias=1.0, scale=1.0), writes=[Fb])
        wps, tps = self.bank[3], self.bank[4]
        flat = self.FL[:].rearrange("p a b -> p (a b)")
        sc.op("pe", lambda e: e.matmul(wps[:, 0:NB * 8], self.trif[:], flat, start=True, stop=True),
              reads=[Fb, self.small_b], writes=[self.bank_b[3]])
        sc.op("pe", lambda e: e.matmul(tps[:, 0:NB * 8], self.onesf[:], flat, start=True, stop=True),
              reads=[Fb, self.const_b], writes=[self.bank_b[4]])
        sc.op("dve", lambda e: e.tensor_copy(self.FL2[:], tps[:, 0:NB * 8].rearrange("p (a b) -> p a b", a=NB)),
              writes=[self.bank_b[4], Fb])
        sc.op("dve", lambda e: e.memset(self.CARRY[:, 0, :], 0.0), writes=[Fb])
        for i in range(1, NB):
            sc.op("dve", lambda e, i=i: e.tensor_tensor(out=self.CARRY[:, i, :], in0=self.CARRY[:, i - 1, :],
                                                        in1=self.FL2[:, i - 1, :], op=ALU.add), writes=[Fb])
        sc.op("dve", lambda e: e.tensor_tensor(out=self.NEGF[:], in0=wps[:, 0:NB * 8].rearrange("p (a b) -> p a b", a=NB),
                                               in1=self.CARRY[:], op=ALU.add), writes=[self.bank_b[3], Fb])
        V = lambda fn: sc.op("dve", fn, writes=[Fb])
        V(lambda e: e.tensor_copy(self.FH[:], self.NEGF[:]))
        V(lambda e: e.tensor_copy(self.F32T[:], self.FH[:]))
        V(lambda e: e.tensor_tensor(out=self.NEGF[:], in0=self.NEGF[:], in1=self.F32T[:], op=ALU.subtract))
        V(lambda e: e.tensor_copy(self.FM[:], self.NEGF[:]))
        V(lambda e: e.tensor_copy(self.F32T[:], self.FM[:]))
        V(lambda e: e.tensor_tensor(out=self.NEGF[:], in0=self.NEGF[:], in1=self.F32T[:], op=ALU.subtract))
        V(lambda e: e.tensor_copy(self.FLo[:], self.NEGF[:]))

    def evac_A(self, blk, ps, bb, vi):
        sc = self.sc
        qk = self.QKtm_b
        VA, VAb = self.VAs[vi], self.VAs_b[vi]
        sc.op("dve", lambda e: e.tensor_copy(VA[:, blk, :, 0:64], ps[:, 256:384].rearrange("p (a b) -> p a b", a=2)),
              writes=[bb, VAb])
        sc.op("dve", lambda e: e.tensor_copy(
            self.ROTF[:, blk, :, :, :], ps[:, 0:256].rearrange("p (q h d) -> p q h d", q=2, h=2)[:, :, :, 0:16]),
            writes=[bb, self.rtmp_b])
        sc.op("act", lambda e: e.activation(self.QKtm[:, blk, 0, :], ps[:, 0:128], AF.Copy, scale=0.125), writes=[bb, qk])
        sc.op("act", lambda e: e.copy(self.QKtm[:, blk, 1, :], ps[:, 128:256]), writes=[bb, qk])

    def rope_A_pieces(self):
        sc = self.sc
        NB, RH = self.NB, self.RH
        rt = self.rtmp_b
        qk = self.QKtm_b
        pieces = []
        for qi, (c2, s1) in enumerate(((self.cq2, self.sq1), (self.ck2, self.sk1))):
            for n0 in range(0, NB, RH):
                def piece(qi=qi, c2=c2, s1=s1, n0=n0):
                    X = self.ROTF[:, n0:n0 + RH, qi, :, :]
                    Dv = self.QKtm[:, n0:n0 + RH, qi, :].rearrange("p n (h d) -> p n h d", h=2)
                    cosb = c2[:, n0:n0 + RH, :].unsqueeze(2).to_broadcast([128, RH, 2, 16])
                    sinb = s1[:, n0:n0 + RH, :].unsqueeze(2).to_broadcast([128, RH, 2, 8])
                    sc.op("dve", lambda e: e.tensor_tensor(out=self.RA[:], in0=X, in1=cosb, op=ALU.mult),
                          reads=[self.rope_b], writes=[rt])
                    sc.op("dve", lambda e: e.tensor_tensor(out=self.RB1[:], in0=X[:, :, :, 8:16], in1=sinb, op=ALU.mult),
                          reads=[self.rope_b], writes=[rt])
                    sc.op("dve", lambda e: e.tensor_tensor(out=self.RB2[:], in0=X[:, :, :, 0:8], in1=sinb, op=ALU.mult),
                          reads=[self.rope_b], writes=[rt])
                    sc.op("dve", lambda e: e.tensor_tensor(out=Dv[:, :, :, 0:8], in0=self.RA[:, :, :, 0:8], in1=self.RB1[:],
                                                           op=ALU.subtract), reads=[rt], writes=[qk])
                    sc.op("dve", lambda e: e.tensor_tensor(out=Dv[:, :, :, 8:16], in0=self.RA[:, :, :, 8:16], in1=self.RB2[:],
                                                           op=ALU.add), reads=[rt], writes=[qk])
                pieces.append(piece)
        return pieces

    def evac_B(self, blk, ps, bb, h, vi):
        sc = self.sc
        qk = self.QKtm_b
        VA, VAb = self.VAs[vi], self.VAs_b[vi]
        sc.op("dve", lambda e: e.tensor_copy(VA[:, blk, 0, 0:64], ps[:, 128:192]), writes=[bb, VAb])
        sc.op("act", lambda e: e.activation(self.QKtm[:, blk, 0, 0:64], ps[:, 0:64], AF.Copy, scale=0.125), writes=[bb, qk])
        sc.op("dve", lambda e: e.tensor_copy(self.QKtm[:, blk, 1, 0:64], ps[:, 64:128]), writes=[bb, qk])
        if blk == self.NB - 1:
            Fb = self.F_b
            sc.op("dve", lambda e: e.memset(self.QKtm[:, :, 0, 67:70], 1.0), writes=[qk])
            sc.op("dve", lambda e: e.memset(self.QKtm[:, :, 1, 64:67], 1.0), writes=[qk])
            for i, src_t in enumerate((self.FH, self.FM, self.FLo)):
                sc.op("dve", lambda e, i=i, src_t=src_t: e.tensor_scalar(out=self.QKtm[:, :, 0, 64 + i], in0=src_t[:, :, h], scalar1=-1.0,
                                                                       scalar2=None, op0=ALU.mult), reads=[Fb], writes=[qk])
                sc.op("dve", lambda e, i=i, src_t=src_t: e.tensor_copy(self.QKtm[:, :, 1, 67 + i], src_t[:, :, h]), reads=[Fb], writes=[qk])

    def unit_transposes(self, isA, u):
        sc = self.sc
        NB = self.NB
        ncol = 128 if isA else 70
        for g4 in range(NB // 4):
            qb, kb = (6, 7) if g4 % 2 == 0 else (2, 3)
            tq = self.tp_view(qb)
            tk = self.tp_view(kb)
            for qi, (tp, tb) in enumerate(((tq, qb), (tk, kb))):
                for bl in range(4):
                    blk = g4 * 4 + bl
                    sc.op("pe", lambda e, qi=qi, bl=bl, blk=blk, tp=tp: e.transpose(
                        tp[0:ncol, bl * 128:(bl + 1) * 128], self.QKtm[:, blk, qi, 0:ncol], self.identb[:]),
                        reads=[self.QKtm_b, self.const_b], writes=[self.bank_b[tb]], signal=(bl == 3))
            sc.op("act", lambda e, tq=tq, g4=g4: e.copy(self.QT[0:ncol, g4 * 512:(g4 + 1) * 512], tq[0:ncol, 0:512]),
                  writes=[self.bank_b[qb], self.QT_b])
            if isA:
                sc.op("dve", lambda e, tk=tk, g4=g4: e.tensor_copy(self.KT[0:64, g4 * 512:(g4 + 1) * 512], tk[0:64, 0:512]),
                      writes=[self.bank_b[kb], self.KT_b])
                sc.op("dve", lambda e, tk=tk, g4=g4: e.tensor_copy(self.KT1[64:128, g4 * 512:(g4 + 1) * 512], tk[64:128, 0:512]),
                      writes=[self.bank_b[kb], self.KT1_b])
            else:
                sc.op("dve", lambda e, tk=tk, g4=g4: e.tensor_copy(self.KT[0:ncol, g4 * 512:(g4 + 1) * 512], tk[0:ncol, 0:512]),
                      writes=[self.bank_b[kb], self.KT_b])

    def attention(self, isA, hp0, K, vsel, col0, vi):
        KTt, KTb = (self.KT1, self.KT1_b) if (isA and hp0 == 64) else (self.KT, self.KT_b)
        VA, VAb = self.VAs[vi], self.VAs_b[vi]
        SB = (2, 3, 6, 7)
        sc = self.sc
        NG = self.NG
        LAG = 5
        NPT = self.NPT
        tiles = [(g, j) for g in range(NG) for j in range(4 * g + 4)]
        n = len(tiles)

        def emit_qk(t):
            g, j = tiles[t]
            c0 = max(j - 4 * g, 0)
            sbk = SB[t % 4]
            sps = self.bank[sbk]
            PT = self.PT[t % NPT]
            ptb = self.PT_b[t % NPT]
            diag = (not isA) and (j >= 4 * g)
            sc.op("pe", lambda e: e.matmul(
                sps[:, c0 * 128:512], KTt[:, j * 128:(j + 1) * 128],
                self.QT[:, g * 512 + c0 * 128:(g + 1) * 512], start=True, stop=(not diag)),
                reads=[self.QT_b, KTb], writes=[self.bank_b[sbk]], signal=(not diag))
            if diag:
                sc.op("pe", lambda e: e.matmul(
                    sps[:, c0 * 128:(c0 + 1) * 128], self.identb[:], self.negm[:], start=False, stop=True),
                    reads=[self.const_b], writes=[self.bank_b[sbk]], signal=True)
            sc.op("act", lambda e: e.activation(PT[:, c0 * 128:512], sps[:, c0 * 128:512], AF.Exp),
                  writes=[self.bank_b[sbk], ptb])
            if isA:
                d0 = 4 * g + c0 - j
                nblk = 4 - c0
                msk = self.multm[:, d0:d0 + nblk, :].rearrange("p a b -> p (a b)")
                sc.op("dve", lambda e: e.tensor_tensor(
                    out=PT[:, c0 * 128:512], in0=PT[:, c0 * 128:512], in1=msk, op=ALU.mult),
                    reads=[self.const_b], writes=[ptb])

        def emit_pv(t):
            g, j = tiles[t]
            c0 = max(j - 4 * g, 0)
            PT = self.PT[t % NPT]
            ptb = self.PT_b[t % NPT]
            ob = 4 + (g % 2)
            opsv = self.bank[ob][:, 0:260].rearrange("p (a b) -> p a b", a=4)
            for c in range(c0, 4):
                sc.op("pe", lambda e, c=c: e.matmul(
                    opsv[:, c, :], PT[:, c * 128:(c + 1) * 128], VA[:, j, vsel, 0:65],
                    start=(j == 0 and c == 0), stop=(j == 4 * g + c), skip_group_check=True),
                    reads=[ptb, VAb], writes=[self.bank_b[ob]], signal=(c == 3))
            if j == 4 * g + 3:
                def norm(opsv=opsv, ob=ob, g=g):
                    sc.op("dve", lambda e: e.reciprocal(out=self.RDEN[:], in_=opsv[:, :, 64]),
                          writes=[self.bank_b[ob], self.rden_b])
                    sc.op("dve", lambda e: e.tensor_tensor(
                        out=self.merged[:, 4 * g:4 * g + 4, col0:col0 + 64], in0=opsv[:, :, 0:64],
                        in1=self.RDEN[:].unsqueeze(2).to_broadcast([128, 4, 64]), op=ALU.mult),
                        reads=[self.rden_b], writes=[self.bank_b[ob], self.merged_b])
                self.defer(4, norm)

        for t in range(n + LAG):
            if t < n:
                emit_qk(t)
            if t - LAG >= 0:
                emit_pv(t - LAG)
            self.tick_deferred()
            yield t

    def outproj(self, b):
        sc = self.sc
        NB, NG = self.NB, self.NG
        s = 1
        mergedT = self.hT_m
        mtb = self.hTm_b
        ssb = self.ss_b
        sc.op("dve", lambda e: e.memset(self.SSAB[:], 0.0), writes=[ssb])
        for blk in range(NB):
            for grp in range(2):
                sc.op("act", lambda e, blk=blk, grp=grp: e.activation(
                    self.SQJ[:], self.merged[:, blk, grp * 512:(grp + 1) * 512], AF.Square,
                    accum_out=self.SSAB[:, blk, grp:grp + 1]),
                    reads=[self.merged_b], writes=[ssb])
        sc.op("act", lambda e: e.activation(self.RSAB[:], self.SSAB[:], AF.Ln, bias=EPS, scale=1.0 / 512), writes=[ssb])
        sc.op("act", lambda e: e.activation(self.RSAB[:], self.RSAB[:], AF.Exp, scale=-0.5), writes=[ssb])
        gout = self.gains[:, 6, :]
        wov = self.w_out_d.rearrange("(k p) f -> p k f", p=128)
        slots = []
        for hf in range(2):
            si = self.slab([(lambda r: r[:].rearrange("p (k f) -> p k f", k=KC), wov[:, :, hf * 512:(hf + 1) * 512])], keep=hf)
            slot = self.ring[si][:].rearrange("p (k f) -> p k f", k=KC)
            slots.append((slot, self.ring_b[si]))

        def norm_transpose(blk):
            mn = self.MN[blk % 2]
            mnb = self.MN_b[blk % 2]
            sc.op("dve", lambda e: e.tensor_tensor(
                out=mn[:].rearrange("p (a b) -> p a b", a=2), in0=self.merged[:, blk, :].rearrange("p (a b) -> p a b", a=2),
                in1=self.RSAB[:, blk, :].unsqueeze(2).to_broadcast([128, 2, 512]), op=ALU.mult),
                reads=[self.merged_b, ssb], writes=[mnb])
            tb = 6 + (blk % 2)
            tp = self.tp_view(tb)
            for c in range(KC):
                sc.op("pe", lambda e, c=c: e.transpose(tp[:, c * 128:(c + 1) * 128], mn[:, c * 128:(c + 1) * 128], self.identb[:]),
                      reads=[mnb, self.const_b], writes=[self.bank_b[tb]], signal=(c == KC - 1))
            sc.op("dve", lambda e: e.tensor_tensor(
                out=mergedT[:, :, blk * 128:(blk + 1) * 128], in0=tp.rearrange("p (a b) -> p a b", a=KC),
                in1=gout.unsqueeze(2).to_broadcast([128, KC, 128]), op=ALU.mult),
                reads=[self.small_b], writes=[self.bank_b[tb], mtb[blk // 4]])

        yi = [0]

        def mm_chunk(sg, c):
            slot, rb = slots[c // 4]
            yb = yi[0] % 4
            yi[0] += 1
            yps = self.bank[yb]
            pnb = 4 + (sg % 2)
            for k in range(KC):
                sc.op("pe", lambda e, k=k: e.matmul(
                    yps[:], slot[:, k, (c % 4) * 128:(c % 4 + 1) * 128], mergedT[:, k, sg * 512:(sg + 1) * 512],
                    start=(k == 0), stop=(k == KC - 1)),
                    reads=[rb, mtb[sg]], writes=[self.bank_b[yb]], signal=(k == KC - 1))
            sq = self.SQ2m[:, c % 4, :]

            def evac():
                sc.op("act", lambda e: e.copy(self.ystage_m[:, c, :], yps[:]), writes=[self.bank_b[yb], self.ysm_b])
                sc.op("act", lambda e: e.activation(sq, yps[:], AF.Square), writes=[self.bank_b[yb], self.SQ2m_b[c % 4]])

            def stat():
                sc.op("pe", lambda e: e.matmul(self.bank[pnb][:], self.onesb[:], sq, start=(c == 0), stop=(c == KC - 1)),
                      reads=[self.SQ2m_b[c % 4], self.const_b], writes=[self.bank_b[pnb]], signal=True)
            return evac, stat

        def pn_sa(sg):
            self.postnorm_update(b, s, sg, self.ystage_m, [self.ysm_b], self.LN2m, self.RSTD2m, self.LN2m_b,
                                 self.UTMPm, self.UTMPm_b, ssbank=4 + (sg % 2))

        for bl in range(4):
            norm_transpose(bl)
        for sg in range(NG):
            if sg + 1 < NG:
                for bl in range(4):
                    norm_transpose((sg + 1) * 4 + bl)
            pend = [mm_chunk(sg, c) for c in range(4)]
            if sg > 0:
                pn_sa(sg - 1)
            prev_stat = None
            for c in range(KC):
                if c < 4:
                    evac, stat = pend[c]
                else:
                    evac, stat = mm_chunk(sg, c)
                evac()
                if prev_stat is not None:
                    prev_stat()
                prev_stat = stat
            prev_stat()
            if sg > 0:
                self.stats_ahead(sg - 1, self.SQ2m, self.SQ2m_b, self.LN2m, self.LN2m_b, ssbank=4 + ((sg - 1) % 2))
        pn_sa(NG - 1)
        self.stats_ahead(NG - 1, self.SQ2m, self.SQ2m_b, self.LN2m, self.LN2m_b, ssbank=4 + ((NG - 1) % 2))


def _consts():
    identf = np.eye(128, dtype=np.float32)
    s_idx = np.arange(128)[:, None]
    t_idx = np.arange(128)[None, :]
    trif = (s_idx <= t_idx).astype(np.float32)
    kk = np.arange(128)[:, None, None]
    DD = np.arange(16)[None, :, None]
    tq = np.arange(128)[None, None, :]
    delta = DD * 128 + tq - kk
    m = ((delta >= 0) & (delta <= 128)).astype(np.float32)
    m += ((delta >= 0) & (delta <= 512) & (delta % 4 == 0)).astype(np.float32)
    m += ((delta >= 0) & (delta <= 2048) & (delta % 16 == 0)).astype(np.float32)
    mult = m.reshape(128, 16 * 128).astype(np.float32)
    neg = np.where(t_idx >= s_idx, 0.0, -30000.0).astype(np.float32)
    inv_freq = (THETA ** (-np.arange(0, ROT, 2, dtype=np.float32) / ROT)).astype(np.float32)
    invfreq = np.ascontiguousarray(np.broadcast_to(inv_freq[None, :], (128, 8))).astype(np.float32)
    return identf, trif, mult, neg, invfreq


def make_in_maps(inputs, S, NSEQ, ncores):
    f = lambda a: np.ascontiguousarray(np.asarray(a))
    x = f(inputs["x"]).astype(np.float32, copy=False)
    c = f(inputs["c"]).astype(np.float32, copy=False)
    pos = f(inputs["positions"]).astype(np.int32, copy=False)
    NB = S // 128
    identf, trif, mult, neg, invfreq = _consts()
    fm = lambda g: np.ascontiguousarray(np.asarray(g, dtype=np.float32).reshape(-1, 128).T)
    gains = np.stack([fm(inputs["g_pre_ff1"][0]), fm(inputs["g_post_ff1"][0]), fm(inputs["g_pre_mix"][0]),
                      fm(inputs["g_post_mix"][0]), fm(inputs["g_pre_ff2"][0]), fm(inputs["g_post_ff2"][0]),
                      fm(np.concatenate([np.asarray(inputs["g_out_a"][0]), np.asarray(inputs["g_out_b"][0])]))], axis=1)
    gains = np.ascontiguousarray(gains.astype(np.float32))
    shared = {
        "w_ada": f(inputs["w_ada"][0]), "b_adaT": fm(inputs["b_ada"][0]), "gains": gains,
        "bforget": np.ascontiguousarray(np.broadcast_to(np.asarray(inputs["b_forget"][0], dtype=np.float32)[None, :], (128, NHB))),
        "invfreq": invfreq,
        "w_ff1_gate": f(inputs["w_ff1_gate"][0]), "w_ff1_up": f(inputs["w_ff1_up"][0]), "w_ff1_down": f(inputs["w_ff1_down"][0]),
        "w_ff2_gate": f(inputs["w_ff2_gate"][0]), "w_ff2_up": f(inputs["w_ff2_up"][0]), "w_ff2_down": f(inputs["w_ff2_down"][0]),
        "w_in": f(inputs["w_in"][0]), "w_out": f(inputs["w_out"][0]),
        "c_identf": identf, "c_trif": trif, "c_mult": mult, "c_neg": neg,
    }
    maps = []
    for ci in range(ncores):
        sl = slice(ci * NSEQ, (ci + 1) * NSEQ)
        m = dict(shared)
        m["x"] = np.ascontiguousarray(x[sl])
        m["cT"] = np.ascontiguousarray(c[sl].reshape(NSEQ, KC, 128).transpose(2, 1, 0))
        m["pos"] = np.ascontiguousarray(pos[sl].reshape(NSEQ, NB, 128).transpose(2, 0, 1))
        maps.append(m)
    return maps


_NC_CACHE = {}


def run(inputs, S, NSEQ, ncores=NCORES, trace=False):
    key = (S, NSEQ)
    if key not in _NC_CACHE:
        _NC_CACHE[key] = Builder(S, NSEQ).build()
    nc = _NC_CACHE[key]
    maps = make_in_maps(inputs, S, NSEQ, ncores)
    res = run_bass_kernel_spmd(nc, maps, core_ids=list(range(ncores)), **({"trace": True} if trace else {}))
    out = np.concatenate([r["out"] for r in res.results], axis=0)
    return out, res


def kernel(**inputs):
    x = np.asarray(inputs["x"])
    Btot, S, _ = x.shape
    NSEQ = Btot // NCORES
    out, _ = run(inputs, S, NSEQ)
    return out.astype(np.float32, copy=False)
```

```python
import contextlib
import math

import numpy as np
import ml_dtypes

import concourse.bass as bass
import concourse.mybir as mybir
from concourse.bass_utils import run_bass_kernel_spmd

F32 = mybir.dt.float32
BF16 = mybir.dt.bfloat16
I32 = mybir.dt.int32
AF = mybir.ActivationFunctionType
ALU = mybir.AluOpType

D = 1024
KC = 8
DFF = 2752
FCH = 22
HD = 64
NHA = 8
NHB = 8
INCOLS = 3080
EPS = 1e-6
NCORES = 8
ROT = 16
THETA = 500000.0

MAGIC = 12582912.0
TWO_PI_HI = 6.28125
TWO_PI_LO = float(np.float32(2 * math.pi - 6.28125))
PI_SAFE = 3.1415925


class Buf:
    __slots__ = ("name", "w", "r", "dsem", "dcount")

    def __init__(self, name):
        self.name = name
        self.w = {}
        self.r = {}
        self.dsem = None
        self.dcount = 0


class Eng:
    def __init__(self, name, h, sem):
        self.name = name
        self.h = h
        self.sem = sem
        self.ticks = 0
        self.seen = {}
        self.stream = []


class Sched:
    def __init__(self, nc, es):
        self.nc = nc
        self.es = es
        self.sems = {}
        self.eng = {}
        for name, h in (("pe", nc.tensor), ("act", nc.scalar), ("dve", nc.vector),
                        ("pool", nc.gpsimd), ("sp", nc.sync)):
            sem = es.enter_context(nc.semaphore("e_" + name))
            self.eng[name] = Eng(name, h, sem)
            self.sems[("eng", name)] = sem
        self.nbuf = 0
        self.dry = False

    def buf(self, name):
        self.nbuf += 1
        return Buf(f"{name}_{self.nbuf}")

    def _deps(self, en, reads, writes):
        raw = {}
        war = {}
        for b in reads:
            for k, v in b.w.items():
                if raw.get(k, 0) < v:
                    raw[k] = v
        for b in writes:
            for k, v in b.w.items():
                if raw.get(k, 0) < v:
                    raw[k] = v
            for k, v in b.r.items():
                if war.get(k, 0) < v:
                    war[k] = v
        own = ("eng", en)
        deps = dict(raw)
        if en == "pe":
            deps.pop(own, None)
        for k, v in war.items():
            if k == own:
                continue
            if deps.get(k, 0) < v:
                deps[k] = v
        return deps

    def _emit_waits(self, E, deps):
        for k, v in deps.items():
            if E.seen.get(k, 0) >= v:
                continue
            E.h.wait_ge(self.sems[k], v)
            E.seen[k] = v
            E.stream.append(("wait", k, v))

    def op(self, en, fn, reads=(), writes=(), signal=True):
        if self.dry:
            return None
        E = self.eng[en]
        deps = self._deps(en, reads, writes)
        self._emit_waits(E, deps)
        inst = fn(E.h)
        key = ("eng", en)
        if signal:
            E.ticks += 1
            inst.then_inc(E.sem, 1)
            tick = E.ticks
            E.stream.append(("op", key, 1))
        else:
            tick = E.ticks + 1
            E.stream.append(("op", None, 0))
        for b in writes:
            if b.w.get(key, 0) < tick:
                b.w[key] = tick
        for b in reads:
            if b.r.get(key, 0) < tick:
                b.r[key] = tick
        return inst

    def dma(self, qn, out, in_, reads=(), writes=()):
        if self.dry:
            return None
        Q = self.eng[qn]
        deps = self._deps(qn, reads, writes)
        self._emit_waits(Q, deps)
        prim = writes[0] if writes else reads[0]
        if prim.dsem is None:
            prim.dsem = self.es.enter_context(self.nc.semaphore("d_" + prim.name))
            self.sems[("dma", prim.name)] = prim.dsem
        prim.dcount += 16
        key = ("dma", prim.name)
        Q.h.dma_start(out=out, in_=in_).then_inc(prim.dsem, 16)
        Q.stream.append(("op", key, 16))
        for b in writes:
            b.w[key] = prim.dcount
        for b in reads:
            b.r[key] = prim.dcount

    def final_wait(self, en, bufs):
        E = self.eng[en]
        deps = {}
        for b in bufs:
            for k, v in list(b.w.items()) + list(b.r.items()):
                if deps.get(k, 0) < v:
                    deps[k] = v
        deps.pop(("eng", en), None)
        self._emit_waits(E, deps)

    def check_deadlock(self):
        cnt = {}
        ptr = {n: 0 for n in self.eng}
        pending_fwd = {n: False for n in self.eng}
        progress = True
        while progress:
            progress = False
            for n, E in self.eng.items():
                while ptr[n] < len(E.stream):
                    kind, k, v = E.stream[ptr[n]]
                    if kind == "wait":
                        if cnt.get(k, 0) >= v:
                            ptr[n] += 1
                            progress = True
                        else:
                            break
                    else:
                        if k is not None:
                            cnt[k] = cnt.get(k, 0) + v
                        ptr[n] += 1
                        progress = True
        stuck = {n: (ptr[n], len(E.stream), E.stream[ptr[n]] if ptr[n] < len(E.stream) else None)
                 for n, E in self.eng.items() if ptr[n] < len(E.stream)}
        if stuck:
            raise RuntimeError(f"semaphore deadlock in generated program: {stuck} cnt={ {k: cnt.get(k) for _, (_, _, s) in stuck.items() for k in [s[1]]} }")


class Builder:
    def __init__(self, S, NSEQ):
        assert S % 512 == 0
        self.S = S
        self.NSEQ = NSEQ
        self.NB = S // 128
        self.NG = S // 512
        self.T = min(1024, S)
        self.NTG = S // self.T
        self.SGT = self.T // 512

    def sb(self, name, shape, dt):
        return self.es.enter_context(self.nc.sbuf_tensor(name, shape, dt))

    def build(self):
        nc = bass.Bass("TRN2", target_bir_lowering=False)
        self.nc = nc
        S, NSEQ, NB, NG = self.S, self.NSEQ, self.NB, self.NG
        dr = lambda name, shape, dt, kind="ExternalInput": nc.dram_tensor(name, shape, dt, kind=kind).ap()
        self.x_d = dr("x", [NSEQ, S, D], F32)
        self.cT_d = dr("cT", [128, KC, NSEQ], F32)
        self.pos_d = dr("pos", [128, NSEQ, NB], I32)
        self.w_ada_d = dr("w_ada", [D, 9 * D], F32)
        self.b_adaT_d = dr("b_adaT", [128, 72], F32)
        self.gains_d = dr("gains", [128, 7, KC], F32)
        self.bforget_d = dr("bforget", [128, NHB], F32)
        self.invfreq_d = dr("invfreq", [128, 8], F32)
        self.wg_d = [dr("w_ff1_gate", [D, DFF], F32), dr("w_ff2_gate", [D, DFF], F32)]
        self.wu_d = [dr("w_ff1_up", [D, DFF], F32), dr("w_ff2_up", [D, DFF], F32)]
        self.wd_d = [dr("w_ff1_down", [DFF, D], F32), dr("w_ff2_down", [DFF, D], F32)]
        self.w_in_d = dr("w_in", [D, INCOLS], F32)
        self.w_out_d = dr("w_out", [D, D], F32)
        self.c_identf_d = dr("c_identf", [128, 128], F32)
        self.c_trif_d = dr("c_trif", [128, 128], F32)
        self.c_mult_d = dr("c_mult", [128, 16 * 128], F32)
        self.c_neg_d = dr("c_neg", [128, 128], F32)
        self.out_d = dr("out", [NSEQ, S, D], F32, kind="ExternalOutput")

        with contextlib.ExitStack() as es:
            self.es = es
            self.sc = Sched(nc, es)
            self.alloc()
            self.slab_plan = []
            self.deferred = []
            for dry in (True, False):
                self.sc.dry = dry
                self.slab_count = 0
                self.slab_emitted = 0
                self.setup()
                for b in range(NSEQ):
                    self.sequence(b)
            self.sc.final_wait("sp", self.stage_b)
            self.sc.check_deadlock()
        return nc

    def alloc(self):
        nc, sc = self.nc, self.sc
        S, NSEQ, NB, T = self.S, self.NSEQ, self.NB, self.T
        sb = self.sb
        B = sc.buf
        self.xT = sb("xT", [128, KC, S], F32)
        self.xT_b = [B("xT") for _ in range(self.NG)]
        self.identb = sb("identb", [128, 128], BF16)
        self.identf = sb("identf", [128, 128], F32)
        self.onesb = sb("onesb", [128, 128], BF16)
        self.onesf = sb("onesf", [128, 128], F32)
        self.trif = sb("trif", [128, 128], F32)
        self.multm = sb("multm", [128, 16, 128], BF16)
        self.negm = sb("negm", [128, 128], BF16)
        self.const_b = B("const")
        self.cT = sb("cT_sb", [128, KC, NSEQ], F32)
        self.scT = sb("scT", [128, KC, NSEQ], BF16)
        self.sctmp = sb("sctmp", [128, KC, NSEQ], F32)
        self.modT = sb("modT", [128, 72, NSEQ], F32)
        self.b_adaT = sb("b_adaT_sb", [128, 72], F32)
        self.gains = sb("gains_sb", [128, 7, KC], F32)
        self.mods = sb("mods", [128, 9, KC, NSEQ], F32)
        self.bforget = sb("bforget_sb", [128, NHB], F32)
        self.invfreq = sb("invfreq_sb", [128, 8], F32)
        self.pos_i = sb("pos_i", [128, NSEQ, NB], I32)
        self.small_b = B("small")
        self.mod_b = B("mod")
        self.cq2 = sb("cq2", [128, NB, 16], F32)
        self.sq1 = sb("sq1", [128, NB, 8], F32)
        self.ck2 = sb("ck2", [128, NB, 16], F32)
        self.sk1 = sb("sk1", [128, NB, 8], F32)
        self.rope_b = B("rope")
        self.ropetmp_b = B("ropetmp")
        self.NSLOT = 3
        self.ring = [sb(f"ring{i}", [128, 4096], BF16) for i in range(self.NSLOT)]
        self.ring_b = [B(f"ring{i}") for i in range(self.NSLOT)]
        self.ring_i = 0
        self.wf = sb("wf", [128, KC, 8], BF16)
        self.wf_b = B("wf")
        REG = (int(self.nc.sbuf_bytes_remaining) - 256) // 256 * 256
        self.region = sb("region", [128, REG // 4], F32)
        self.REG = REG

        def carve(off, shape, dt):
            esz = 2 if dt == BF16 else 4
            n = int(np.prod(shape[1:]))
            assert off % 4 == 0 and off + n * esz <= REG, (off, shape, REG)
            ap = self.region[:, off // 4: off // 4 + (n * esz) // 4]
            if dt != F32:
                ap = ap.bitcast(dt)
            if len(shape) == 3:
                ap = ap.rearrange("p (a b) -> p a b", a=shape[1])
            elif len(shape) == 4:
                ap = ap.rearrange("p (a b c) -> p a b c", a=shape[1], b=shape[2])
            elif len(shape) == 5:
                ap = ap.rearrange("p (a b c d) -> p a b c d", a=shape[1], b=shape[2], c=shape[3])
            return ap, off + n * esz

        self.carve = carve
        SGT = self.SGT
        o = 0
        self.aT, o = carve(o, [128, FCH, T], BF16)
        self.aT_b = [B("aT") for _ in range(SGT)]
        r1 = o
        self.hT_f, o = carve(o, [128, KC, T], BF16)
        self.hTf_b = [B("hTf") for _ in range(SGT)]
        self.SQ, o = carve(o, [128, 4, 512], BF16)
        self.SQ_b = B("SQ")
        self.E = []
        for i in range(2):
            e, o = carve(o, [128, 512], F32)
            self.E.append(e)
        self.E_b = [B("E0"), B("E1")]
        self.LN, o = carve(o, [128, 512], F32)
        self.RSTD, o = carve(o, [128, 512], F32)
        self.LN_b = B("LN")
        self.HTMP = []
        for i in range(2):
            e, o = carve(o, [128, 512], F32)
            self.HTMP.append(e)
        self.HTMP_b = [B("HT0"), B("HT1")]
        self.ystage_f, o2 = carve(r1, [128, KC, T], F32)
        o = max(o, o2)
        self.R1_bufs = self.hTf_b + [self.SQ_b, self.LN_b] + self.E_b + self.HTMP_b
        self.SQ2, o = carve(o, [128, 4, 512], BF16)
        self.SQ2_b = [B("SQ2") for _ in range(4)]
        self.LN2, o = carve(o, [128, 512], F32)
        self.RSTD2, o = carve(o, [128, 512], F32)
        self.LN2_b = B("LN2")
        self.UTMP = []
        for i in range(2):
            e, o = carve(o, [128, 512], F32)
            self.UTMP.append(e)
        self.UTMP_b = [B("UT0"), B("UT1")]
        self.ffn_end = o
        o = 0
        self.merged, o = carve(o, [128, NB, D], BF16)
        self.merged_b = B("merged")
        self.hT_m, o = carve(o, [128, KC, S], BF16)
        self.hTm_b = [B("hTm") for _ in range(self.NG)]
        m_un = o
        self.QT, o = carve(o, [128, S], BF16)
        self.KT, o = carve(o, [128, S], BF16)
        self.KT1, o = carve(o, [128, S], BF16)
        self.VAs = []
        for i in range(2):
            e, o = carve(o, [128, NB, 2, 66], BF16)
            self.VAs.append(e)
        self.VA = self.VAs[0]
        self.QKtm, o = carve(o, [128, NB, 2, 128], BF16)
        self.NPT = 7
        self.PT = []
        for i in range(self.NPT):
            e, o = carve(o, [128, 512], BF16)
            self.PT.append(e)
        self.PT_b = [B("PT") for _ in range(self.NPT)]
        rotf_off = o
        self.ROTF, o = carve(o, [128, NB, 2, 2, 16], F32)
        self.RH = max(NB // 2, 1)
        self.RA, o = carve(o, [128, self.RH, 2, 16], F32)
        self.RB1, o = carve(o, [128, self.RH, 2, 8], F32)
        self.RB2, o = carve(o, [128, self.RH, 2, 8], F32)
        self.RDEN, o = carve(o, [128, 4], F32)
        m1_end = o
        self.QT_b, self.KT_b, self.QKtm_b = B("QT"), B("KT"), B("QKtm")
        self.VAs_b = [B("VA0"), B("VA1")]
        self.KT1_b = B("KT1")
        self.rtmp_b = B("rtmp")
        self.rden_b = B("rden")
        o = m_un
        self.SQm, o = carve(o, [128, 4, 512], BF16)
        self.LNm, o = carve(o, [128, 512], F32)
        self.RSTDm, o = carve(o, [128, 512], F32)
        self.HTMPm = []
        for i in range(2):
            e, o = carve(o, [128, 512], F32)
            self.HTMPm.append(e)
        self.SQm_b, self.LNm_b, self.HTMPm_b = B("SQm"), B("LNm"), [B("HTm0"), B("HTm1")]
        m0_end = o
        o = m_un
        self.ystage_m, o = carve(o, [128, KC, 512], F32)
        self.ysm_b = B("ysm")
        self.MN = []
        for i in range(2):
            e, o = carve(o, [128, D], BF16)
            self.MN.append(e)
        self.MN_b = [B("MN0"), B("MN1")]
        self.SSAB, o = carve(o, [128, NB, 2], F32)
        self.RSAB, o = carve(o, [128, NB, 2], F32)
        self.SQJ, o = carve(o, [128, 512], BF16)
        self.ss_b = B("ssab")
        self.SQ2m, o = carve(o, [128, 4, 512], BF16)
        self.SQ2m_b = [B("SQ2m") for _ in range(4)]
        self.LN2m, o = carve(o, [128, 512], F32)
        self.RSTD2m, o = carve(o, [128, 512], F32)
        self.LN2m_b = B("LN2m")
        self.UTMPm = []
        for i in range(2):
            e, o = carve(o, [128, 512], F32)
            self.UTMPm.append(e)
        self.UTMPm_b = [B("UTm0"), B("UTm1")]
        m2_end = o
        alias_f = rotf_off >= m0_end
        o = rotf_off if alias_f else max(m1_end, m0_end)
        self.FL, o = carve(o, [128, NB, 8], F32)
        self.FL2, o = carve(o, [128, NB, 8], F32)
        self.NEGF, o = carve(o, [128, NB, 8], F32)
        self.CARRY, o = carve(o, [128, NB, 8], F32)
        self.F32T, o = carve(o, [128, NB, 8], F32)
        if alias_f:
            assert o <= rotf_off + NB * 2 * 2 * 16 * 4
            o = max(m1_end, m0_end)
        self.FH, o = carve(o, [128, NB, 8], BF16)
        self.FM, o = carve(o, [128, NB, 8], BF16)
        self.FLo, o = carve(o, [128, NB, 8], BF16)
        self.F_b = B("F")
        self.aug_b = B("aug")
        self.mix_end = max(o, m2_end)
        o = max(m2_end, self.ffn_end)
        self.RSTDN, o = carve(o, [128, self.NG, 512], F32)
        self.rstdn_b = [B("rstdn") for _ in range(self.NG)]
        self.layout_info = dict(REG=REG, ffn_end=self.ffn_end, m1_end=m1_end, m0_end=m0_end, m2_end=m2_end, mix_end=self.mix_end)
        self.NSTAGE = 4
        self.stage = []
        o = 0
        for i in range(self.NSTAGE):
            e, o = carve(o, [128, D], F32)
            self.stage.append(e)
        self.stage_b = [B(f"stage{i}") for i in range(self.NSTAGE)]
        self.rt = []
        for i in range(6):
            e, o = carve(o, [128, NB, 8], F32)
            self.rt.append(e)
        self.region_b = B("region")
        self.bank = [self.es.enter_context(nc.psum_tensor(f"bank{i}", [128, 512], F32)) for i in range(8)]
        self.bank_b = [B(f"bank{i}") for i in range(8)]

    def defer(self, delay, fn):
        self.deferred.append([delay, fn])

    def tick_deferred(self, flush=False):
        keep = []
        for item in self.deferred:
            item[0] -= 1
            if flush or item[0] <= 0:
                item[1]()
            else:
                keep.append(item)
        self.deferred = keep

    def tp_view(self, i):
        return self.bank[i][:].bitcast(BF16)

    def phase_sync(self):
        sc = self.sc
        if sc.dry:
            return
        snap = {("eng", n): E.ticks for n, E in sc.eng.items() if E.ticks > 0}
        for k, v in self.region_b.w.items():
            if k[0] == "dma":
                snap[k] = max(snap.get(k, 0), v)
        for n, E in sc.eng.items():
            deps = dict(snap)
            deps.pop(("eng", n), None)
            sc._emit_waits(E, deps)

    def slab(self, parts, keep=0):
        idx = self.slab_count
        self.slab_count += 1
        if self.sc.dry:
            self.slab_plan.append(parts)
            return idx % self.NSLOT
        plan = self.slab_plan
        while self.slab_emitted <= min(idx + self.NSLOT - 1 - keep, len(plan) - 1):
            i = self.slab_emitted
            si = i % self.NSLOT
            for dst_fn, srcap in plan[i]:
                self.sc.dma("pool", dst_fn(self.ring[si]), srcap, writes=[self.ring_b[si]])
            self.slab_emitted += 1
        return idx % self.NSLOT

    def setup(self):
        nc, sc = self.nc, self.sc
        NSEQ = self.NSEQ
        small = self.small_b
        for dst, src in ((self.cT, self.cT_d), (self.b_adaT, self.b_adaT_d), (self.gains, self.gains_d),
                         (self.bforget, self.bforget_d), (self.invfreq, self.invfreq_d),
                         (self.pos_i, self.pos_d), (self.identf, self.c_identf_d), (self.trif, self.c_trif_d)):
            sc.dma("sp", dst[:], src, writes=[small])
        cb = self.const_b
        sc.dma("pool", self.identb[:], self.c_identf_d, writes=[cb])
        sc.dma("pool", self.multm[:].rearrange("p a b -> p (a b)"), self.c_mult_d, writes=[cb])
        sc.dma("pool", self.negm[:], self.c_neg_d, writes=[cb])
        sc.op("dve", lambda e: e.memset(self.onesb[:], 1.0), writes=[cb])
        sc.op("dve", lambda e: e.memset(self.onesf[:], 1.0), writes=[cb])
        mb = self.mod_b
        sc.op("act", lambda e: e.activation(self.sctmp[:], self.cT[:], AF.Exp, scale=-1.0), reads=[small], writes=[mb])
        sc.op("dve", lambda e: e.tensor_scalar(out=self.sctmp[:], in0=self.sctmp[:], scalar1=1.0, scalar2=None, op0=ALU.add), writes=[mb])
        sc.op("dve", lambda e: e.reciprocal(out=self.sctmp[:], in_=self.sctmp[:]), writes=[mb])
        sc.op("dve", lambda e: e.tensor_tensor(out=self.scT[:], in0=self.cT[:], in1=self.sctmp[:], op=ALU.mult), reads=[small], writes=[mb])
        wv = self.w_ada_d.rearrange("(k p) f -> p k f", p=128)
        for sl in range(18):
            si = self.slab([(lambda r: r[:].rearrange("p (k f) -> p k f", k=KC), wv[:, :, sl * 512:(sl + 1) * 512])])
            slot = self.ring[si][:].rearrange("p (k f) -> p k f", k=KC)
            bk = sl % 2
            ps = self.bank[bk]
            for sub in range(4):
                for k in range(KC):
                    sc.op("pe", lambda e, k=k, sub=sub: e.matmul(ps[:, sub * NSEQ:(sub + 1) * NSEQ],
                                                                   slot[:, k, sub * 128:(sub + 1) * 128],
                                                                   self.scT[:, k, :], start=(k == 0), stop=(k == KC - 1)),
                          reads=[self.ring_b[si], mb], writes=[self.bank_b[bk]], signal=(k == KC - 1))
            sc.op("dve", lambda e, sl=sl, ps=ps: e.tensor_tensor(
                out=self.modT[:, sl * 4:(sl + 1) * 4, :],
                in0=ps[:, 0:4 * NSEQ].rearrange("p (a b) -> p a b", a=4),
                in1=self.b_adaT[:, sl * 4:(sl + 1) * 4].unsqueeze(2).to_broadcast([128, 4, NSEQ]), op=ALU.add),
                reads=[small], writes=[self.bank_b[bk], mb])
        gi_pre = (0, 2, 4)
        gi_post = (1, 3, 5)
        for s in range(3):
            m_shift = self.modT[:, (3 * s) * 8:(3 * s + 1) * 8, :]
            m_scale = self.modT[:, (3 * s + 1) * 8:(3 * s + 2) * 8, :]
            m_gate = self.modT[:, (3 * s + 2) * 8:(3 * s + 3) * 8, :]
            gpre = self.gains[:, gi_pre[s], :].unsqueeze(2).to_broadcast([128, KC, NSEQ])
            gpost = self.gains[:, gi_post[s], :].unsqueeze(2).to_broadcast([128, KC, NSEQ])
            sc.op("dve", lambda e, m_scale=m_scale, gpre=gpre, s=s: e.scalar_tensor_tensor(
                out=self.mods[:, 3 * s + 0, :, :], in0=m_scale, scalar=1.0, in1=gpre, op0=ALU.add, op1=ALU.mult),
                reads=[small], writes=[mb])
            sc.op("dve", lambda e, m_shift=m_shift, s=s: e.tensor_copy(self.mods[:, 3 * s + 1, :, :], m_shift), writes=[mb])
            sc.op("dve", lambda e, m_gate=m_gate, gpost=gpost, s=s: e.scalar_tensor_tensor(
                out=self.mods[:, 3 * s + 2, :, :], in0=m_gate, scalar=(0.5 if s != 1 else 1.0), in1=gpost,
                op0=ALU.mult, op1=ALU.mult), reads=[small], writes=[mb])

    def sequence(self, b):
        self.phase_sync()
        self.load_x(b)
        self.rope_tables(b)
        self.phase_sync()
        self.ffn(b, 0, 0)
        self.phase_sync()
        self.mixer(b)
        self.phase_sync()
        self.ffn(b, 2, 1)
        self.phase_sync()
        self.store_x(b)

    def load_x(self, b):
        sc = self.sc
        for blk in range(self.NB):
            st = blk % self.NSTAGE
            sc.dma("sp", self.stage[st][:], self.x_d[b, blk * 128:(blk + 1) * 128, :], writes=[self.stage_b[st]])
            for half in range(2):
                bk = 2 * (blk % 2) + half
                for cc in range(4):
                    c = half * 4 + cc
                    sc.op("pe", lambda e, c=c, cc=cc, st=st, bk=bk: e.transpose(
                        self.bank[bk][:, cc * 128:(cc + 1) * 128], self.stage[st][:, c * 128:(c + 1) * 128], self.identf[:]),
                        reads=[self.stage_b[st], self.small_b], writes=[self.bank_b[bk]], signal=(cc == 3))
                src = self.bank[bk][:].rearrange("p (a b) -> p a b", a=4)
                dst = self.xT[:, half * 4:(half + 1) * 4, blk * 128:(blk + 1) * 128]
                if half == 0:
                    sc.op("act", lambda e, dst=dst, src=src: e.copy(dst, src), writes=[self.bank_b[bk], self.xT_b[blk // 4]])
                else:
                    sc.op("dve", lambda e, dst=dst, src=src: e.tensor_copy(dst, src), writes=[self.bank_b[bk], self.xT_b[blk // 4]])
            if blk % 4 == 3:
                self.stats_ahead(blk // 4, self.SQ2, self.SQ2_b, self.LN2, self.LN2_b, ssbank=4)

    def store_x(self, b):
        sc = self.sc
        for blk in range(self.NB):
            st = blk % self.NSTAGE
            for half in range(2):
                bk = 2 * (blk % 2) + half
                for cc in range(4):
                    c = half * 4 + cc
                    sc.op("pe", lambda e, c=c, cc=cc, bk=bk, blk=blk: e.transpose(
                        self.bank[bk][:, cc * 128:(cc + 1) * 128], self.xT[:, c, blk * 128:(blk + 1) * 128], self.identf[:]),
                        reads=[self.xT_b[blk // 4], self.small_b], writes=[self.bank_b[bk]], signal=(cc == 3))
                dst = self.stage[st][:, half * 512:(half + 1) * 512]
                src = self.bank[bk][:]
                if half == 0:
                    sc.op("act", lambda e, dst=dst, src=src: e.copy(dst, src), writes=[self.bank_b[bk], self.stage_b[st]])
                else:
                    sc.op("dve", lambda e, dst=dst, src=src: e.tensor_copy(dst, src), writes=[self.bank_b[bk], self.stage_b[st]])
            sc.dma("sp", self.out_d[b, blk * 128:(blk + 1) * 128, :], self.stage[st][:], reads=[self.stage_b[st]])
        for st in range(self.NSTAGE):
            for k, v in list(self.stage_b[st].r.items()) + list(self.stage_b[st].w.items()):
                self.region_b.w[k] = max(self.region_b.w.get(k, 0), v)

    def rope_tables(self, b):
        sc = self.sc
        NB = self.NB
        rb, tb = self.rope_b, self.ropetmp_b
        posf, ang, t, k, r1, r2 = self.rt
        V = lambda fn, reads=(), writes=(): sc.op("dve", fn, reads=reads, writes=writes)
        V(lambda e: e.tensor_copy(posf[:], self.pos_i[:, b, :].unsqueeze(2).to_broadcast([128, NB, 8])),
          reads=[self.small_b], writes=[tb])
        V(lambda e: e.tensor_tensor(out=ang[:], in0=posf[:], in1=self.invfreq[:].unsqueeze(1).to_broadcast([128, NB, 8]),
                                    op=ALU.mult), reads=[self.small_b], writes=[tb])
        for which in ("sin", "cos"):
            src = ang
            if which == "cos":
                V(lambda e: e.tensor_scalar(out=posf[:], in0=ang[:], scalar1=math.pi / 2, scalar2=None, op0=ALU.add), writes=[tb])
                src = posf
            V(lambda e, src=src: e.tensor_scalar(out=t[:], in0=src[:], scalar1=1.0 / (2 * math.pi), scalar2=MAGIC,
                                                 op0=ALU.mult, op1=ALU.add), writes=[tb])
            V(lambda e: e.tensor_scalar(out=k[:], in0=t[:], scalar1=MAGIC, scalar2=None, op0=ALU.subtract), writes=[tb])
            V(lambda e, src=src: e.scalar_tensor_tensor(out=r1[:], in0=k[:], scalar=-TWO_PI_HI, in1=src[:],
                                                        op0=ALU.mult, op1=ALU.add), writes=[tb])
            V(lambda e: e.scalar_tensor_tensor(out=r2[:], in0=k[:], scalar=-TWO_PI_LO, in1=r1[:],
                                               op0=ALU.mult, op1=ALU.add), writes=[tb])
            V(lambda e: e.tensor_scalar(out=r2[:], in0=r2[:], scalar1=PI_SAFE, scalar2=-PI_SAFE, op0=ALU.min, op1=ALU.max),
              writes=[tb])
            if which == "sin":
                sc.op("act", lambda e: e.activation(self.sk1[:], r2[:], AF.Sin), reads=[tb], writes=[rb])
                sc.op("dve", lambda e: e.tensor_scalar(out=self.sq1[:], in0=self.sk1[:], scalar1=0.125, scalar2=None,
                                                       op0=ALU.mult), writes=[rb])
            else:
                for hf in range(2):
                    sc.op("act", lambda e, hf=hf: e.activation(self.ck2[:, :, hf * 8:(hf + 1) * 8], r2[:], AF.Sin),
                          reads=[tb], writes=[rb])
                sc.op("dve", lambda e: e.tensor_scalar(out=self.cq2[:], in0=self.ck2[:], scalar1=0.125, scalar2=None,
                                                       op0=ALU.mult), writes=[rb])

    def stats_ahead(self, sg, SQ, SQ_b, LN, LN_b, ssbank):
        sc = self.sc
        t0 = sg * 512
        xb = self.xT_b[sg]
        bb = self.bank_b[ssbank]
        ps = self.bank[ssbank]
        for half in range(2):
            for cc in range(4):
                c = half * 4 + cc
                if cc % 2 == 0:
                    sc.op("act", lambda e, c=c, cc=cc: e.activation(SQ[:, cc, :], self.xT[:, c, t0:t0 + 512], AF.Square),
                          reads=[xb], writes=[SQ_b[cc]])
                else:
                    sc.op("dve", lambda e, c=c, cc=cc: e.tensor_tensor(out=SQ[:, cc, :], in0=self.xT[:, c, t0:t0 + 512],
                                                                       in1=self.xT[:, c, t0:t0 + 512], op=ALU.mult),
                          reads=[xb], writes=[SQ_b[cc]])
            for cc in range(4):
                c = half * 4 + cc
                sc.op("pe", lambda e, c=c, cc=cc: e.matmul(ps[:], self.onesb[:], SQ[:, cc, :], start=(c == 0), stop=(c == KC - 1)),
                      reads=[SQ_b[cc], self.const_b], writes=[bb], signal=(cc == 3))
        sc.op("act", lambda e: e.activation(LN[:], ps[:], AF.Ln, bias=EPS, scale=1.0 / D), writes=[bb, LN_b])
        sc.op("act", lambda e: e.activation(self.RSTDN[:, sg, :], LN[:], AF.Exp, scale=-0.5), reads=[LN_b], writes=[self.rstdn_b[sg]])

    def prenorm_apply(self, b, s, sg, hT_dst, hT_buf, HTMP, HTMP_b):
        sc = self.sc
        t0 = sg * 512
        xb = self.xT_b[sg]
        G = self.mods[:, 3 * s + 0, :, :]
        Sh = self.mods[:, 3 * s + 1, :, :]
        for c in range(KC):
            tmp = HTMP[c % 2]
            tb = HTMP_b[c % 2]
            sc.op("dve", lambda e, c=c, tmp=tmp: e.scalar_tensor_tensor(
                out=tmp[:], in0=self.xT[:, c, t0:t0 + 512], scalar=G[:, c, b:b + 1], in1=self.RSTDN[:, sg, :],
                op0=ALU.mult, op1=ALU.mult), reads=[xb, self.mod_b, self.rstdn_b[sg]], writes=[tb])
            sc.op("act", lambda e, c=c, tmp=tmp: e.activation(hT_dst[:, c, :], tmp[:], AF.Identity,
                                                               bias=Sh[:, c, b:b + 1], scale=1.0),
                  reads=[self.mod_b, tb], writes=[hT_buf])

    def postnorm_update(self, b, s, sg, ystage, ybufs, LN2, RSTD2, LN2_b, UTMP, UTMP_b, ssbank):
        sc = self.sc
        t0 = sg * 512
        xb = self.xT_b[sg]
        bb = self.bank_b[ssbank]
        ps = self.bank[ssbank]
        sc.op("act", lambda e: e.activation(LN2[:], ps[:], AF.Ln, bias=EPS, scale=1.0 / D), writes=[bb, LN2_b])
        sc.op("act", lambda e: e.activation(RSTD2[:], LN2[:], AF.Exp, scale=-0.5), writes=[LN2_b])
        Gp = self.mods[:, 3 * s + 2, :, :]
        for c in range(KC):
            tmp = UTMP[c % 2]
            tb = UTMP_b[c % 2]
            sc.op("dve", lambda e, c=c, tmp=tmp: e.tensor_tensor(out=tmp[:], in0=ystage[:, c, :], in1=RSTD2[:], op=ALU.mult),
                  reads=list(ybufs) + [LN2_b], writes=[tb])
            sc.op("dve", lambda e, c=c, tmp=tmp: e.scalar_tensor_tensor(
                out=self.xT[:, c, t0:t0 + 512], in0=tmp[:], scalar=Gp[:, c, b:b + 1], in1=self.xT[:, c, t0:t0 + 512],
                op0=ALU.mult, op1=ALU.add), reads=[self.mod_b, tb], writes=[xb])

    def ffn(self, b, s, wi):
        sc = self.sc
        T, SGT = self.T, self.SGT
        wgv = self.wg_d[wi].rearrange("(k p) f -> p k f", p=128)
        wuv = self.wu_d[wi].rearrange("(k p) f -> p k f", p=128)
        wd = self.wd_d[wi]
        R1 = self.R1_bufs
        for tg in range(self.NTG):
            sg0 = tg * SGT
            for sgl in range(SGT):
                self.prenorm_apply(b, s, sg0 + sgl, self.hT_f[:, :, sgl * 512:(sgl + 1) * 512], self.hTf_b[sgl],
                                   self.HTMP, self.HTMP_b)
            nslab = (DFF + 255) // 256
            gi = 0
            for sl in range(nslab):
                c0 = sl * 256
                ncol = min(256, DFF - c0)
                si = self.slab([
                    (lambda r, ncol=ncol: r[:, 0:2048].rearrange("p (k f) -> p k f", k=KC)[:, :, 0:ncol], wgv[:, :, c0:c0 + ncol]),
                    (lambda r, ncol=ncol: r[:, 2048:4096].rearrange("p (k f) -> p k f", k=KC)[:, :, 0:ncol], wuv[:, :, c0:c0 + ncol])])
                rb = self.ring_b[si]
                slotg = self.ring[si][:, 0:2048].rearrange("p (k f) -> p k f", k=KC)
                slotu = self.ring[si][:, 2048:4096].rearrange("p (k f) -> p k f", k=KC)
                nfc = (ncol + 127) // 128
                for sgl in range(SGT):
                    hb = self.hTf_b[sgl]
                    for fcl in range(nfc):
                        fc = sl * 2 + fcl
                        fr = min(128, ncol - fcl * 128)
                        gb, ub = (gi % 2), 2 + (gi % 2)
                        E = self.E[gi % 2]
                        eb = self.E_b[gi % 2]
                        gi += 1
                        gps, ups = self.bank[gb], self.bank[ub]
                        for k in range(KC):
                            sc.op("pe", lambda e, k=k, fcl=fcl, fr=fr, gps=gps, slotg=slotg, sgl=sgl: e.matmul(
                                gps[0:fr, :], slotg[:, k, fcl * 128:fcl * 128 + fr], self.hT_f[:, k, sgl * 512:(sgl + 1) * 512],
                                start=(k == 0), stop=(k == KC - 1)),
                                reads=[rb, hb], writes=[self.bank_b[gb]], signal=(k == KC - 1))
                        for k in range(KC):
                            sc.op("pe", lambda e, k=k, fcl=fcl, fr=fr, ups=ups, slotu=slotu, sgl=sgl: e.matmul(
                                ups[0:fr, :], slotu[:, k, fcl * 128:fcl * 128 + fr], self.hT_f[:, k, sgl * 512:(sgl + 1) * 512],
                                start=(k == 0), stop=(k == KC - 1)),
                                reads=[rb, hb], writes=[self.bank_b[ub]], signal=(k == KC - 1))
                        sc.op("act", lambda e, E=E, gps=gps, fr=fr: e.activation(E[0:fr, :], gps[0:fr, :], AF.Silu),
                              writes=[self.bank_b[gb], eb])
                        sc.op("dve", lambda e, E=E, ups=ups, fr=fr, fc=fc, sgl=sgl: e.tensor_tensor(
                            out=self.aT[0:fr, fc, sgl * 512:(sgl + 1) * 512], in0=ups[0:fr, :], in1=E[0:fr, :], op=ALU.mult),
                            reads=[eb], writes=[self.bank_b[ub], self.aT_b[sgl]])
            for c in range(KC):
                si = self.slab([
                    (lambda r: r[:, 0:FCH * 128].rearrange("p (j d) -> p j d", j=FCH)[:, 0:21, :],
                     wd[0:21 * 128, c * 128:(c + 1) * 128].rearrange("(j p) d -> p j d", p=128)),
                    (lambda r: r[:, 0:FCH * 128].rearrange("p (j d) -> p j d", j=FCH)[0:64, 21, :],
                     wd[21 * 128:DFF, c * 128:(c + 1) * 128])])
                rb = self.ring_b[si]
                slot = self.ring[si][:, 0:FCH * 128].rearrange("p (j d) -> p j d", j=FCH)
                for sgl in range(SGT):
                    yb = 4 + ((c * SGT + sgl) % 2)
                    yps = self.bank[yb]
                    for j in range(FCH):
                        fr = 128 if j < 21 else 64
                        sc.op("pe", lambda e, j=j, fr=fr, yps=yps, slot=slot, sgl=sgl: e.matmul(
                            yps[:], slot[0:fr, j, :], self.aT[0:fr, j, sgl * 512:(sgl + 1) * 512],
                            start=(j == 0), stop=(j == FCH - 1)),
                            reads=[rb, self.aT_b[sgl]], writes=[self.bank_b[yb]], signal=(j == FCH - 1))
                    ssb = 6 + sgl
                    sqi = (c * SGT + sgl) % 4
                    sq = self.SQ2[:, sqi, :]
                    sc.op("act", lambda e, c=c, sgl=sgl, yps=yps: e.copy(self.ystage_f[:, c, sgl * 512:(sgl + 1) * 512], yps[:]),
                          writes=[self.bank_b[yb]] + R1)
                    sc.op("act", lambda e, sq=sq, yps=yps: e.activation(sq, yps[:], AF.Square),
                          writes=[self.bank_b[yb], self.SQ2_b[sqi]])
                    self.tick_deferred()
                    self.defer(1, lambda c=c, ssb=ssb, sq=sq, sqi=sqi: sc.op(
                        "pe", lambda e: e.matmul(self.bank[ssb][:], self.onesb[:], sq, start=(c == 0), stop=(c == KC - 1)),
                        reads=[self.SQ2_b[sqi], self.const_b], writes=[self.bank_b[ssb]], signal=True))
            self.tick_deferred(flush=True)
            for sgl in range(SGT):
                self.postnorm_update(b, s, sg0 + sgl, self.ystage_f[:, :, sgl * 512:(sgl + 1) * 512], R1,
                                     self.LN2, self.RSTD2, self.LN2_b, self.UTMP, self.UTMP_b, ssbank=6 + sgl)
            if s == 0:
                for sgl in range(SGT):
                    self.stats_ahead(sg0 + sgl, self.SQ2, self.SQ2_b, self.LN2, self.LN2_b, ssbank=6 + sgl)

    def mixer(self, b):
        sc = self.sc
        S, NB, NG = self.S, self.NB, self.NG
        s = 1
        for sg in range(NG):
            self.prenorm_apply(b, s, sg, self.hT_m[:, :, sg * 512:(sg + 1) * 512], self.hTm_b[sg], self.HTMPm, self.HTMPm_b)
        self.fox_prep(b)
        self.phase_sync()
        for vi in range(2):
            sc.op("dve", lambda e, vi=vi: e.memset(self.VAs[vi][:, :, :, 64:65], 1.0), writes=[self.VAs_b[vi]])
        sc.op("dve", lambda e: e.memset(self.KT[64:128, :], 0.0), writes=[self.KT_b])
        sc.op("dve", lambda e: e.memset(self.KT1[0:64, :], 0.0), writes=[self.KT1_b])
        winv = self.w_in_d.rearrange("(k p) f -> p k f", p=128)
        NU = 4 + NHB

        def inproj_gen(u):
            isA = u < 4
            vi = u % 2
            if isA:
                ncol = 384
                parts = [(lambda r, i=i: r[:, 0:KC * 384].rearrange("p (k f) -> p k f", k=KC)[:, :, i * 128:(i + 1) * 128],
                          winv[:, :, base + u * 128: base + (u + 1) * 128]) for i, base in enumerate((0, 512, 1024))]
            else:
                h = u - 4
                ncol = 192
                parts = [(lambda r, i=i: r[:, 0:KC * 192].rearrange("p (k f) -> p k f", k=KC)[:, :, i * 64:(i + 1) * 64],
                          winv[:, :, base + h * 64: base + (h + 1) * 64]) for i, base in enumerate((1536, 2048, 2560))]
            si = self.slab(parts)
            rb = self.ring_b[si]
            slot = self.ring[si][:, 0:KC * ncol].rearrange("p (k f) -> p k f", k=KC)
            for blk in range(NB):
                bk = blk % 2
                ps = self.bank[bk]
                for k in range(KC):
                    sc.op("pe", lambda e, k=k: e.matmul(
                        ps[:, 0:ncol], self.hT_m[:, k, blk * 128:(blk + 1) * 128], slot[:, k, 0:ncol],
                        start=(k == 0), stop=(k == KC - 1)),
                        reads=[rb, self.hTm_b[blk // 4]], writes=[self.bank_b[bk]], signal=(k == KC - 1))
                if isA:
                    self.defer(2, lambda blk=blk, ps=ps, bk=bk: self.evac_A(blk, ps, self.bank_b[bk], vi))
                else:
                    self.defer(2, lambda blk=blk, ps=ps, bk=bk: self.evac_B(blk, ps, self.bank_b[bk], u - 4, vi))
                yield blk

        def stageB(u):
            self.tick_deferred(flush=True)
            isA = u < 4
            if u == 4:
                sc.op("dve", lambda e: e.memset(self.QT[64:128, :], 0.0), writes=[self.QT_b])
            self.unit_transposes(isA, u)

        def att_gen(u):
            vi = u % 2
            if u < 4:
                for h2 in range(2):
                    yield from self.attention(True, hp0=h2 * 64, K=64, vsel=h2, col0=(u * 2 + h2) * 64, vi=vi)
            else:
                yield from self.attention(False, hp0=0, K=70, vsel=0, col0=512 + (u - 4) * 64, vi=vi)

        for _ in inproj_gen(0):
            self.tick_deferred()
        self.tick_deferred(flush=True)
        for p in self.rope_A_pieces():
            p()
        stageB(0)
        for u in range(NU):
            nxt = inproj_gen(u + 1) if u + 1 < NU else None
            nsteps = (2 if u < 4 else 1) * (sum(4 * g + 4 for g in range(NG)) + 5)
            stride = max(2, (nsteps * 55 // 100) // NB) if (u + 1 < 4) else max(2, nsteps // (NB + 1))
            for i, _ in enumerate(att_gen(u)):
                if nxt is not None and i % stride == stride - 1:
                    if next(nxt, None) is None:
                        nxt = None
                        if u + 1 < 4:
                            for pi, p in enumerate(self.rope_A_pieces()):
                                self.defer(4 + 3 * pi, p)
            if nxt is not None:
                for _ in nxt:
                    self.tick_deferred()
                if u + 1 < 4:
                    self.tick_deferred(flush=True)
                    for p in self.rope_A_pieces():
                        p()
            if u + 1 < NU:
                stageB(u + 1)
        self.tick_deferred(flush=True)
        self.phase_sync()
        self.outproj(b)

    def fox_prep(self, b):
        sc = self.sc
        NB = self.NB
        Fb = self.F_b
        winv = self.w_in_d.rearrange("(k p) f -> p k f", p=128)
        sc.dma("pool", self.wf[:], winv[:, :, 3072:3080], writes=[self.wf_b])
        ps = self.bank[2]
        for blk in range(NB):
            for k in range(KC):
                sc.op("pe", lambda e, k=k, blk=blk: e.matmul(ps[:, blk * 8:(blk + 1) * 8], self.hT_m[:, k, blk * 128:(blk + 1) * 128],
                                                            self.wf[:, k, :], start=(k == 0), stop=(k == KC - 1)),
                      reads=[self.wf_b, self.hTm_b[blk // 4]], writes=[self.bank_b[2]], signal=(k == KC - 1))
        psv = ps[:, 0:NB * 8].rearrange("p (a b) -> p a b", a=NB)
        sc.op("dve", lambda e: e.tensor_tensor(out=self.FL[:], in0=psv, in1=self.bforget[:].unsqueeze(1).to_broadcast([128, NB, 8]),
                                               op=ALU.add), reads=[self.small_b], writes=[self.bank_b[2], Fb] + self.rstdn_b)
        sc.op("act", lambda e: e.activation(self.FL2[:], self.FL[:], AF.Exp, scale=-1.0), writes=[Fb])
        sc.op("act", lambda e: e.activation(self.FL[:], self.FL2[:], AF.Ln, bias=1.0, scale=1.0), writes=[Fb])
        wps, tps = self.bank[3], self.bank[4]
        flat = self.FL[:].rearrange("p a b -> p (a b)")
        sc.op("pe", lambda e: e.matmul(wps[:, 0:NB * 8], self.trif[:], flat, start=True, stop=True),
              reads=[Fb, self.small_b], writes=[self.bank_b[3]])
        sc.op("pe", lambda e: e.matmul(tps[:, 0:NB * 8], self.onesf[:], flat, start=True, stop=True),
              reads=[Fb, self.const_b], writes=[self.bank_b[4]])
        sc.op("dve", lambda e: e.tensor_copy(self.FL2[:], tps[:, 0:NB * 8].rearrange("p (a b) -> p a b", a=NB)),
              writes=[self.bank_b[4], Fb])
        sc.op("dve", lambda e: e.memset(self.CARRY[:, 0, :], 0.0), writes=[Fb])
        for i in range(1, NB):
            sc.op("dve", lambda e, i=i: e.tensor_tensor(out=self.CARRY[:, i, :], in0=self.CARRY[:, i - 1, :],
                                                        in1=self.FL2[:, i - 1, :], op=ALU.add), writes=[Fb])
        sc.op("dve", lambda e: e.tensor_tensor(out=self.NEGF[:], in0=wps[:, 0:NB * 8].rearrange("p (a b) -> p a b", a=NB),
                                               in1=self.CARRY[:], op=ALU.add), writes=[self.bank_b[3], Fb])
        V = lambda fn: sc.op("dve", fn, writes=[Fb])
        V(lambda e: e.tensor_copy(self.FH[:], self.NEGF[:]))
        V(lambda e: e.tensor_copy(self.F32T[:], self.FH[:]))
        V(lambda e: e.tensor_tensor(out=self.NEGF[:], in0=self.NEGF[:], in1=self.F32T[:], op=ALU.subtract))
        V(lambda e: e.tensor_copy(self.FM[:], self.NEGF[:]))
        V(lambda e: e.tensor_copy(self.F32T[:], self.FM[:]))
        V(lambda e: e.tensor_tensor(out=self.NEGF[:], in0=self.NEGF[:], in1=self.F32T[:], op=ALU.subtract))
        V(lambda e: e.tensor_copy(self.FLo[:], self.NEGF[:]))

    def _cp(self, eng, out, in_, scale, writes):
        sc = self.sc
        if eng == "act":
            if scale == 1.0:
                sc.op("act", lambda e: e.copy(out, in_), writes=writes)
            else:
                sc.op("act", lambda e: e.activation(out, in_, AF.Copy, scale=scale), writes=writes)
        else:
            if scale == 1.0:
                sc.op("dve", lambda e: e.tensor_copy(out, in_), writes=writes)
            else:
                sc.op("dve", lambda e: e.tensor_scalar(out=out, in0=in_, scalar1=scale, scalar2=None, op0=ALU.mult), writes=writes)

    def evac_A(self, blk, ps, bb, vi):
        eng = "act" if blk % 2 == 0 else "dve"
        qk = self.QKtm_b
        VA, VAb = self.VAs[vi], self.VAs_b[vi]
        self._cp(eng, self.QKtm[:, blk, 0, :], ps[:, 0:128], 0.125, [bb, qk])
        self._cp(eng, self.QKtm[:, blk, 1, :], ps[:, 128:256], 1.0, [bb, qk])
        self._cp(eng, self.ROTF[:, blk, :, :, :], ps[:, 0:256].rearrange("p (q h d) -> p q h d", q=2, h=2)[:, :, :, 0:16], 1.0,
                 [bb, self.rtmp_b])
        self._cp(eng, VA[:, blk, :, 0:64], ps[:, 256:384].rearrange("p (a b) -> p a b", a=2), 1.0, [bb, VAb])

    def rope_A_pieces(self):
        sc = self.sc
        NB, RH = self.NB, self.RH
        rt = self.rtmp_b
        qk = self.QKtm_b
        pieces = []
        for qi, (c2, s1) in enumerate(((self.cq2, self.sq1), (self.ck2, self.sk1))):
            for n0 in range(0, NB, RH):
                def piece(qi=qi, c2=c2, s1=s1, n0=n0):
                    X = self.ROTF[:, n0:n0 + RH, qi, :, :]
                    Dv = self.QKtm[:, n0:n0 + RH, qi, :].rearrange("p n (h d) -> p n h d", h=2)
                    cosb = c2[:, n0:n0 + RH, :].unsqueeze(2).to_broadcast([128, RH, 2, 16])
                    sinb = s1[:, n0:n0 + RH, :].unsqueeze(2).to_broadcast([128, RH, 2, 8])
                    sc.op("dve", lambda e: e.tensor_tensor(out=self.RA[:], in0=X, in1=cosb, op=ALU.mult),
                          reads=[self.rope_b], writes=[rt])
                    sc.op("dve", lambda e: e.tensor_tensor(out=self.RB1[:], in0=X[:, :, :, 8:16], in1=sinb, op=ALU.mult),
                          reads=[self.rope_b], writes=[rt])
                    sc.op("dve", lambda e: e.tensor_tensor(out=self.RB2[:], in0=X[:, :, :, 0:8], in1=sinb, op=ALU.mult),
                          reads=[self.rope_b], writes=[rt])
                    sc.op("dve", lambda e: e.tensor_tensor(out=Dv[:, :, :, 0:8], in0=self.RA[:, :, :, 0:8], in1=self.RB1[:],
                                                           op=ALU.subtract), reads=[rt], writes=[qk])
                    sc.op("dve", lambda e: e.tensor_tensor(out=Dv[:, :, :, 8:16], in0=self.RA[:, :, :, 8:16], in1=self.RB2[:],
                                                           op=ALU.add), reads=[rt], writes=[qk])
                pieces.append(piece)
        return pieces

    def evac_B(self, blk, ps, bb, h, vi):
        sc = self.sc
        qk = self.QKtm_b
        VA, VAb = self.VAs[vi], self.VAs_b[vi]
        eng = "act" if blk % 2 == 0 else "dve"
        self._cp(eng, self.QKtm[:, blk, 0, 0:64], ps[:, 0:64], 0.125, [bb, qk])
        self._cp(eng, self.QKtm[:, blk, 1, 0:64], ps[:, 64:128], 1.0, [bb, qk])
        self._cp(eng, VA[:, blk, 0, 0:64], ps[:, 128:192], 1.0, [bb, VAb])
        if blk == self.NB - 1:
            Fb = self.F_b
            sc.op("dve", lambda e: e.memset(self.QKtm[:, :, 0, 67:70], 1.0), writes=[qk])
            sc.op("dve", lambda e: e.memset(self.QKtm[:, :, 1, 64:67], 1.0), writes=[qk])
            for i, src_t in enumerate((self.FH, self.FM, self.FLo)):
                sc.op("dve", lambda e, i=i, src_t=src_t: e.tensor_scalar(out=self.QKtm[:, :, 0, 64 + i], in0=src_t[:, :, h], scalar1=-1.0,
                                                                       scalar2=None, op0=ALU.mult), reads=[Fb], writes=[qk])
                sc.op("dve", lambda e, i=i, src_t=src_t: e.tensor_copy(self.QKtm[:, :, 1, 67 + i], src_t[:, :, h]), reads=[Fb], writes=[qk])

    def unit_transposes(self, isA, u):
        sc = self.sc
        NB = self.NB
        ncol = 128 if isA else 70
        for g4 in range(NB // 4):
            qb, kb = (6, 7) if g4 % 2 == 0 else (2, 3)
            tq = self.tp_view(qb)
            tk = self.tp_view(kb)
            for qi, (tp, tb) in enumerate(((tq, qb), (tk, kb))):
                for bl in range(4):
                    blk = g4 * 4 + bl
                    sc.op("pe", lambda e, qi=qi, bl=bl, blk=blk, tp=tp: e.transpose(
                        tp[0:ncol, bl * 128:(bl + 1) * 128], self.QKtm[:, blk, qi, 0:ncol], self.identb[:]),
                        reads=[self.QKtm_b, self.const_b], writes=[self.bank_b[tb]], signal=(bl == 3))
            sc.op("act", lambda e, tq=tq, g4=g4: e.copy(self.QT[0:ncol, g4 * 512:(g4 + 1) * 512], tq[0:ncol, 0:512]),
                  writes=[self.bank_b[qb], self.QT_b])
            if isA:
                sc.op("dve", lambda e, tk=tk, g4=g4: e.tensor_copy(self.KT[0:64, g4 * 512:(g4 + 1) * 512], tk[0:64, 0:512]),
                      writes=[self.bank_b[kb], self.KT_b])
                sc.op("dve", lambda e, tk=tk, g4=g4: e.tensor_copy(self.KT1[64:128, g4 * 512:(g4 + 1) * 512], tk[64:128, 0:512]),
                      writes=[self.bank_b[kb], self.KT1_b])
            else:
                sc.op("dve", lambda e, tk=tk, g4=g4: e.tensor_copy(self.KT[0:ncol, g4 * 512:(g4 + 1) * 512], tk[0:ncol, 0:512]),
                      writes=[self.bank_b[kb], self.KT_b])

    def attention(self, isA, hp0, K, vsel, col0, vi):
        KTt, KTb = (self.KT1, self.KT1_b) if (isA and hp0 == 64) else (self.KT, self.KT_b)
        VA, VAb = self.VAs[vi], self.VAs_b[vi]
        SB = (2, 3, 6, 7)
        sc = self.sc
        NG = self.NG
        LAG = 5
        NPT = self.NPT
        tiles = [(g, j) for g in range(NG) for j in range(4 * g + 4)]
        n = len(tiles)

        def emit_qk(t):
            g, j = tiles[t]
            c0 = max(j - 4 * g, 0)
            sbk = SB[t % 4]
            sps = self.bank[sbk]
            PT = self.PT[t % NPT]
            ptb = self.PT_b[t % NPT]
            diag = (not isA) and (j >= 4 * g)
            sc.op("pe", lambda e: e.matmul(
                sps[:, c0 * 128:512], KTt[:, j * 128:(j + 1) * 128],
                self.QT[:, g * 512 + c0 * 128:(g + 1) * 512], start=True, stop=(not diag)),
                reads=[self.QT_b, KTb], writes=[self.bank_b[sbk]], signal=(not diag))
            if diag:
                sc.op("pe", lambda e: e.matmul(
                    sps[:, c0 * 128:(c0 + 1) * 128], self.identb[:], self.negm[:], start=False, stop=True),
                    reads=[self.const_b], writes=[self.bank_b[sbk]], signal=True)
            sc.op("act", lambda e: e.activation(PT[:, c0 * 128:512], sps[:, c0 * 128:512], AF.Exp),
                  writes=[self.bank_b[sbk], ptb])
            if isA:
                d0 = 4 * g + c0 - j
                nblk = 4 - c0
                msk = self.multm[:, d0:d0 + nblk, :].rearrange("p a b -> p (a b)")
                sc.op("dve", lambda e: e.tensor_tensor(
                    out=PT[:, c0 * 128:512], in0=PT[:, c0 * 128:512], in1=msk, op=ALU.mult),
                    reads=[self.const_b], writes=[ptb])

        def emit_pv(t):
            g, j = tiles[t]
            c0 = max(j - 4 * g, 0)
            PT = self.PT[t % NPT]
            ptb = self.PT_b[t % NPT]
            ob = 4 + (g % 2)
            opsv = self.bank[ob][:, 0:260].rearrange("p (a b) -> p a b", a=4)
            for c in range(c0, 4):
                sc.op("pe", lambda e, c=c: e.matmul(
                    opsv[:, c, :], PT[:, c * 128:(c + 1) * 128], VA[:, j, vsel, 0:65],
                    start=(j == 0 and c == 0), stop=(j == 4 * g + c), skip_group_check=True),
                    reads=[ptb, VAb], writes=[self.bank_b[ob]], signal=(c == 3))
            if j == 4 * g + 3:
                def norm(opsv=opsv, ob=ob, g=g):
                    sc.op("dve", lambda e: e.reciprocal(out=self.RDEN[:], in_=opsv[:, :, 64]),
                          writes=[self.bank_b[ob], self.rden_b])
                    sc.op("dve", lambda e: e.tensor_tensor(
                        out=self.merged[:, 4 * g:4 * g + 4, col0:col0 + 64], in0=opsv[:, :, 0:64],
                        in1=self.RDEN[:].unsqueeze(2).to_broadcast([128, 4, 64]), op=ALU.mult),
                        reads=[self.rden_b], writes=[self.bank_b[ob], self.merged_b])
                self.defer(4, norm)

        for t in range(n + LAG):
            if t < n:
                emit_qk(t)
            if t - LAG >= 0:
                emit_pv(t - LAG)
            self.tick_deferred()
            yield t

    def outproj(self, b):
        sc = self.sc
        NB, NG = self.NB, self.NG
        s = 1
        mergedT = self.hT_m
        mtb = self.hTm_b
        ssb = self.ss_b
        sc.op("dve", lambda e: e.memset(self.SSAB[:], 0.0), writes=[ssb])
        for blk in range(NB):
            for grp in range(2):
                sc.op("act", lambda e, blk=blk, grp=grp: e.activation(
                    self.SQJ[:], self.merged[:, blk, grp * 512:(grp + 1) * 512], AF.Square,
                    accum_out=self.SSAB[:, blk, grp:grp + 1]),
                    reads=[self.merged_b], writes=[ssb])
        sc.op("act", lambda e: e.activation(self.RSAB[:], self.SSAB[:], AF.Ln, bias=EPS, scale=1.0 / 512), writes=[ssb])
        sc.op("act", lambda e: e.activation(self.RSAB[:], self.RSAB[:], AF.Exp, scale=-0.5), writes=[ssb])
        gout = self.gains[:, 6, :]
        wov = self.w_out_d.rearrange("(k p) f -> p k f", p=128)
        slots = []
        for hf in range(2):
            si = self.slab([(lambda r: r[:].rearrange("p (k f) -> p k f", k=KC), wov[:, :, hf * 512:(hf + 1) * 512])], keep=hf)
            slot = self.ring[si][:].rearrange("p (k f) -> p k f", k=KC)
            slots.append((slot, self.ring_b[si]))

        def norm_transpose(blk):
            mn = self.MN[blk % 2]
            mnb = self.MN_b[blk % 2]
            sc.op("dve", lambda e: e.tensor_tensor(
                out=mn[:].rearrange("p (a b) -> p a b", a=2), in0=self.merged[:, blk, :].rearrange("p (a b) -> p a b", a=2),
                in1=self.RSAB[:, blk, :].unsqueeze(2).to_broadcast([128, 2, 512]), op=ALU.mult),
                reads=[self.merged_b, ssb], writes=[mnb])
            tb = 6 + (blk % 2)
            tp = self.tp_view(tb)
            for c in range(KC):
                sc.op("pe", lambda e, c=c: e.transpose(tp[:, c * 128:(c + 1) * 128], mn[:, c * 128:(c + 1) * 128], self.identb[:]),
                      reads=[mnb, self.const_b], writes=[self.bank_b[tb]], signal=(c == KC - 1))
            for c in range(KC):
                sc.op("act", lambda e, c=c: e.activation(mergedT[:, c, blk * 128:(blk + 1) * 128], tp[:, c * 128:(c + 1) * 128],
                                                         AF.Copy, scale=gout[:, c:c + 1]),
                      reads=[self.small_b], writes=[self.bank_b[tb], mtb[blk // 4]])

        yi = [0]

        def mm_chunk(sg, c):
            slot, rb = slots[c // 4]
            yb = yi[0] % 4
            yi[0] += 1
            yps = self.bank[yb]
            pnb = 4 + (sg % 2)
            for k in range(KC):
                sc.op("pe", lambda e, k=k: e.matmul(
                    yps[:], slot[:, k, (c % 4) * 128:(c % 4 + 1) * 128], mergedT[:, k, sg * 512:(sg + 1) * 512],
                    start=(k == 0), stop=(k == KC - 1)),
                    reads=[rb, mtb[sg]], writes=[self.bank_b[yb]], signal=(k == KC - 1))
            sq = self.SQ2m[:, c % 4, :]

            def evac():
                sc.op("act", lambda e: e.copy(self.ystage_m[:, c, :], yps[:]), writes=[self.bank_b[yb], self.ysm_b])
                sc.op("act", lambda e: e.activation(sq, yps[:], AF.Square), writes=[self.bank_b[yb], self.SQ2m_b[c % 4]])

            def stat():
                sc.op("pe", lambda e: e.matmul(self.bank[pnb][:], self.onesb[:], sq, start=(c == 0), stop=(c == KC - 1)),
                      reads=[self.SQ2m_b[c % 4], self.const_b], writes=[self.bank_b[pnb]], signal=True)
            return evac, stat

        def pn_sa(sg):
            self.postnorm_update(b, s, sg, self.ystage_m, [self.ysm_b], self.LN2m, self.RSTD2m, self.LN2m_b,
                                 self.UTMPm, self.UTMPm_b, ssbank=4 + (sg % 2))

        for bl in range(4):
            norm_transpose(bl)
        for sg in range(NG):
            if sg + 1 < NG:
                for bl in range(4):
                    norm_transpose((sg + 1) * 4 + bl)
            pend = [mm_chunk(sg, c) for c in range(4)]
            if sg > 0:
                pn_sa(sg - 1)
            prev_stat = None
            for c in range(KC):
                if c < 4:
                    evac, stat = pend[c]
                else:
                    evac, stat = mm_chunk(sg, c)
                evac()
                if prev_stat is not None:
                    prev_stat()
                prev_stat = stat
            prev_stat()
            if sg > 0:
                self.stats_ahead(sg - 1, self.SQ2m, self.SQ2m_b, self.LN2m, self.LN2m_b, ssbank=4 + ((sg - 1) % 2))
        pn_sa(NG - 1)
        self.stats_ahead(NG - 1, self.SQ2m, self.SQ2m_b, self.LN2m, self.LN2m_b, ssbank=4 + ((NG - 1) % 2))


def _consts():
    identf = np.eye(128, dtype=np.float32)
    s_idx = np.arange(128)[:, None]
    t_idx = np.arange(128)[None, :]
    trif = (s_idx <= t_idx).astype(np.float32)
    kk = np.arange(128)[:, None, None]
    DD = np.arange(16)[None, :, None]
    tq = np.arange(128)[None, None, :]
    delta = DD * 128 + tq - kk
    m = ((delta >= 0) & (delta <= 128)).astype(np.float32)
    m += ((delta >= 0) & (delta <= 512) & (delta % 4 == 0)).astype(np.float32)
    m += ((delta >= 0) & (delta <= 2048) & (delta % 16 == 0)).astype(np.float32)
    mult = m.reshape(128, 16 * 128).astype(np.float32)
    neg = np.where(t_idx >= s_idx, 0.0, -30000.0).astype(np.float32)
    inv_freq = (THETA ** (-np.arange(0, ROT, 2, dtype=np.float32) / ROT)).astype(np.float32)
    invfreq = np.ascontiguousarray(np.broadcast_to(inv_freq[None, :], (128, 8))).astype(np.float32)
    return identf, trif, mult, neg, invfreq


def make_in_maps(inputs, S, NSEQ, ncores):
    f = lambda a: np.ascontiguousarray(np.asarray(a))
    x = f(inputs["x"]).astype(np.float32, copy=False)
    c = f(inputs["c"]).astype(np.float32, copy=False)
    pos = f(inputs["positions"]).astype(np.int32, copy=False)
    NB = S // 128
    identf, trif, mult, neg, invfreq = _consts()
    fm = lambda g: np.ascontiguousarray(np.asarray(g, dtype=np.float32).reshape(-1, 128).T)
    gains = np.stack([fm(inputs["g_pre_ff1"][0]), fm(inputs["g_post_ff1"][0]), fm(inputs["g_pre_mix"][0]),
                      fm(inputs["g_post_mix"][0]), fm(inputs["g_pre_ff2"][0]), fm(inputs["g_post_ff2"][0]),
                      fm(np.concatenate([np.asarray(inputs["g_out_a"][0]), np.asarray(inputs["g_out_b"][0])]))], axis=1)
    gains = np.ascontiguousarray(gains.astype(np.float32))
    shared = {
        "w_ada": f(inputs["w_ada"][0]), "b_adaT": fm(inputs["b_ada"][0]), "gains": gains,
        "bforget": np.ascontiguousarray(np.broadcast_to(np.asarray(inputs["b_forget"][0], dtype=np.float32)[None, :], (128, NHB))),
        "invfreq": invfreq,
        "w_ff1_gate": f(inputs["w_ff1_gate"][0]), "w_ff1_up": f(inputs["w_ff1_up"][0]), "w_ff1_down": f(inputs["w_ff1_down"][0]),
        "w_ff2_gate": f(inputs["w_ff2_gate"][0]), "w_ff2_up": f(inputs["w_ff2_up"][0]), "w_ff2_down": f(inputs["w_ff2_down"][0]),
        "w_in": f(inputs["w_in"][0]), "w_out": f(inputs["w_out"][0]),
        "c_identf": identf, "c_trif": trif, "c_mult": mult, "c_neg": neg,
    }
    maps = []
    for ci in range(ncores):
        sl = slice(ci * NSEQ, (ci + 1) * NSEQ)
        m = dict(shared)
        m["x"] = np.ascontiguousarray(x[sl])
        m["cT"] = np.ascontiguousarray(c[sl].reshape(NSEQ, KC, 128).transpose(2, 1, 0))
        m["pos"] = np.ascontiguousarray(pos[sl].reshape(NSEQ, NB, 128).transpose(2, 0, 1))
        maps.append(m)
    return maps


_NC_CACHE = {}


def run(inputs, S, NSEQ, ncores=NCORES, trace=False):
    key = (S, NSEQ)
    if key not in _NC_CACHE:
        _NC_CACHE[key] = Builder(S, NSEQ).build()
    nc = _NC_CACHE[key]
    maps = make_in_maps(inputs, S, NSEQ, ncores)
    res = run_bass_kernel_spmd(nc, maps, core_ids=list(range(ncores)), **({"trace": True} if trace else {}))
    out = np.concatenate([r["out"] for r in res.results], axis=0)
    return out, res


def kernel(**inputs):
    x = np.asarray(inputs["x"])
    Btot, S, _ = x.shape
    NSEQ = Btot // NCORES
    out, _ = run(inputs, S, NSEQ)
    return out.astype(np.float32, copy=False)
```

```python
import contextlib
import math

import numpy as np
import ml_dtypes

import concourse.bass as bass
import concourse.mybir as mybir
from concourse.bass_utils import run_bass_kernel_spmd

F32 = mybir.dt.float32
BF16 = mybir.dt.bfloat16
I32 = mybir.dt.int32
AF = mybir.ActivationFunctionType
ALU = mybir.AluOpType

D = 1024
KC = 8
DFF = 2752
FCH = 22
HD = 64
NHA = 8
NHB = 8
INCOLS = 3080
EPS = 1e-6
NCORES = 8
ROT = 16
THETA = 500000.0

MAGIC = 12582912.0
TWO_PI_HI = 6.28125
TWO_PI_LO = float(np.float32(2 * math.pi - 6.28125))
PI_SAFE = 3.1415925


class Buf:
    __slots__ = ("name", "w", "r", "dsem", "dcount")

    def __init__(self, name):
        self.name = name
        self.w = {}
        self.r = {}
        self.dsem = None
        self.dcount = 0


class Eng:
    def __init__(self, name, h, sem):
        self.name = name
        self.h = h
        self.sem = sem
        self.ticks = 0
        self.seen = {}
        self.stream = []


class Sched:
    def __init__(self, nc, es):
        self.nc = nc
        self.es = es
        self.sems = {}
        self.eng = {}
        for name, h in (("pe", nc.tensor), ("act", nc.scalar), ("dve", nc.vector),
                        ("pool", nc.gpsimd), ("sp", nc.sync)):
            sem = es.enter_context(nc.semaphore("e_" + name))
            self.eng[name] = Eng(name, h, sem)
            self.sems[("eng", name)] = sem
        self.nbuf = 0
        self.dry = False

    def buf(self, name):
        self.nbuf += 1
        return Buf(f"{name}_{self.nbuf}")

    def _deps(self, en, reads, writes):
        raw = {}
        war = {}
        for b in reads:
            for k, v in b.w.items():
                if raw.get(k, 0) < v:
                    raw[k] = v
        for b in writes:
            for k, v in b.w.items():
                if raw.get(k, 0) < v:
                    raw[k] = v
            for k, v in b.r.items():
                if war.get(k, 0) < v:
                    war[k] = v
        own = ("eng", en)
        deps = dict(raw)
        if en == "pe":
            deps.pop(own, None)
        for k, v in war.items():
            if k == own:
                continue
            if deps.get(k, 0) < v:
                deps[k] = v
        return deps

    def _emit_waits(self, E, deps):
        for k, v in deps.items():
            if E.seen.get(k, 0) >= v:
                continue
            E.h.wait_ge(self.sems[k], v)
            E.seen[k] = v
            E.stream.append(("wait", k, v))

    def op(self, en, fn, reads=(), writes=(), signal=True):
        if self.dry:
            return None
        E = self.eng[en]
        deps = self._deps(en, reads, writes)
        self._emit_waits(E, deps)
        inst = fn(E.h)
        key = ("eng", en)
        if signal:
            E.ticks += 1
            inst.then_inc(E.sem, 1)
            tick = E.ticks
            E.stream.append(("op", key, 1))
        else:
            tick = E.ticks + 1
            E.stream.append(("op", None, 0))
        for b in writes:
            if b.w.get(key, 0) < tick:
                b.w[key] = tick
        for b in reads:
            if b.r.get(key, 0) < tick:
                b.r[key] = tick
        return inst

    def dma(self, qn, out, in_, reads=(), writes=()):
        if self.dry:
            return None
        Q = self.eng[qn]
        deps = self._deps(qn, reads, writes)
        self._emit_waits(Q, deps)
        prim = writes[0] if writes else reads[0]
        if prim.dsem is None:
            prim.dsem = self.es.enter_context(self.nc.semaphore("d_" + prim.name))
            self.sems[("dma", prim.name)] = prim.dsem
        prim.dcount += 16
        key = ("dma", prim.name)
        Q.h.dma_start(out=out, in_=in_).then_inc(prim.dsem, 16)
        Q.stream.append(("op", key, 16))
        for b in writes:
            b.w[key] = prim.dcount
        for b in reads:
            b.r[key] = prim.dcount

    def final_wait(self, en, bufs):
        E = self.eng[en]
        deps = {}
        for b in bufs:
            for k, v in list(b.w.items()) + list(b.r.items()):
                if deps.get(k, 0) < v:
                    deps[k] = v
        deps.pop(("eng", en), None)
        self._emit_waits(E, deps)

    def check_deadlock(self):
        cnt = {}
        ptr = {n: 0 for n in self.eng}
        pending_fwd = {n: False for n in self.eng}
        progress = True
        while progress:
            progress = False
            for n, E in self.eng.items():
                while ptr[n] < len(E.stream):
                    kind, k, v = E.stream[ptr[n]]
                    if kind == "wait":
                        if cnt.get(k, 0) >= v:
                            ptr[n] += 1
                            progress = True
                        else:
                            break
                    else:
                        if k is not None:
                            cnt[k] = cnt.get(k, 0) + v
                        ptr[n] += 1
                        progress = True
        stuck = {n: (ptr[n], len(E.stream), E.stream[ptr[n]] if ptr[n] < len(E.stream) else None)
                 for n, E in self.eng.items() if ptr[n] < len(E.stream)}
        if stuck:
            raise RuntimeError(f"semaphore deadlock in generated program: {stuck} cnt={ {k: cnt.get(k) for _, (_, _, s) in stuck.items() for k in [s[1]]} }")


class Builder:
    def __init__(self, S, NSEQ):
        assert S % 512 == 0
        self.S = S
        self.NSEQ = NSEQ
        self.NB = S // 128
        self.NG = S // 512
        self.T = min(1024, S)
        self.NTG = S // self.T
        self.SGT = self.T // 512

    def sb(self, name, shape, dt):
        return self.es.enter_context(self.nc.sbuf_tensor(name, shape, dt))

    def build(self):
        nc = bass.Bass("TRN2", target_bir_lowering=False)
        self.nc = nc
        S, NSEQ, NB, NG = self.S, self.NSEQ, self.NB, self.NG
        dr = lambda name, shape, dt, kind="ExternalInput": nc.dram_tensor(name, shape, dt, kind=kind).ap()
        self.x_d = dr("x", [NSEQ, S, D], F32)
        self.cT_d = dr("cT", [128, KC, NSEQ], F32)
        self.pos_d = dr("pos", [128, NSEQ, NB], I32)
        self.w_ada_d = dr("w_ada", [D, 9 * D], F32)
        self.b_adaT_d = dr("b_adaT", [128, 72], F32)
        self.gains_d = dr("gains", [128, 7, KC], F32)
        self.bforget_d = dr("bforget", [128, NHB], F32)
        self.invfreq_d = dr("invfreq", [128, 8], F32)
        self.wg_d = [dr("w_ff1_gate", [D, DFF], F32), dr("w_ff2_gate", [D, DFF], F32)]
        self.wu_d = [dr("w_ff1_up", [D, DFF], F32), dr("w_ff2_up", [D, DFF], F32)]
        self.wd_d = [dr("w_ff1_down", [DFF, D], F32), dr("w_ff2_down", [DFF, D], F32)]
        self.w_in_d = dr("w_in", [D, INCOLS], F32)
        self.w_out_d = dr("w_out", [D, D], F32)
        self.c_identf_d = dr("c_identf", [128, 128], F32)
        self.c_trif_d = dr("c_trif", [128, 128], F32)
        self.c_mult_d = dr("c_mult", [128, 16 * 128], F32)
        self.c_neg_d = dr("c_neg", [128, 128], F32)
        self.out_d = dr("out", [NSEQ, S, D], F32, kind="ExternalOutput")

        with contextlib.ExitStack() as es:
            self.es = es
            self.sc = Sched(nc, es)
            self.alloc()
            self.slab_plan = []
            self.deferred = []
            for dry in (True, False):
                self.sc.dry = dry
                self.slab_count = 0
                self.slab_emitted = 0
                self.setup()
                for b in range(NSEQ):
                    self.sequence(b)
            self.sc.final_wait("sp", self.stage_b)
            self.sc.check_deadlock()
        return nc

    def alloc(self):
        nc, sc = self.nc, self.sc
        S, NSEQ, NB, T = self.S, self.NSEQ, self.NB, self.T
        sb = self.sb
        B = sc.buf
        self.xT = sb("xT", [128, KC, S], F32)
        self.xT_b = [B("xT") for _ in range(self.NG)]
        self.identb = sb("identb", [128, 128], BF16)
        self.identf = sb("identf", [128, 128], F32)
        self.onesb = sb("onesb", [128, 128], BF16)
        self.onesf = sb("onesf", [128, 128], F32)
        self.trif = sb("trif", [128, 128], F32)
        self.multm = sb("multm", [128, 16, 128], BF16)
        self.negm = sb("negm", [128, 128], BF16)
        self.const_b = B("const")
        self.cT = sb("cT_sb", [128, KC, NSEQ], F32)
        self.scT = sb("scT", [128, KC, NSEQ], BF16)
        self.sctmp = sb("sctmp", [128, KC, NSEQ], F32)
        self.modT = sb("modT", [128, 72, NSEQ], F32)
        self.b_adaT = sb("b_adaT_sb", [128, 72], F32)
        self.gains = sb("gains_sb", [128, 7, KC], F32)
        self.mods = sb("mods", [128, 9, KC, NSEQ], F32)
        self.bforget = sb("bforget_sb", [128, NHB], F32)
        self.invfreq = sb("invfreq_sb", [128, 8], F32)
        self.pos_i = sb("pos_i", [128, NSEQ, NB], I32)
        self.small_b = B("small")
        self.mod_b = B("mod")
        self.cq2 = sb("cq2", [128, NB, 16], F32)
        self.sq1 = sb("sq1", [128, NB, 8], F32)
        self.ck2 = sb("ck2", [128, NB, 16], F32)
        self.sk1 = sb("sk1", [128, NB, 8], F32)
        self.rope_b = B("rope")
        self.ropetmp_b = B("ropetmp")
        self.NSLOT = 3
        self.ring = [sb(f"ring{i}", [128, 4096], BF16) for i in range(self.NSLOT)]
        self.ring_b = [B(f"ring{i}") for i in range(self.NSLOT)]
        self.ring_i = 0
        self.wf = sb("wf", [128, KC, 8], BF16)
        self.wf_b = B("wf")
        REG = (int(self.nc.sbuf_bytes_remaining) - 256) // 256 * 256
        self.region = sb("region", [128, REG // 4], F32)
        self.REG = REG

        def carve(off, shape, dt):
            esz = 2 if dt == BF16 else 4
            n = int(np.prod(shape[1:]))
            assert off % 4 == 0 and off + n * esz <= REG, (off, shape, REG)
            ap = self.region[:, off // 4: off // 4 + (n * esz) // 4]
            if dt != F32:
                ap = ap.bitcast(dt)
            if len(shape) == 3:
                ap = ap.rearrange("p (a b) -> p a b", a=shape[1])
            elif len(shape) == 4:
                ap = ap.rearrange("p (a b c) -> p a b c", a=shape[1], b=shape[2])
            elif len(shape) == 5:
                ap = ap.rearrange("p (a b c d) -> p a b c d", a=shape[1], b=shape[2], c=shape[3])
            return ap, off + n * esz

        self.carve = carve
        SGT = self.SGT
        o = 0
        self.aT, o = carve(o, [128, FCH, T], BF16)
        self.aT_b = [B("aT") for _ in range(SGT)]
        r1 = o
        self.hT_f, o = carve(o, [128, KC, T], BF16)
        self.hTf_b = [B("hTf") for _ in range(SGT)]
        self.SQ, o = carve(o, [128, 4, 512], BF16)
        self.SQ_b = B("SQ")
        self.E = []
        for i in range(2):
            e, o = carve(o, [128, 512], F32)
            self.E.append(e)
        self.E_b = [B("E0"), B("E1")]
        self.LN, o = carve(o, [128, 512], F32)
        self.RSTD, o = carve(o, [128, 512], F32)
        self.LN_b = B("LN")
        self.HTMP = []
        for i in range(2):
            e, o = carve(o, [128, 512], F32)
            self.HTMP.append(e)
        self.HTMP_b = [B("HT0"), B("HT1")]
        self.ystage_f, o2 = carve(r1, [128, KC, T], F32)
        o = max(o, o2)
        self.R1_bufs = self.hTf_b + [self.SQ_b, self.LN_b] + self.E_b + self.HTMP_b
        self.SQ2, o = carve(o, [128, 4, 512], BF16)
        self.SQ2_b = [B("SQ2") for _ in range(4)]
        self.LN2, o = carve(o, [128, 512], F32)
        self.RSTD2, o = carve(o, [128, 512], F32)
        self.LN2_b = B("LN2")
        self.UTMP = []
        for i in range(2):
            e, o = carve(o, [128, 512], F32)
            self.UTMP.append(e)
        self.UTMP_b = [B("UT0"), B("UT1")]
        self.ffn_end = o
        o = 0
        self.merged, o = carve(o, [128, NB, D], BF16)
        self.merged_b = B("merged")
        self.hT_m, o = carve(o, [128, KC, S], BF16)
        self.hTm_b = [B("hTm") for _ in range(self.NG)]
        m_un = o
        self.QT, o = carve(o, [128, S], BF16)
        self.KT, o = carve(o, [128, S], BF16)
        self.KT1, o = carve(o, [128, S], BF16)
        self.VAs = []
        for i in range(2):
            e, o = carve(o, [128, NB, 2, 66], BF16)
            self.VAs.append(e)
        self.VA = self.VAs[0]
        self.QKtm, o = carve(o, [128, NB, 2, 128], BF16)
        self.NPT = 7
        self.PT = []
        for i in range(self.NPT):
            e, o = carve(o, [128, 512], BF16)
            self.PT.append(e)
        self.PT_b = [B("PT") for _ in range(self.NPT)]
        rotf_off = o
        self.ROTF, o = carve(o, [128, NB, 2, 2, 16], F32)
        self.RH = max(NB // 2, 1)
        self.RA, o = carve(o, [128, self.RH, 2, 16], F32)
        self.RB1, o = carve(o, [128, self.RH, 2, 8], F32)
        self.RB2, o = carve(o, [128, self.RH, 2, 8], F32)
        self.RDEN, o = carve(o, [128, 4], F32)
        m1_end = o
        self.QT_b, self.KT_b, self.QKtm_b = B("QT"), B("KT"), B("QKtm")
        self.VAs_b = [B("VA0"), B("VA1")]
        self.KT1_b = B("KT1")
        self.rtmp_b = B("rtmp")
        self.rden_b = B("rden")
        o = m_un
        self.SQm, o = carve(o, [128, 4, 512], BF16)
        self.LNm, o = carve(o, [128, 512], F32)
        self.RSTDm, o = carve(o, [128, 512], F32)
        self.HTMPm = []
        for i in range(2):
            e, o = carve(o, [128, 512], F32)
            self.HTMPm.append(e)
        self.SQm_b, self.LNm_b, self.HTMPm_b = B("SQm"), B("LNm"), [B("HTm0"), B("HTm1")]
        m0_end = o
        o = m_un
        self.ystage_m, o = carve(o, [128, KC, 512], F32)
        self.ysm_b = [B("ysm") for _ in range(KC)]
        self.MN = []
        for i in range(2):
            e, o = carve(o, [128, D], BF16)
            self.MN.append(e)
        self.MN_b = [B("MN0"), B("MN1")]
        self.SSAB, o = carve(o, [128, NB, 2], F32)
        self.RSAB, o = carve(o, [128, NB, 2], F32)
        self.SQJ, o = carve(o, [128, 512], BF16)
        self.ss_b = B("ssab")
        self.SQ2m, o = carve(o, [128, 4, 512], BF16)
        self.SQ2m_b = [B("SQ2m") for _ in range(4)]
        self.LN2m, o = carve(o, [128, 512], F32)
        self.RSTD2m, o = carve(o, [128, 512], F32)
        self.LN2m_b = B("LN2m")
        self.UTMPm = []
        for i in range(2):
            e, o = carve(o, [128, 512], F32)
            self.UTMPm.append(e)
        self.UTMPm_b = [B("UTm0"), B("UTm1")]
        m2_end = o
        alias_f = rotf_off >= m0_end
        o = rotf_off if alias_f else max(m1_end, m0_end)
        self.FL, o = carve(o, [128, NB, 8], F32)
        self.FL2, o = carve(o, [128, NB, 8], F32)
        self.NEGF, o = carve(o, [128, NB, 8], F32)
        self.CARRY, o = carve(o, [128, NB, 8], F32)
        self.F32T, o = carve(o, [128, NB, 8], F32)
        if alias_f:
            assert o <= rotf_off + NB * 2 * 2 * 16 * 4
            o = max(m1_end, m0_end)
        self.FH, o = carve(o, [128, NB, 8], BF16)
        self.FM, o = carve(o, [128, NB, 8], BF16)
        self.FLo, o = carve(o, [128, NB, 8], BF16)
        self.F_b = B("F")
        self.aug_b = B("aug")
        self.mix_end = max(o, m2_end)
        o = max(m2_end, self.ffn_end)
        self.RSTDN, o = carve(o, [128, self.NG, 512], F32)
        self.rstdn_b = [B("rstdn") for _ in range(self.NG)]
        self.layout_info = dict(REG=REG, ffn_end=self.ffn_end, m1_end=m1_end, m0_end=m0_end, m2_end=m2_end, mix_end=self.mix_end)
        self.NSTAGE = 4
        self.stage = []
        o = 0
        for i in range(self.NSTAGE):
            e, o = carve(o, [128, D], F32)
            self.stage.append(e)
        self.stage_b = [B(f"stage{i}") for i in range(self.NSTAGE)]
        self.rt = []
        for i in range(6):
            e, o = carve(o, [128, NB, 8], F32)
            self.rt.append(e)
        self.region_b = B("region")
        self.bank = [self.es.enter_context(nc.psum_tensor(f"bank{i}", [128, 512], F32)) for i in range(8)]
        self.bank_b = [B(f"bank{i}") for i in range(8)]

    def defer(self, delay, fn):
        self.deferred.append([delay, fn])

    def tick_deferred(self, flush=False):
        keep = []
        for item in self.deferred:
            item[0] -= 1
            if flush or item[0] <= 0:
                item[1]()
            else:
                keep.append(item)
        self.deferred = keep

    def tp_view(self, i):
        return self.bank[i][:].bitcast(BF16)

    def phase_sync(self):
        sc = self.sc
        if sc.dry:
            return
        snap = {("eng", n): E.ticks for n, E in sc.eng.items() if E.ticks > 0}
        for k, v in self.region_b.w.items():
            if k[0] == "dma":
                snap[k] = max(snap.get(k, 0), v)
        for n, E in sc.eng.items():
            deps = dict(snap)
            deps.pop(("eng", n), None)
            sc._emit_waits(E, deps)

    def slab(self, parts, keep=0):
        idx = self.slab_count
        self.slab_count += 1
        if self.sc.dry:
            self.slab_plan.append(parts)
            return idx % self.NSLOT
        plan = self.slab_plan
        while self.slab_emitted <= min(idx + self.NSLOT - 1 - keep, len(plan) - 1):
            i = self.slab_emitted
            si = i % self.NSLOT
            for dst_fn, srcap in plan[i]:
                self.sc.dma("pool", dst_fn(self.ring[si]), srcap, writes=[self.ring_b[si]])
            self.slab_emitted += 1
        return idx % self.NSLOT

    def setup(self):
        nc, sc = self.nc, self.sc
        NSEQ = self.NSEQ
        small = self.small_b
        for dst, src in ((self.cT, self.cT_d), (self.b_adaT, self.b_adaT_d), (self.gains, self.gains_d),
                         (self.bforget, self.bforget_d), (self.invfreq, self.invfreq_d),
                         (self.pos_i, self.pos_d), (self.identf, self.c_identf_d), (self.trif, self.c_trif_d)):
            sc.dma("sp", dst[:], src, writes=[small])
        cb = self.const_b
        sc.dma("pool", self.identb[:], self.c_identf_d, writes=[cb])
        sc.dma("pool", self.multm[:].rearrange("p a b -> p (a b)"), self.c_mult_d, writes=[cb])
        sc.dma("pool", self.negm[:], self.c_neg_d, writes=[cb])
        sc.op("dve", lambda e: e.memset(self.onesb[:], 1.0), writes=[cb])
        sc.op("dve", lambda e: e.memset(self.onesf[:], 1.0), writes=[cb])
        mb = self.mod_b
        sc.op("act", lambda e: e.activation(self.sctmp[:], self.cT[:], AF.Exp, scale=-1.0), reads=[small], writes=[mb])
        sc.op("dve", lambda e: e.tensor_scalar(out=self.sctmp[:], in0=self.sctmp[:], scalar1=1.0, scalar2=None, op0=ALU.add), writes=[mb])
        sc.op("dve", lambda e: e.reciprocal(out=self.sctmp[:], in_=self.sctmp[:]), writes=[mb])
        sc.op("dve", lambda e: e.tensor_tensor(out=self.scT[:], in0=self.cT[:], in1=self.sctmp[:], op=ALU.mult), reads=[small], writes=[mb])
        wv = self.w_ada_d.rearrange("(k p) f -> p k f", p=128)
        for sl in range(18):
            si = self.slab([(lambda r: r[:].rearrange("p (k f) -> p k f", k=KC), wv[:, :, sl * 512:(sl + 1) * 512])])
            slot = self.ring[si][:].rearrange("p (k f) -> p k f", k=KC)
            bk = sl % 2
            ps = self.bank[bk]
            for sub in range(4):
                for k in range(KC):
                    sc.op("pe", lambda e, k=k, sub=sub: e.matmul(ps[:, sub * NSEQ:(sub + 1) * NSEQ],
                                                                   slot[:, k, sub * 128:(sub + 1) * 128],
                                                                   self.scT[:, k, :], start=(k == 0), stop=(k == KC - 1)),
                          reads=[self.ring_b[si], mb], writes=[self.bank_b[bk]], signal=(k == KC - 1))
            sc.op("dve", lambda e, sl=sl, ps=ps: e.tensor_tensor(
                out=self.modT[:, sl * 4:(sl + 1) * 4, :],
                in0=ps[:, 0:4 * NSEQ].rearrange("p (a b) -> p a b", a=4),
                in1=self.b_adaT[:, sl * 4:(sl + 1) * 4].unsqueeze(2).to_broadcast([128, 4, NSEQ]), op=ALU.add),
                reads=[small], writes=[self.bank_b[bk], mb])
        gi_pre = (0, 2, 4)
        gi_post = (1, 3, 5)
        for s in range(3):
            m_shift = self.modT[:, (3 * s) * 8:(3 * s + 1) * 8, :]
            m_scale = self.modT[:, (3 * s + 1) * 8:(3 * s + 2) * 8, :]
            m_gate = self.modT[:, (3 * s + 2) * 8:(3 * s + 3) * 8, :]
            gpre = self.gains[:, gi_pre[s], :].unsqueeze(2).to_broadcast([128, KC, NSEQ])
            gpost = self.gains[:, gi_post[s], :].unsqueeze(2).to_broadcast([128, KC, NSEQ])
            sc.op("dve", lambda e, m_scale=m_scale, gpre=gpre, s=s: e.scalar_tensor_tensor(
                out=self.mods[:, 3 * s + 0, :, :], in0=m_scale, scalar=1.0, in1=gpre, op0=ALU.add, op1=ALU.mult),
                reads=[small], writes=[mb])
            sc.op("dve", lambda e, m_shift=m_shift, s=s: e.tensor_copy(self.mods[:, 3 * s + 1, :, :], m_shift), writes=[mb])
            sc.op("dve", lambda e, m_gate=m_gate, gpost=gpost, s=s: e.scalar_tensor_tensor(
                out=self.mods[:, 3 * s + 2, :, :], in0=m_gate, scalar=(0.5 if s != 1 else 1.0), in1=gpost,
                op0=ALU.mult, op1=ALU.mult), reads=[small], writes=[mb])

    def sequence(self, b):
        self.phase_sync()
        self.load_x(b)
        self.rope_tables(b)
        self.phase_sync()
        self.ffn(b, 0, 0)
        self.phase_sync()
        self.mixer(b)
        self.phase_sync()
        self.ffn(b, 2, 1)
        self.phase_sync()
        self.store_x(b)

    def load_x(self, b):
        sc = self.sc
        for blk in range(self.NB):
            st = blk % self.NSTAGE
            sc.dma("sp", self.stage[st][:], self.x_d[b, blk * 128:(blk + 1) * 128, :], writes=[self.stage_b[st]])
            for half in range(2):
                bk = 2 * (blk % 2) + half
                for cc in range(4):
                    c = half * 4 + cc
                    sc.op("pe", lambda e, c=c, cc=cc, st=st, bk=bk: e.transpose(
                        self.bank[bk][:, cc * 128:(cc + 1) * 128], self.stage[st][:, c * 128:(c + 1) * 128], self.identf[:]),
                        reads=[self.stage_b[st], self.small_b], writes=[self.bank_b[bk]], signal=(cc == 3))
                src = self.bank[bk][:].rearrange("p (a b) -> p a b", a=4)
                dst = self.xT[:, half * 4:(half + 1) * 4, blk * 128:(blk + 1) * 128]
                if half == 0:
                    sc.op("act", lambda e, dst=dst, src=src: e.copy(dst, src), writes=[self.bank_b[bk], self.xT_b[blk // 4]])
                else:
                    sc.op("dve", lambda e, dst=dst, src=src: e.tensor_copy(dst, src), writes=[self.bank_b[bk], self.xT_b[blk // 4]])
            if blk % 4 == 3:
                self.stats_ahead(blk // 4, self.SQ2, self.SQ2_b, self.LN2, self.LN2_b, ssbank=4)

    def store_x(self, b):
        sc = self.sc
        for blk in range(self.NB):
            st = blk % self.NSTAGE
            for half in range(2):
                bk = 2 * (blk % 2) + half
                for cc in range(4):
                    c = half * 4 + cc
                    sc.op("pe", lambda e, c=c, cc=cc, bk=bk, blk=blk: e.transpose(
                        self.bank[bk][:, cc * 128:(cc + 1) * 128], self.xT[:, c, blk * 128:(blk + 1) * 128], self.identf[:]),
                        reads=[self.xT_b[blk // 4], self.small_b], writes=[self.bank_b[bk]], signal=(cc == 3))
                dst = self.stage[st][:, half * 512:(half + 1) * 512]
                src = self.bank[bk][:]
                if half == 0:
                    sc.op("act", lambda e, dst=dst, src=src: e.copy(dst, src), writes=[self.bank_b[bk], self.stage_b[st]])
                else:
                    sc.op("dve", lambda e, dst=dst, src=src: e.tensor_copy(dst, src), writes=[self.bank_b[bk], self.stage_b[st]])
            sc.dma("sp", self.out_d[b, blk * 128:(blk + 1) * 128, :], self.stage[st][:], reads=[self.stage_b[st]])
        for st in range(self.NSTAGE):
            for k, v in list(self.stage_b[st].r.items()) + list(self.stage_b[st].w.items()):
                self.region_b.w[k] = max(self.region_b.w.get(k, 0), v)

    def rope_tables(self, b):
        sc = self.sc
        NB = self.NB
        rb, tb = self.rope_b, self.ropetmp_b
        posf, ang, t, k, r1, r2 = self.rt
        V = lambda fn, reads=(), writes=(): sc.op("dve", fn, reads=reads, writes=writes)
        V(lambda e: e.tensor_copy(posf[:], self.pos_i[:, b, :].unsqueeze(2).to_broadcast([128, NB, 8])),
          reads=[self.small_b], writes=[tb])
        V(lambda e: e.tensor_tensor(out=ang[:], in0=posf[:], in1=self.invfreq[:].unsqueeze(1).to_broadcast([128, NB, 8]),
                                    op=ALU.mult), reads=[self.small_b], writes=[tb])
        for which in ("sin", "cos"):
            src = ang
            if which == "cos":
                V(lambda e: e.tensor_scalar(out=posf[:], in0=ang[:], scalar1=math.pi / 2, scalar2=None, op0=ALU.add), writes=[tb])
                src = posf
            V(lambda e, src=src: e.tensor_scalar(out=t[:], in0=src[:], scalar1=1.0 / (2 * math.pi), scalar2=MAGIC,
                                                 op0=ALU.mult, op1=ALU.add), writes=[tb])
            V(lambda e: e.tensor_scalar(out=k[:], in0=t[:], scalar1=MAGIC, scalar2=None, op0=ALU.subtract), writes=[tb])
            V(lambda e, src=src: e.scalar_tensor_tensor(out=r1[:], in0=k[:], scalar=-TWO_PI_HI, in1=src[:],
                                                        op0=ALU.mult, op1=ALU.add), writes=[tb])
            V(lambda e: e.scalar_tensor_tensor(out=r2[:], in0=k[:], scalar=-TWO_PI_LO, in1=r1[:],
                                               op0=ALU.mult, op1=ALU.add), writes=[tb])
            V(lambda e: e.tensor_scalar(out=r2[:], in0=r2[:], scalar1=PI_SAFE, scalar2=-PI_SAFE, op0=ALU.min, op1=ALU.max),
              writes=[tb])
            if which == "sin":
                sc.op("act", lambda e: e.activation(self.sk1[:], r2[:], AF.Sin), reads=[tb], writes=[rb])
                sc.op("dve", lambda e: e.tensor_scalar(out=self.sq1[:], in0=self.sk1[:], scalar1=0.125, scalar2=None,
                                                       op0=ALU.mult), writes=[rb])
            else:
                for hf in range(2):
                    sc.op("act", lambda e, hf=hf: e.activation(self.ck2[:, :, hf * 8:(hf + 1) * 8], r2[:], AF.Sin),
                          reads=[tb], writes=[rb])
                sc.op("dve", lambda e: e.tensor_scalar(out=self.cq2[:], in0=self.ck2[:], scalar1=0.125, scalar2=None,
                                                       op0=ALU.mult), writes=[rb])

    def stats_ahead(self, sg, SQ, SQ_b, LN, LN_b, ssbank):
        sc = self.sc
        t0 = sg * 512
        xb = self.xT_b[sg]
        bb = self.bank_b[ssbank]
        ps = self.bank[ssbank]
        for half in range(2):
            for cc in range(4):
                c = half * 4 + cc
                if cc % 2 == 0:
                    sc.op("act", lambda e, c=c, cc=cc: e.activation(SQ[:, cc, :], self.xT[:, c, t0:t0 + 512], AF.Square),
                          reads=[xb], writes=[SQ_b[cc]])
                else:
                    sc.op("dve", lambda e, c=c, cc=cc: e.tensor_tensor(out=SQ[:, cc, :], in0=self.xT[:, c, t0:t0 + 512],
                                                                       in1=self.xT[:, c, t0:t0 + 512], op=ALU.mult),
                          reads=[xb], writes=[SQ_b[cc]])
            for cc in range(4):
                c = half * 4 + cc
                sc.op("pe", lambda e, c=c, cc=cc: e.matmul(ps[:], self.onesb[:], SQ[:, cc, :], start=(c == 0), stop=(c == KC - 1)),
                      reads=[SQ_b[cc], self.const_b], writes=[bb], signal=(cc == 3))
        sc.op("act", lambda e: e.activation(LN[:], ps[:], AF.Ln, bias=EPS, scale=1.0 / D), writes=[bb, LN_b])
        sc.op("act", lambda e: e.activation(self.RSTDN[:, sg, :], LN[:], AF.Exp, scale=-0.5), reads=[LN_b], writes=[self.rstdn_b[sg]])

    def prenorm_apply(self, b, s, sg, hT_dst, hT_buf, HTMP, HTMP_b):
        sc = self.sc
        t0 = sg * 512
        xb = self.xT_b[sg]
        G = self.mods[:, 3 * s + 0, :, :]
        Sh = self.mods[:, 3 * s + 1, :, :]
        for c in range(KC):
            tmp = HTMP[c % 2]
            tb = HTMP_b[c % 2]
            sc.op("dve", lambda e, c=c, tmp=tmp: e.tensor_tensor(
                out=tmp[:], in0=self.xT[:, c, t0:t0 + 512], in1=self.RSTDN[:, sg, :], op=ALU.mult),
                reads=[xb, self.rstdn_b[sg]], writes=[tb])
            sc.op("act", lambda e, c=c, tmp=tmp: e.activation(hT_dst[:, c, :], tmp[:], AF.Identity,
                                                               bias=Sh[:, c, b:b + 1], scale=G[:, c, b:b + 1]),
                  reads=[self.mod_b, tb], writes=[hT_buf])

    def postnorm_update(self, b, s, sg, ystage, ybufs, LN2, RSTD2, LN2_b, UTMP, UTMP_b, ssbank):
        sc = self.sc
        t0 = sg * 512
        xb = self.xT_b[sg]
        bb = self.bank_b[ssbank]
        ps = self.bank[ssbank]
        sc.op("act", lambda e: e.activation(LN2[:], ps[:], AF.Ln, bias=EPS, scale=1.0 / D), writes=[bb, LN2_b])
        sc.op("act", lambda e: e.activation(RSTD2[:], LN2[:], AF.Exp, scale=-0.5), writes=[LN2_b])
        Gp = self.mods[:, 3 * s + 2, :, :]
        for c in range(KC):
            tmp = UTMP[c % 2]
            tb = UTMP_b[c % 2]
            sc.op("dve", lambda e, c=c, tmp=tmp: e.tensor_tensor(out=tmp[:], in0=ystage[:, c, :], in1=RSTD2[:], op=ALU.mult),
                  reads=list(ybufs) + [LN2_b], writes=[tb])
            sc.op("dve", lambda e, c=c, tmp=tmp: e.tensor_tensor(
                out=self.xT[:, c, t0:t0 + 512], in0=tmp[:], in1=self.xT[:, c, t0:t0 + 512], op=ALU.add),
                reads=[tb], writes=[xb])

    def ffn(self, b, s, wi):
        sc = self.sc
        T, SGT = self.T, self.SGT
        wgv = self.wg_d[wi].rearrange("(k p) f -> p k f", p=128)
        wuv = self.wu_d[wi].rearrange("(k p) f -> p k f", p=128)
        wd = self.wd_d[wi]
        R1 = self.R1_bufs
        for tg in range(self.NTG):
            sg0 = tg * SGT
            for sgl in range(SGT):
                self.prenorm_apply(b, s, sg0 + sgl, self.hT_f[:, :, sgl * 512:(sgl + 1) * 512], self.hTf_b[sgl],
                                   self.HTMP, self.HTMP_b)
            nslab = (DFF + 255) // 256
            gi = 0
            for sl in range(nslab):
                c0 = sl * 256
                ncol = min(256, DFF - c0)
                si = self.slab([
                    (lambda r, ncol=ncol: r[:, 0:2048].rearrange("p (k f) -> p k f", k=KC)[:, :, 0:ncol], wgv[:, :, c0:c0 + ncol]),
                    (lambda r, ncol=ncol: r[:, 2048:4096].rearrange("p (k f) -> p k f", k=KC)[:, :, 0:ncol], wuv[:, :, c0:c0 + ncol])])
                rb = self.ring_b[si]
                slotg = self.ring[si][:, 0:2048].rearrange("p (k f) -> p k f", k=KC)
                slotu = self.ring[si][:, 2048:4096].rearrange("p (k f) -> p k f", k=KC)
                nfc = (ncol + 127) // 128
                for sgl in range(SGT):
                    hb = self.hTf_b[sgl]
                    for fcl in range(nfc):
                        fc = sl * 2 + fcl
                        fr = min(128, ncol - fcl * 128)
                        gb, ub = (gi % 2), 2 + (gi % 2)
                        E = self.E[gi % 2]
                        eb = self.E_b[gi % 2]
                        gi += 1
                        gps, ups = self.bank[gb], self.bank[ub]
                        for k in range(KC):
                            sc.op("pe", lambda e, k=k, fcl=fcl, fr=fr, gps=gps, slotg=slotg, sgl=sgl: e.matmul(
                                gps[0:fr, :], slotg[:, k, fcl * 128:fcl * 128 + fr], self.hT_f[:, k, sgl * 512:(sgl + 1) * 512],
                                start=(k == 0), stop=(k == KC - 1)),
                                reads=[rb, hb], writes=[self.bank_b[gb]], signal=(k == KC - 1))
                        for k in range(KC):
                            sc.op("pe", lambda e, k=k, fcl=fcl, fr=fr, ups=ups, slotu=slotu, sgl=sgl: e.matmul(
                                ups[0:fr, :], slotu[:, k, fcl * 128:fcl * 128 + fr], self.hT_f[:, k, sgl * 512:(sgl + 1) * 512],
                                start=(k == 0), stop=(k == KC - 1)),
                                reads=[rb, hb], writes=[self.bank_b[ub]], signal=(k == KC - 1))
                        sc.op("act", lambda e, E=E, gps=gps, fr=fr: e.activation(E[0:fr, :], gps[0:fr, :], AF.Silu),
                              writes=[self.bank_b[gb], eb])
                        sc.op("dve", lambda e, E=E, ups=ups, fr=fr, fc=fc, sgl=sgl: e.tensor_tensor(
                            out=self.aT[0:fr, fc, sgl * 512:(sgl + 1) * 512], in0=ups[0:fr, :], in1=E[0:fr, :], op=ALU.mult),
                            reads=[eb], writes=[self.bank_b[ub], self.aT_b[sgl]])
            for c in range(KC):
                si = self.slab([
                    (lambda r: r[:, 0:FCH * 128].rearrange("p (j d) -> p j d", j=FCH)[:, 0:21, :],
                     wd[0:21 * 128, c * 128:(c + 1) * 128].rearrange("(j p) d -> p j d", p=128)),
                    (lambda r: r[:, 0:FCH * 128].rearrange("p (j d) -> p j d", j=FCH)[0:64, 21, :],
                     wd[21 * 128:DFF, c * 128:(c + 1) * 128])])
                rb = self.ring_b[si]
                slot = self.ring[si][:, 0:FCH * 128].rearrange("p (j d) -> p j d", j=FCH)
                for sgl in range(SGT):
                    yb = 4 + ((c * SGT + sgl) % 2)
                    yps = self.bank[yb]
                    for j in range(FCH):
                        fr = 128 if j < 21 else 64
                        sc.op("pe", lambda e, j=j, fr=fr, yps=yps, slot=slot, sgl=sgl: e.matmul(
                            yps[:], slot[0:fr, j, :], self.aT[0:fr, j, sgl * 512:(sgl + 1) * 512],
                            start=(j == 0), stop=(j == FCH - 1)),
                            reads=[rb, self.aT_b[sgl]], writes=[self.bank_b[yb]], signal=(j == FCH - 1))
                    ssb = 6 + sgl
                    sqi = (c * SGT + sgl) % 4
                    sq = self.SQ2[:, sqi, :]
                    sc.op("act", lambda e, c=c, sgl=sgl, yps=yps: e.activation(
                        self.ystage_f[:, c, sgl * 512:(sgl + 1) * 512], yps[:], AF.Copy,
                        scale=self.mods[:, 3 * s + 2, c, b:b + 1]), reads=[self.mod_b], writes=[self.bank_b[yb]] + R1)
                    sc.op("act", lambda e, sq=sq, yps=yps: e.activation(sq, yps[:], AF.Square),
                          writes=[self.bank_b[yb], self.SQ2_b[sqi]])
                    self.tick_deferred()
                    self.defer(1, lambda c=c, ssb=ssb, sq=sq, sqi=sqi: sc.op(
                        "pe", lambda e: e.matmul(self.bank[ssb][:], self.onesb[:], sq, start=(c == 0), stop=(c == KC - 1)),
                        reads=[self.SQ2_b[sqi], self.const_b], writes=[self.bank_b[ssb]], signal=True))
            self.tick_deferred(flush=True)
            for sgl in range(SGT):
                self.postnorm_update(b, s, sg0 + sgl, self.ystage_f[:, :, sgl * 512:(sgl + 1) * 512], R1,
                                     self.LN2, self.RSTD2, self.LN2_b, self.UTMP, self.UTMP_b, ssbank=6 + sgl)
            if s == 0:
                for sgl in range(SGT):
                    self.stats_ahead(sg0 + sgl, self.SQ2, self.SQ2_b, self.LN2, self.LN2_b, ssbank=6 + sgl)

    def mixer(self, b):
        sc = self.sc
        S, NB, NG = self.S, self.NB, self.NG
        s = 1
        for sg in range(NG):
            self.prenorm_apply(b, s, sg, self.hT_m[:, :, sg * 512:(sg + 1) * 512], self.hTm_b[sg], self.HTMPm, self.HTMPm_b)
        self.fox_prep(b)
        self.phase_sync()
        for vi in range(2):
            sc.op("dve", lambda e, vi=vi: e.memset(self.VAs[vi][:, :, :, 64:65], 1.0), writes=[self.VAs_b[vi]])
        sc.op("dve", lambda e: e.memset(self.KT[64:128, :], 0.0), writes=[self.KT_b])
        sc.op("dve", lambda e: e.memset(self.KT1[0:64, :], 0.0), writes=[self.KT1_b])
        winv = self.w_in_d.rearrange("(k p) f -> p k f", p=128)
        NU = 4 + NHB

        def inproj_gen(u):
            isA = u < 4
            vi = u % 2
            if isA:
                ncol = 384
                parts = [(lambda r, i=i: r[:, 0:KC * 384].rearrange("p (k f) -> p k f", k=KC)[:, :, i * 128:(i + 1) * 128],
                          winv[:, :, base + u * 128: base + (u + 1) * 128]) for i, base in enumerate((0, 512, 1024))]
            else:
                h = u - 4
                ncol = 192
                parts = [(lambda r, i=i: r[:, 0:KC * 192].rearrange("p (k f) -> p k f", k=KC)[:, :, i * 64:(i + 1) * 64],
                          winv[:, :, base + h * 64: base + (h + 1) * 64]) for i, base in enumerate((1536, 2048, 2560))]
            si = self.slab(parts)
            rb = self.ring_b[si]
            slot = self.ring[si][:, 0:KC * ncol].rearrange("p (k f) -> p k f", k=KC)
            for blk in range(NB):
                bk = blk % 2
                ps = self.bank[bk]
                for k in range(KC):
                    sc.op("pe", lambda e, k=k: e.matmul(
                        ps[:, 0:ncol], self.hT_m[:, k, blk * 128:(blk + 1) * 128], slot[:, k, 0:ncol],
                        start=(k == 0), stop=(k == KC - 1)),
                        reads=[rb, self.hTm_b[blk // 4]], writes=[self.bank_b[bk]], signal=(k == KC - 1))
                if isA:
                    self.defer(2, lambda blk=blk, ps=ps, bk=bk: self.evac_A(blk, ps, self.bank_b[bk], vi))
                else:
                    self.defer(2, lambda blk=blk, ps=ps, bk=bk: self.evac_B(blk, ps, self.bank_b[bk], u - 4, vi))
                yield blk

        def stageB(u):
            self.tick_deferred(flush=True)
            isA = u < 4
            if u == 4:
                sc.op("dve", lambda e: e.memset(self.QT[64:128, :], 0.0), writes=[self.QT_b])
            self.unit_transposes(isA, u)

        def att_gen(u):
            vi = u % 2
            if u < 4:
                for h2 in range(2):
                    yield from self.attention(True, hp0=h2 * 64, K=64, vsel=h2, col0=(u * 2 + h2) * 64, vi=vi)
            else:
                yield from self.attention(False, hp0=0, K=70, vsel=0, col0=512 + (u - 4) * 64, vi=vi)

        for _ in inproj_gen(0):
            self.tick_deferred()
        self.tick_deferred(flush=True)
        for p in self.rope_A_pieces():
            p()
        stageB(0)
        for u in range(NU):
            nxt = inproj_gen(u + 1) if u + 1 < NU else None
            nsteps = (2 if u < 4 else 1) * (sum(4 * g + 4 for g in range(NG)) + 5)
            stride = max(2, (nsteps * 55 // 100) // NB) if (u + 1 < 4) else max(2, nsteps // (NB + 1))
            for i, _ in enumerate(att_gen(u)):
                if nxt is not None and i % stride == stride - 1:
                    if next(nxt, None) is None:
                        nxt = None
                        if u + 1 < 4:
                            for pi, p in enumerate(self.rope_A_pieces()):
                                self.defer(4 + 3 * pi, p)
            if nxt is not None:
                for _ in nxt:
                    self.tick_deferred()
                if u + 1 < 4:
                    self.tick_deferred(flush=True)
                    for p in self.rope_A_pieces():
                        p()
            if u + 1 < NU:
                stageB(u + 1)
        self.tick_deferred(flush=True)
        self.phase_sync()
        self.outproj(b)

    def fox_prep(self, b):
        sc = self.sc
        NB = self.NB
        Fb = self.F_b
        winv = self.w_in_d.rearrange("(k p) f -> p k f", p=128)
        sc.dma("pool", self.wf[:], winv[:, :, 3072:3080], writes=[self.wf_b])
        ps = self.bank[2]
        for blk in range(NB):
            for k in range(KC):
                sc.op("pe", lambda e, k=k, blk=blk: e.matmul(ps[:, blk * 8:(blk + 1) * 8], self.hT_m[:, k, blk * 128:(blk + 1) * 128],
                                                            self.wf[:, k, :], start=(k == 0), stop=(k == KC - 1)),
                      reads=[self.wf_b, self.hTm_b[blk // 4]], writes=[self.bank_b[2]], signal=(k == KC - 1))
        psv = ps[:, 0:NB * 8].rearrange("p (a b) -> p a b", a=NB)
        sc.op("dve", lambda e: e.tensor_tensor(out=self.FL[:], in0=psv, in1=self.bforget[:].unsqueeze(1).to_broadcast([128, NB, 8]),
                                               op=ALU.add), reads=[self.small_b], writes=[self.bank_b[2], Fb] + self.rstdn_b)
        sc.op("act", lambda e: e.activation(self.FL2[:], self.FL[:], AF.Exp, scale=-1.0), writes=[Fb])
        sc.op("act", lambda e: e.activation(self.FL[:], self.FL2[:], AF.Ln, bias=1.0, scale=1.0), writes=[Fb])
        wps, tps = self.bank[3], self.bank[4]
        flat = self.FL[:].rearrange("p a b -> p (a b)")
        sc.op("pe", lambda e: e.matmul(wps[:, 0:NB * 8], self.trif[:], flat, start=True, stop=True),
              reads=[Fb, self.small_b], writes=[self.bank_b[3]])
        sc.op("pe", lambda e: e.matmul(tps[:, 0:NB * 8], self.onesf[:], flat, start=True, stop=True),
              reads=[Fb, self.const_b], writes=[self.bank_b[4]])
        sc.op("dve", lambda e: e.tensor_copy(self.FL2[:], tps[:, 0:NB * 8].rearrange("p (a b) -> p a b", a=NB)),
              writes=[self.bank_b[4], Fb])
        sc.op("dve", lambda e: e.memset(self.CARRY[:, 0, :], 0.0), writes=[Fb])
        for i in range(1, NB):
            sc.op("dve", lambda e, i=i: e.tensor_tensor(out=self.CARRY[:, i, :], in0=self.CARRY[:, i - 1, :],
                                                        in1=self.FL2[:, i - 1, :], op=ALU.add), writes=[Fb])
        sc.op("dve", lambda e: e.tensor_tensor(out=self.NEGF[:], in0=wps[:, 0:NB * 8].rearrange("p (a b) -> p a b", a=NB),
                                               in1=self.CARRY[:], op=ALU.add), writes=[self.bank_b[3], Fb])
        V = lambda fn: sc.op("dve", fn, writes=[Fb])
        V(lambda e: e.tensor_copy(self.FH[:], self.NEGF[:]))
        V(lambda e: e.tensor_copy(self.F32T[:], self.FH[:]))
        V(lambda e: e.tensor_tensor(out=self.NEGF[:], in0=self.NEGF[:], in1=self.F32T[:], op=ALU.subtract))
        V(lambda e: e.tensor_copy(self.FM[:], self.NEGF[:]))
        V(lambda e: e.tensor_copy(self.F32T[:], self.FM[:]))
        V(lambda e: e.tensor_tensor(out=self.NEGF[:], in0=self.NEGF[:], in1=self.F32T[:], op=ALU.subtract))
        V(lambda e: e.tensor_copy(self.FLo[:], self.NEGF[:]))

    def _cp(self, eng, out, in_, scale, writes):
        sc = self.sc
        if eng == "act":
            if scale == 1.0:
                sc.op("act", lambda e: e.copy(out, in_), writes=writes)
            else:
                sc.op("act", lambda e: e.activation(out, in_, AF.Copy, scale=scale), writes=writes)
        else:
            if scale == 1.0:
                sc.op("dve", lambda e: e.tensor_copy(out, in_), writes=writes)
            else:
                sc.op("dve", lambda e: e.tensor_scalar(out=out, in0=in_, scalar1=scale, scalar2=None, op0=ALU.mult), writes=writes)

    def evac_A(self, blk, ps, bb, vi):
        eng = "act" if blk % 2 == 0 else "dve"
        qk = self.QKtm_b
        VA, VAb = self.VAs[vi], self.VAs_b[vi]
        self._cp(eng, self.QKtm[:, blk, 0, :], ps[:, 0:128], 0.125, [bb, qk])
        self._cp(eng, self.QKtm[:, blk, 1, :], ps[:, 128:256], 1.0, [bb, qk])
        self._cp(eng, self.ROTF[:, blk, :, :, :], ps[:, 0:256].rearrange("p (q h d) -> p q h d", q=2, h=2)[:, :, :, 0:16], 1.0,
                 [bb, self.rtmp_b])
        self._cp(eng, VA[:, blk, :, 0:64], ps[:, 256:384].rearrange("p (a b) -> p a b", a=2), 1.0, [bb, VAb])

    def rope_A_pieces(self):
        sc = self.sc
        NB, RH = self.NB, self.RH
        rt = self.rtmp_b
        qk = self.QKtm_b
        pieces = []
        for qi, (c2, s1) in enumerate(((self.cq2, self.sq1), (self.ck2, self.sk1))):
            for n0 in range(0, NB, RH):
                def piece(qi=qi, c2=c2, s1=s1, n0=n0):
                    X = self.ROTF[:, n0:n0 + RH, qi, :, :]
                    Dv = self.QKtm[:, n0:n0 + RH, qi, :].rearrange("p n (h d) -> p n h d", h=2)
                    cosb = c2[:, n0:n0 + RH, :].unsqueeze(2).to_broadcast([128, RH, 2, 16])
                    sinb = s1[:, n0:n0 + RH, :].unsqueeze(2).to_broadcast([128, RH, 2, 8])
                    sc.op("dve", lambda e: e.tensor_tensor(out=self.RA[:], in0=X, in1=cosb, op=ALU.mult),
                          reads=[self.rope_b], writes=[rt])
                    sc.op("dve", lambda e: e.tensor_tensor(out=self.RB1[:], in0=X[:, :, :, 8:16], in1=sinb, op=ALU.mult),
                          reads=[self.rope_b], writes=[rt])
                    sc.op("dve", lambda e: e.tensor_tensor(out=self.RB2[:], in0=X[:, :, :, 0:8], in1=sinb, op=ALU.mult),
                          reads=[self.rope_b], writes=[rt])
                    sc.op("dve", lambda e: e.tensor_tensor(out=Dv[:, :, :, 0:8], in0=self.RA[:, :, :, 0:8], in1=self.RB1[:],
                                                           op=ALU.subtract), reads=[rt], writes=[qk])
                    sc.op("dve", lambda e: e.tensor_tensor(out=Dv[:, :, :, 8:16], in0=self.RA[:, :, :, 8:16], in1=self.RB2[:],
                                                           op=ALU.add), reads=[rt], writes=[qk])
                pieces.append(piece)
        return pieces

    def evac_B(self, blk, ps, bb, h, vi):
        sc = self.sc
        qk = self.QKtm_b
        VA, VAb = self.VAs[vi], self.VAs_b[vi]
        eng = "act" if blk % 2 == 0 else "dve"
        self._cp(eng, self.QKtm[:, blk, 0, 0:64], ps[:, 0:64], 0.125, [bb, qk])
        self._cp(eng, self.QKtm[:, blk, 1, 0:64], ps[:, 64:128], 1.0, [bb, qk])
        self._cp(eng, VA[:, blk, 0, 0:64], ps[:, 128:192], 1.0, [bb, VAb])
        if blk == self.NB - 1:
            Fb = self.F_b
            sc.op("dve", lambda e: e.memset(self.QKtm[:, :, 0, 67:70], 1.0), writes=[qk])
            sc.op("dve", lambda e: e.memset(self.QKtm[:, :, 1, 64:67], 1.0), writes=[qk])
            for i, src_t in enumerate((self.FH, self.FM, self.FLo)):
                sc.op("dve", lambda e, i=i, src_t=src_t: e.tensor_scalar(out=self.QKtm[:, :, 0, 64 + i], in0=src_t[:, :, h], scalar1=-1.0,
                                                                       scalar2=None, op0=ALU.mult), reads=[Fb], writes=[qk])
                sc.op("dve", lambda e, i=i, src_t=src_t: e.tensor_copy(self.QKtm[:, :, 1, 67 + i], src_t[:, :, h]), reads=[Fb], writes=[qk])

    def unit_transposes(self, isA, u):
        sc = self.sc
        NB = self.NB
        ncol = 128 if isA else 70
        for g4 in range(NB // 4):
            qb, kb = (6, 7) if g4 % 2 == 0 else (2, 3)
            tq = self.tp_view(qb)
            tk = self.tp_view(kb)
            for qi, (tp, tb) in enumerate(((tq, qb), (tk, kb))):
                for bl in range(4):
                    blk = g4 * 4 + bl
                    sc.op("pe", lambda e, qi=qi, bl=bl, blk=blk, tp=tp: e.transpose(
                        tp[0:ncol, bl * 128:(bl + 1) * 128], self.QKtm[:, blk, qi, 0:ncol], self.identb[:]),
                        reads=[self.QKtm_b, self.const_b], writes=[self.bank_b[tb]], signal=(bl == 3))
            sc.op("act", lambda e, tq=tq, g4=g4: e.copy(self.QT[0:ncol, g4 * 512:(g4 + 1) * 512], tq[0:ncol, 0:512]),
                  writes=[self.bank_b[qb], self.QT_b])
            if isA:
                sc.op("dve", lambda e, tk=tk, g4=g4: e.tensor_copy(self.KT[0:64, g4 * 512:(g4 + 1) * 512], tk[0:64, 0:512]),
                      writes=[self.bank_b[kb], self.KT_b])
                sc.op("dve", lambda e, tk=tk, g4=g4: e.tensor_copy(self.KT1[64:128, g4 * 512:(g4 + 1) * 512], tk[64:128, 0:512]),
                      writes=[self.bank_b[kb], self.KT1_b])
            else:
                sc.op("dve", lambda e, tk=tk, g4=g4: e.tensor_copy(self.KT[0:ncol, g4 * 512:(g4 + 1) * 512], tk[0:ncol, 0:512]),
                      writes=[self.bank_b[kb], self.KT_b])

    def attention(self, isA, hp0, K, vsel, col0, vi):
        KTt, KTb = (self.KT1, self.KT1_b) if (isA and hp0 == 64) else (self.KT, self.KT_b)
        VA, VAb = self.VAs[vi], self.VAs_b[vi]
        SB = (2, 3, 6, 7)
        sc = self.sc
        NG = self.NG
        LAG = 5
        NPT = self.NPT
        tiles = [(g, j) for g in range(NG) for j in range(4 * g + 4)]
        n = len(tiles)

        def emit_qk(t):
            g, j = tiles[t]
            c0 = max(j - 4 * g, 0)
            sbk = SB[t % 4]
            sps = self.bank[sbk]
            PT = self.PT[t % NPT]
            ptb = self.PT_b[t % NPT]
            diag = (not isA) and (j >= 4 * g)
            sc.op("pe", lambda e: e.matmul(
                sps[:, c0 * 128:512], KTt[:, j * 128:(j + 1) * 128],
                self.QT[:, g * 512 + c0 * 128:(g + 1) * 512], start=True, stop=(not diag)),
                reads=[self.QT_b, KTb], writes=[self.bank_b[sbk]], signal=(not diag))
            if diag:
                sc.op("pe", lambda e: e.matmul(
                    sps[:, c0 * 128:(c0 + 1) * 128], self.identb[:], self.negm[:], start=False, stop=True),
                    reads=[self.const_b], writes=[self.bank_b[sbk]], signal=True)
            sc.op("act", lambda e: e.activation(PT[:, c0 * 128:512], sps[:, c0 * 128:512], AF.Exp),
                  writes=[self.bank_b[sbk], ptb])
            if isA:
                d0 = 4 * g + c0 - j
                nblk = 4 - c0
                msk = self.multm[:, d0:d0 + nblk, :].rearrange("p a b -> p (a b)")
                sc.op("dve", lambda e: e.tensor_tensor(
                    out=PT[:, c0 * 128:512], in0=PT[:, c0 * 128:512], in1=msk, op=ALU.mult),
                    reads=[self.const_b], writes=[ptb])

        def emit_pv(t):
            g, j = tiles[t]
            c0 = max(j - 4 * g, 0)
            PT = self.PT[t % NPT]
            ptb = self.PT_b[t % NPT]
            ob = 4 + (g % 2)
            opsv = self.bank[ob][:, 0:260].rearrange("p (a b) -> p a b", a=4)
            for c in range(c0, 4):
                sc.op("pe", lambda e, c=c: e.matmul(
                    opsv[:, c, :], PT[:, c * 128:(c + 1) * 128], VA[:, j, vsel, 0:65],
                    start=(j == 0 and c == 0), stop=(j == 4 * g + c), skip_group_check=True),
                    reads=[ptb, VAb], writes=[self.bank_b[ob]], signal=(c == 3))
            if j == 4 * g + 3:
                def norm(opsv=opsv, ob=ob, g=g):
                    sc.op("dve", lambda e: e.reciprocal(out=self.RDEN[:], in_=opsv[:, :, 64]),
                          writes=[self.bank_b[ob], self.rden_b])
                    sc.op("dve", lambda e: e.tensor_tensor(
                        out=self.merged[:, 4 * g:4 * g + 4, col0:col0 + 64], in0=opsv[:, :, 0:64],
                        in1=self.RDEN[:].unsqueeze(2).to_broadcast([128, 4, 64]), op=ALU.mult),
                        reads=[self.rden_b], writes=[self.bank_b[ob], self.merged_b])
                self.defer(4, norm)

        for t in range(n + LAG):
            if t < n:
                emit_qk(t)
            if t - LAG >= 0:
                emit_pv(t - LAG)
            self.tick_deferred()
            yield t

    def outproj(self, b):
        sc = self.sc
        NB, NG = self.NB, self.NG
        s = 1
        mergedT = self.hT_m
        mtb = self.hTm_b
        ssb = self.ss_b
        sc.op("dve", lambda e: e.memset(self.SSAB[:], 0.0), writes=[ssb])
        for blk in range(NB):
            for grp in range(2):
                sc.op("act", lambda e, blk=blk, grp=grp: e.activation(
                    self.SQJ[:], self.merged[:, blk, grp * 512:(grp + 1) * 512], AF.Square,
                    accum_out=self.SSAB[:, blk, grp:grp + 1]),
                    reads=[self.merged_b], writes=[ssb])
        sc.op("act", lambda e: e.activation(self.RSAB[:], self.SSAB[:], AF.Ln, bias=EPS, scale=1.0 / 512), writes=[ssb])
        sc.op("act", lambda e: e.activation(self.RSAB[:], self.RSAB[:], AF.Exp, scale=-0.5), writes=[ssb])
        gout = self.gains[:, 6, :]
        wov = self.w_out_d.rearrange("(k p) f -> p k f", p=128)
        slots = []
        for hf in range(2):
            si = self.slab([(lambda r: r[:].rearrange("p (k f) -> p k f", k=KC), wov[:, :, hf * 512:(hf + 1) * 512])], keep=hf)
            slot = self.ring[si][:].rearrange("p (k f) -> p k f", k=KC)
            slots.append((slot, self.ring_b[si]))

        def nt_front(blk):
            mn = self.MN[blk % 2]
            mnb = self.MN_b[blk % 2]
            tb = 6 + (blk % 2)
            tp = self.tp_view(tb)
            sc.op("dve", lambda e: e.tensor_tensor(
                out=mn[:].rearrange("p (a b) -> p a b", a=2), in0=self.merged[:, blk, :].rearrange("p (a b) -> p a b", a=2),
                in1=self.RSAB[:, blk, :].unsqueeze(2).to_broadcast([128, 2, 512]), op=ALU.mult),
                reads=[self.merged_b, ssb], writes=[mnb])
            for c in range(KC):
                sc.op("pe", lambda e, c=c: e.transpose(tp[:, c * 128:(c + 1) * 128], mn[:, c * 128:(c + 1) * 128], self.identb[:]),
                      reads=[mnb, self.const_b], writes=[self.bank_b[tb]], signal=(c == KC - 1))

        def nt_back(blk):
            tb = 6 + (blk % 2)
            tp = self.tp_view(tb)
            sc.op("dve", lambda e: e.tensor_tensor(
                out=mergedT[:, :, blk * 128:(blk + 1) * 128], in0=tp.rearrange("p (a b) -> p a b", a=KC),
                in1=gout.unsqueeze(2).to_broadcast([128, KC, 128]), op=ALU.mult),
                reads=[self.small_b], writes=[self.bank_b[tb], mtb[blk // 4]])

        def nt_closures(sg):
            b0 = sg * 4
            return [lambda: nt_front(b0), lambda: nt_front(b0 + 1), lambda: nt_back(b0), lambda: nt_front(b0 + 2),
                    lambda: nt_back(b0 + 1), lambda: nt_front(b0 + 3), lambda: nt_back(b0 + 2), lambda: nt_back(b0 + 3)]

        yi = [0]
        Gp = self.mods[:, 3 * s + 2, :, :]

        def mm_chunk(sg, c):
            slot, rb = slots[c // 4]
            yb = yi[0] % 4
            yi[0] += 1
            yps = self.bank[yb]
            pnb = 4 + (sg % 2)
            for k in range(KC):
                sc.op("pe", lambda e, k=k: e.matmul(
                    yps[:], slot[:, k, (c % 4) * 128:(c % 4 + 1) * 128], mergedT[:, k, sg * 512:(sg + 1) * 512],
                    start=(k == 0), stop=(k == KC - 1)),
                    reads=[rb, mtb[sg]], writes=[self.bank_b[yb]], signal=(k == KC - 1))
            sq = self.SQ2m[:, c % 4, :]

            def evac():
                sc.op("act", lambda e: e.activation(self.ystage_m[:, c, :], yps[:], AF.Copy, scale=Gp[:, c, b:b + 1]),
                      reads=[self.mod_b], writes=[self.bank_b[yb], self.ysm_b[c]])
                sc.op("act", lambda e: e.activation(sq, yps[:], AF.Square), writes=[self.bank_b[yb], self.SQ2m_b[c % 4]])

            def stat():
                sc.op("pe", lambda e: e.matmul(self.bank[pnb][:], self.onesb[:], sq, start=(c == 0), stop=(c == KC - 1)),
                      reads=[self.SQ2m_b[c % 4], self.const_b], writes=[self.bank_b[pnb]], signal=True)
            return evac, stat

        def pn_head(sg):
            pnb = 4 + (sg % 2)
            sc.op("act", lambda e: e.activation(self.LN2m[:], self.bank[pnb][:], AF.Ln, bias=EPS, scale=1.0 / D),
                  writes=[self.bank_b[pnb], self.LN2m_b])
            sc.op("act", lambda e: e.activation(self.RSTD2m[:], self.LN2m[:], AF.Exp, scale=-0.5), writes=[self.LN2m_b])

        def pn_chunk(sg, c):
            t0 = sg * 512
            tmp = self.UTMPm[c % 2]
            tb = self.UTMPm_b[c % 2]
            sc.op("dve", lambda e: e.tensor_tensor(out=tmp[:], in0=self.ystage_m[:, c, :], in1=self.RSTD2m[:], op=ALU.mult),
                  reads=[self.ysm_b[c], self.LN2m_b], writes=[tb])
            sc.op("dve", lambda e: e.tensor_tensor(
                out=self.xT[:, c, t0:t0 + 512], in0=tmp[:], in1=self.xT[:, c, t0:t0 + 512], op=ALU.add),
                reads=[tb], writes=[self.xT_b[sg]])

        for cl in nt_closures(0):
            cl()
        for sg in range(NG):
            ntq = nt_closures(sg + 1) if sg + 1 < NG else []
            if sg > 0:
                pn_head(sg - 1)
            prev_stat = None
            for c in range(KC):
                evac, stat = mm_chunk(sg, c)
                if sg > 0:
                    pn_chunk(sg - 1, c)
                evac()
                if prev_stat is not None:
                    prev_stat()
                prev_stat = stat
                if ntq:
                    ntq.pop(0)()
            prev_stat()
            while ntq:
                ntq.pop(0)()
            if sg > 0:
                self.stats_ahead(sg - 1, self.SQ2m, self.SQ2m_b, self.LN2m, self.LN2m_b, ssbank=4 + ((sg - 1) % 2))
        pn_head(NG - 1)
        for c in range(KC):
            pn_chunk(NG - 1, c)
        self.stats_ahead(NG - 1, self.SQ2m, self.SQ2m_b, self.LN2m, self.LN2m_b, ssbank=4 + ((NG - 1) % 2))


def _consts():
    identf = np.eye(128, dtype=np.float32)
    s_idx = np.arange(128)[:, None]
    t_idx = np.arange(128)[None, :]
    trif = (s_idx <= t_idx).astype(np.float32)
    kk = np.arange(128)[:, None, None]
    DD = np.arange(16)[None, :, None]
    tq = np.arange(128)[None, None, :]
    delta = DD * 128 + tq - kk
    m = ((delta >= 0) & (delta <= 128)).astype(np.float32)
    m += ((delta >= 0) & (delta <= 512) & (delta % 4 == 0)).astype(np.float32)
    m += ((delta >= 0) & (delta <= 2048) & (delta % 16 == 0)).astype(np.float32)
    mult = m.reshape(128, 16 * 128).astype(np.float32)
    neg = np.where(t_idx >= s_idx, 0.0, -30000.0).astype(np.float32)
    inv_freq = (THETA ** (-np.arange(0, ROT, 2, dtype=np.float32) / ROT)).astype(np.float32)
    invfreq = np.ascontiguousarray(np.broadcast_to(inv_freq[None, :], (128, 8))).astype(np.float32)
    return identf, trif, mult, neg, invfreq


def make_in_maps(inputs, S, NSEQ, ncores):
    f = lambda a: np.ascontiguousarray(np.asarray(a))
    x = f(inputs["x"]).astype(np.float32, copy=False)
    c = f(inputs["c"]).astype(np.float32, copy=False)
    pos = f(inputs["positions"]).astype(np.int32, copy=False)
    NB = S // 128
    identf, trif, mult, neg, invfreq = _consts()
    fm = lambda g: np.ascontiguousarray(np.asarray(g, dtype=np.float32).reshape(-1, 128).T)
    gains = np.stack([fm(inputs["g_pre_ff1"][0]), fm(inputs["g_post_ff1"][0]), fm(inputs["g_pre_mix"][0]),
                      fm(inputs["g_post_mix"][0]), fm(inputs["g_pre_ff2"][0]), fm(inputs["g_post_ff2"][0]),
                      fm(np.concatenate([np.asarray(inputs["g_out_a"][0]), np.asarray(inputs["g_out_b"][0])]))], axis=1)
    gains = np.ascontiguousarray(gains.astype(np.float32))
    shared = {
        "w_ada": f(inputs["w_ada"][0]), "b_adaT": fm(inputs["b_ada"][0]), "gains": gains,
        "bforget": np.ascontiguousarray(np.broadcast_to(np.asarray(inputs["b_forget"][0], dtype=np.float32)[None, :], (128, NHB))),
        "invfreq": invfreq,
        "w_ff1_gate": f(inputs["w_ff1_gate"][0]), "w_ff1_up": f(inputs["w_ff1_up"][0]), "w_ff1_down": f(inputs["w_ff1_down"][0]),
        "w_ff2_gate": f(inputs["w_ff2_gate"][0]), "w_ff2_up": f(inputs["w_ff2_up"][0]), "w_ff2_down": f(inputs["w_ff2_down"][0]),
        "w_in": f(inputs["w_in"][0]), "w_out": f(inputs["w_out"][0]),
        "c_identf": identf, "c_trif": trif, "c_mult": mult, "c_neg": neg,
    }
    maps = []
    for ci in range(ncores):
        sl = slice(ci * NSEQ, (ci + 1) * NSEQ)
        m = dict(shared)
        m["x"] = np.ascontiguousarray(x[sl])
        m["cT"] = np.ascontiguousarray(c[sl].reshape(NSEQ, KC, 128).transpose(2, 1, 0))
        m["pos"] = np.ascontiguousarray(pos[sl].reshape(NSEQ, NB, 128).transpose(2, 0, 1))
        maps.append(m)
    return maps


_NC_CACHE = {}


def run(inputs, S, NSEQ, ncores=NCORES, trace=False):
    key = (S, NSEQ)
    if key not in _NC_CACHE:
        _NC_CACHE[key] = Builder(S, NSEQ).build()
    nc = _NC_CACHE[key]
    maps = make_in_maps(inputs, S, NSEQ, ncores)
    res = run_bass_kernel_spmd(nc, maps, core_ids=list(range(ncores)), **({"trace": True} if trace else {}))
    out = np.concatenate([r["out"] for r in res.results], axis=0)
    return out, res


def kernel(**inputs):
    x = np.asarray(inputs["x"])
    Btot, S, _ = x.shape
    NSEQ = Btot // NCORES
    out, _ = run(inputs, S, NSEQ)
    return out.astype(np.float32, copy=False)
```
